# Optimizing a Trainium2 kernel written in Bass

```python
import jax, jax.numpy as jnp
from jax import lax
import numpy as np

D_MODEL = 1024
BATCH = 8
SEQ = 2048
DEPTH = 2

MIX_WIDTH = D_MODEL
POOL_WIDTH = D_MODEL // 4
POOL_WINDOWS = (2, 4, 8, 16)
POOL_GROUPS = len(POOL_WINDOWS)
POOL_GROUP_DIM = POOL_WIDTH // POOL_GROUPS
RET_WIDTH = (MIX_WIDTH - POOL_WIDTH) // 2
RET_HEADS = 4
RET_DV = RET_WIDTH // RET_HEADS
RET_DK = RET_DV // 2
RET_CHUNK = 128
ROPE_BASE = 10000.0
GLA_WIDTH = MIX_WIDTH - POOL_WIDTH - RET_WIDTH
GLA_HEADS = 4
GLA_DV = GLA_WIDTH // GLA_HEADS
GLA_DK = GLA_DV // 2
GLA_CHUNK = 64
GLA_GATE_RANK = 16
GLA_GATE_TAU = 16.0
D_FF = 2816
CONV_WIDTH = 3
EPS = 1e-6

IN_SPLITS = (POOL_WIDTH,
             RET_HEADS * RET_DK, RET_HEADS * RET_DK, RET_WIDTH, RET_WIDTH,
             GLA_HEADS * GLA_DK, GLA_HEADS * GLA_DK, GLA_WIDTH, GLA_GATE_RANK, GLA_WIDTH)
D_IN = (POOL_WIDTH + 2 * RET_HEADS * RET_DK + 2 * RET_WIDTH
        + 2 * GLA_HEADS * GLA_DK + 2 * GLA_WIDTH + GLA_GATE_RANK)

kernel_name = "hybrid_pool_retention_gla_convffn_adaln"


def rms_norm(x, g):
    xf = x.astype(jnp.float32)
    y = xf * lax.rsqrt(jnp.mean(xf * xf, axis=-1, keepdims=True) + EPS)
    return (y * g.astype(jnp.float32)).astype(x.dtype)


def head_rms(y):
    return y * lax.rsqrt(jnp.mean(y * y, axis=-1, keepdims=True) + EPS)


def chunk_view(t, C):
    B, T, H, d = t.shape
    return t.reshape(B, T // C, C, H, d).transpose(1, 0, 3, 2, 4)


def unchunk(y):
    N, B, H, C, d = y.shape
    return y.transpose(1, 0, 3, 2, 4).reshape(B, N * C, H * d)


def rotary(t, cos, sin):
    t1, t2 = jnp.split(t, 2, axis=-1)
    return jnp.concatenate([t1 * cos - t2 * sin, t2 * cos + t1 * sin], axis=-1)


def causal_multiscale_pool(u, pool_w, pool_scale):
    B, T, _ = u.shape
    uf = u.astype(jnp.float32).reshape(B, T, POOL_GROUPS, POOL_GROUP_DIM)
    cs = jnp.concatenate([jnp.zeros((B, 1, POOL_GROUPS, POOL_GROUP_DIM), jnp.float32),
                          jnp.cumsum(uf, axis=1)], axis=1)
    t = jnp.arange(T)
    outs = []
    for g, w in enumerate(POOL_WINDOWS):
        lo = jnp.maximum(t + 1 - w, 0)
        window_sum = cs[:, 1:, g] - cs[:, lo, g]
        count = jnp.minimum(t + 1, w).astype(jnp.float32)[None, :, None]
        outs.append(window_sum / count - uf[:, :, g])
    pooled = jnp.stack(outs, axis=2)
    mixed = jnp.einsum('btgi,gio->btgo', pooled, pool_w.astype(jnp.float32))
    return (mixed.reshape(B, T, POOL_WIDTH) * pool_scale.astype(jnp.float32)).astype(u.dtype)


def chunkwise_retention(q, k, v):
    B, T, H, _ = q.shape
    C = RET_CHUNK
    log_gamma = jnp.log(1.0 - 2.0 ** (-5.0 - jnp.arange(H, dtype=jnp.float32)))
    idx = jnp.arange(C, dtype=jnp.float32)
    rel = idx[:, None] - idx[None, :]
    decay = jnp.where(rel >= 0, jnp.exp(jnp.maximum(rel, 0.0)[None] * log_gamma[:, None, None]), 0.0)
    xi = jnp.exp((idx + 1.0)[None, :] * log_gamma[:, None])[None, :, :, None]
    zeta = jnp.exp((C - 1.0 - idx)[None, :] * log_gamma[:, None])[None, :, :, None]
    gamma_c = jnp.exp(C * log_gamma)[None, :, None, None]

    def step(S, inp):
        qi, ki, vi = inp
        scores = jnp.einsum('bhid,bhjd->bhij', qi, ki) * decay
        o = (jnp.einsum('bhij,bhjv->bhiv', scores, vi)
             + jnp.einsum('bhid,bhdv->bhiv', qi, S) * xi)
        S = S * gamma_c + jnp.einsum('bhjd,bhjv->bhdv', ki * zeta, vi)
        return S, o

    S0 = jnp.zeros((B, H, q.shape[-1], v.shape[-1]), jnp.float32)
    _, o = lax.scan(step, S0, (chunk_view(q, C), chunk_view(k, C), chunk_view(v, C)))
    return o


def chunked_gla(q, k, v, log_alpha):
    B, T, H, _ = q.shape
    C = GLA_CHUNK
    causal = jnp.tril(jnp.ones((C, C), dtype=bool))[:, :, None]

    def step(S, inp):
        qi, ki, vi, lai = inp
        b = jnp.cumsum(lai, axis=2)
        diff = b[:, :, :, None, :] - b[:, :, None, :, :]
        ratio = jnp.exp(jnp.where(causal, diff, -jnp.inf))
        scores = jnp.einsum('bhid,bhijd,bhjd->bhij', qi, ratio, ki)
        b_last = b[:, :, -1:, :]
        o = (jnp.einsum('bhij,bhjv->bhiv', scores, vi)
             + jnp.einsum('bhid,bhdv->bhiv', qi * jnp.exp(b), S))
        S = (jnp.exp(b_last[:, :, 0, :])[..., None] * S
             + jnp.einsum('bhjd,bhjv->bhdv', ki * jnp.exp(b_last - b), vi))
        return S, o

    S0 = jnp.zeros((B, H, q.shape[-1], v.shape[-1]), jnp.float32)
    _, o = lax.scan(step, S0, (chunk_view(q, C), chunk_view(k, C),
                               chunk_view(v, C), chunk_view(log_alpha, C)))
    return o


def causal_depthwise_conv(u, w, b):
    T = u.shape[1]
    up = jnp.pad(u, ((0, 0), (CONV_WIDTH - 1, 0), (0, 0)))
    y = up[:, 0:T] * w[0]
    for kk in range(1, CONV_WIDTH):
        y = y + up[:, kk:kk + T] * w[kk]
    return y + b


def hybrid_layer(x, cos, sin, mod, norm1_g, w_in, pool_w, pool_scale, gla_wa2, gla_ba,
                 gla_norm_g, w_out, norm2_g, w_up, conv_w, conv_b, w_down):
    B, T, _ = x.shape
    shift1, scale1, gate1, shift2, scale2, gate2 = jnp.split(mod[:, None, :], 6, axis=-1)

    h = rms_norm(x, norm1_g) * (1 + scale1) + shift1
    proj = h @ w_in
    offsets = [int(o) for o in np.cumsum(IN_SPLITS)[:-1]]
    u_pool, r_q, r_k, r_v, r_g, g_q, g_k, g_v, g_a, g_g = jnp.split(proj, offsets, axis=-1)

    y_pool = causal_multiscale_pool(u_pool, pool_w, pool_scale)

    rq = rotary(r_q.astype(jnp.float32).reshape(B, T, RET_HEADS, RET_DK), cos, sin)
    rk = rotary(r_k.astype(jnp.float32).reshape(B, T, RET_HEADS, RET_DK), cos, sin) * (RET_DK ** -0.5)
    rv = r_v.astype(jnp.float32).reshape(B, T, RET_HEADS, RET_DV)
    o_ret = unchunk(head_rms(chunkwise_retention(rq, rk, rv)))
    y_ret = (jax.nn.silu(r_g.astype(jnp.float32)) * o_ret).astype(x.dtype)

    gq = g_q.astype(jnp.float32).reshape(B, T, GLA_HEADS, GLA_DK) * (GLA_DK ** -0.5)
    gk = g_k.astype(jnp.float32).reshape(B, T, GLA_HEADS, GLA_DK)
    gv = g_v.astype(jnp.float32).reshape(B, T, GLA_HEADS, GLA_DV)
    gate_logits = (g_a @ gla_wa2 + gla_ba).astype(jnp.float32)
    log_alpha = (jax.nn.log_sigmoid(gate_logits) / GLA_GATE_TAU).reshape(B, T, GLA_HEADS, GLA_DK)
    o_gla = unchunk(head_rms(chunked_gla(gq, gk, gv, log_alpha))) * gla_norm_g.astype(jnp.float32)
    y_gla = (jax.nn.silu(g_g.astype(jnp.float32)) * o_gla).astype(x.dtype)

    mixed = jnp.concatenate([y_pool, y_ret, y_gla], axis=-1) @ w_out
    x = x + gate1 * mixed

    h = rms_norm(x, norm2_g) * (1 + scale2) + shift2
    u = causal_depthwise_conv(h @ w_up, conv_w, conv_b)
    a, g = jnp.split(u, 2, axis=-1)
    y = (jax.nn.silu(g) * a) @ w_down
    return x + gate2 * y


def setup_inputs(seed: int = 0) -> dict:
    key = jax.random.key(seed)
    ks = jax.random.split(key, 20)
    L, D, F = DEPTH, D_MODEL, D_FF
    nrm = jax.random.normal
    x = nrm(ks[0], (BATCH, SEQ, D), jnp.float32)
    c = nrm(ks[1], (BATCH, D), jnp.float32)
    offsets = jax.random.randint(ks[2], (BATCH, 1), 0, 4096, dtype=jnp.int32)
    positions = (jnp.arange(SEQ, dtype=jnp.int32)[None, :] + offsets).astype(jnp.int32)
    ada_w = nrm(ks[3], (L, D, 6 * D), jnp.float32) * 0.02
    ada_b = nrm(ks[4], (L, 6 * D), jnp.float32) * 0.02
    norm1_g = 1.0 + 0.1 * nrm(ks[5], (L, D), jnp.float32)
    w_in = nrm(ks[6], (L, D, D_IN), jnp.float32) * D ** -0.5
    pool_w = nrm(ks[7], (L, POOL_GROUPS, POOL_GROUP_DIM, POOL_GROUP_DIM), jnp.float32) * POOL_GROUP_DIM ** -0.5
    pool_scale = 1.0 + 0.1 * nrm(ks[8], (L, POOL_WIDTH), jnp.float32)
    gla_wa2 = nrm(ks[9], (L, GLA_GATE_RANK, GLA_HEADS * GLA_DK), jnp.float32) * GLA_GATE_RANK ** -0.5
    gla_ba = 0.1 * nrm(ks[10], (L, GLA_HEADS * GLA_DK), jnp.float32)
    gla_norm_g = 1.0 + 0.1 * nrm(ks[11], (L, GLA_WIDTH), jnp.float32)
    w_out = nrm(ks[12], (L, MIX_WIDTH, D), jnp.float32) * MIX_WIDTH ** -0.5
    norm2_g = 1.0 + 0.1 * nrm(ks[13], (L, D), jnp.float32)
    w_up = nrm(ks[14], (L, D, 2 * F), jnp.float32) * D ** -0.5
    conv_w = nrm(ks[15], (L, CONV_WIDTH, 2 * F), jnp.float32) * CONV_WIDTH ** -0.5
    conv_b = 0.02 * nrm(ks[16], (L, 2 * F), jnp.float32)
    w_down = nrm(ks[17], (L, F, D), jnp.float32) * F ** -0.5
    final_g = 1.0 + 0.1 * nrm(ks[18], (D,), jnp.float32)
    return {"x": x, "c": c, "positions": positions, "ada_w": ada_w, "ada_b": ada_b,
            "norm1_g": norm1_g, "w_in": w_in, "pool_w": pool_w, "pool_scale": pool_scale,
            "gla_wa2": gla_wa2, "gla_ba": gla_ba, "gla_norm_g": gla_norm_g, "w_out": w_out,
            "norm2_g": norm2_g, "w_up": w_up, "conv_w": conv_w, "conv_b": conv_b,
            "w_down": w_down, "final_g": final_g}


def reference(x, c, positions, ada_w, ada_b, norm1_g, w_in, pool_w, pool_scale, gla_wa2, gla_ba,
              gla_norm_g, w_out, norm2_g, w_up, conv_w, conv_b, w_down, final_g):
    inv_freq = ROPE_BASE ** (-jnp.arange(0, RET_DK, 2, dtype=jnp.float32) / RET_DK)
    ang = positions.astype(jnp.float32)[..., None] * inv_freq
    cos = jnp.cos(ang)[:, :, None, :]
    sin = jnp.sin(ang)[:, :, None, :]
    c_act = jax.nn.silu(c)
    for l in range(DEPTH):
        mod = c_act @ ada_w[l] + ada_b[l]
        x = hybrid_layer(x, cos, sin, mod, norm1_g[l], w_in[l], pool_w[l], pool_scale[l],
                         gla_wa2[l], gla_ba[l], gla_norm_g[l], w_out[l], norm2_g[l],
                         w_up[l], conv_w[l], conv_b[l], w_down[l])
    return rms_norm(x, final_g)
```

```python
import math
import numpy as np
import concourse.bass as bass
import concourse.mybir as mybir
from concourse.bass_utils import run_bass_kernel_spmd

F32 = mybir.dt.float32; BF16 = mybir.dt.bfloat16; I32 = mybir.dt.int32; U8 = mybir.dt.uint8
ALU = mybir.AluOpType; AF = mybir.ActivationFunctionType


def _rng(ap):
    sp = str(ap.space)
    if 'SB' not in sp and 'PSUM' not in sp:
        return None
    if 'PSUM' in sp:
        pat = ap.ap
        ds = mybir.dt.size(ap.dtype)
        pstep = pat[0][0]
        col0 = ap.offset % pstep if pstep > 0 else ap.offset
        ext = 1
        for st, cn in pat[1:]:
            ext += (cn - 1) * abs(st)
        b0 = (col0 * ds) // 2048; b1 = ((col0 + ext) * ds - 1) // 2048
        return [("%s#%d" % (ap.tensor.name, b), (0, 2048, ((0, 2048),))) for b in range(b0, b1 + 1)]
    pat = ap.ap
    ds = mybir.dt.size(ap.dtype)
    pstep = pat[0][0]
    col0 = ap.offset % pstep if pstep > 0 else ap.offset
    dims = sorted([(abs(st), cn) for st, cn in pat[1:] if cn > 1 and st != 0])
    run = 1
    rest = []
    for st, cn in dims:
        if st <= run:
            run = max(run, (cn - 1) * st + run)
        else:
            rest.append((st, cn))
    starts = [0]
    nint = 1
    for st, cn in rest:
        nint *= cn
    if nint > 64:
        ext = run
        for st, cn in rest:
            ext += (cn - 1) * st
        ivs = [(col0 * ds, (col0 + ext) * ds)]
    else:
        for st, cn in rest:
            starts = [a + i * st for a in starts for i in range(cn)]
        starts.sort()
        ivs = []
        for a in starts:
            s0 = (col0 + a) * ds; e0 = (col0 + a + run) * ds
            if ivs and s0 <= ivs[-1][1]:
                ivs[-1] = (ivs[-1][0], max(ivs[-1][1], e0))
            else:
                ivs.append((s0, e0))
    return (ap.tensor.name, (ivs[0][0], ivs[-1][1], tuple(ivs)))


def _ovl(a, b):
    if not (a[0] < b[1] and b[0] < a[1]):
        return False
    ia = a[2]; ib = b[2]
    if len(ia) == 1 and len(ib) == 1:
        return True
    i = j = 0
    while i < len(ia) and j < len(ib):
        if ia[i][0] < ib[j][1] and ib[j][0] < ia[i][1]:
            return True
        if ia[i][1] <= ib[j][1]:
            i += 1
        else:
            j += 1
    return False


def _covers(a, b):
    if not (a[0] <= b[0] and b[1] <= a[1]):
        return False
    ia = a[2]
    for (s, e) in b[2]:
        ok = False
        for (s2, e2) in ia:
            if s2 <= s and e <= e2:
                ok = True; break
        if not ok:
            return False
    return True


class Prog:
    ENG = ('pe', 'act', 'dve', 'pool', 'sp')

    def __init__(self, nc):
        self.nc = nc
        self.ops = []
        self.grp_ctr = 0

    def op(self, eng, fn, r=(), w=()):
        def flat(lst):
            out = []
            for x in lst:
                if not x:
                    continue
                if isinstance(x, list):
                    out.extend(x)
                else:
                    out.append(x)
            return out
        rr = flat(_rng(a) for a in r if a is not None and not isinstance(a, (int, float)))
        ww = flat(_rng(a) for a in w)
        ww = ww + [x for x in rr if x[0].startswith('ps') and x not in ww]
        self.ops.append(dict(k='c', eng=eng, fn=fn, r=rr, w=ww))

    def dma(self, eng, out, in_, sem, grp=None, **kw):
        if grp is None:
            self.grp_ctr += 1
            grp = ('_g', self.grp_ctr)
        rr = [x for x in (_rng(in_),) if x and not isinstance(x, list)]
        ww = [x for x in (_rng(out),) if x and not isinstance(x, list)]
        self.ops.append(dict(k='d', eng=eng, out=out, in_=in_, r=rr, w=ww, sem=sem, grp=grp, kw=kw))

    def mm(self, out, lhsT, rhs, start=True, stop=True):
        self.op('pe', lambda e: e.matmul(out, lhsT, rhs, start=start, stop=stop), r=[lhsT, rhs], w=[out])

    def tr(self, out, in_, ident):
        self.op('pe', lambda e: e.transpose(out, in_, ident), r=[in_, ident], w=[out])

    def act(self, out, in_, func, bias=None, scale=None, eng='act'):
        kw = {}
        if bias is not None: kw['bias'] = bias
        if scale is not None: kw['scale'] = scale
        self.op(eng, lambda e: e.activation(out, in_, func, **kw), r=[in_, bias, scale], w=[out])

    def tt(self, eng, out, in0, in1, op):
        self.op(eng, lambda e: e.tensor_tensor(out, in0, in1, op), r=[in0, in1], w=[out])

    def ts(self, eng, out, in0, s1, s2, op0, op1=None):
        if op1 is None:
            self.op(eng, lambda e: e.tensor_scalar(out, in0, s1, None, op0), r=[in0, s1], w=[out])
        else:
            self.op(eng, lambda e: e.tensor_scalar(out, in0, s1, s2, op0, op1), r=[in0, s1, s2], w=[out])

    def stt(self, out, in0, scalar, in1, op0, op1, eng='dve'):
        self.op(eng, lambda e: e.scalar_tensor_tensor(out, in0, scalar, in1, op0, op1), r=[in0, scalar, in1], w=[out])

    def copy(self, eng, out, in_):
        if eng == 'act':
            self.op(eng, lambda e: e.copy(out, in_), r=[in_], w=[out])
        else:
            self.op(eng, lambda e: e.tensor_copy(out, in_), r=[in_], w=[out])

    def memset(self, eng, out, val):
        self.op(eng, lambda e: e.memset(out, val), r=[], w=[out])

    def emit(self):
        nc = self.nc
        engs = {'pe': nc.tensor, 'act': nc.scalar, 'dve': nc.vector, 'pool': nc.gpsimd, 'sp': nc.sync}
        ops = self.ops
        n = len(ops)
        W = {}
        R = {}
        deps = [None] * n
        needed = [False] * n
        for i, o in enumerate(ops):
            raw = set(); oth = set()
            for (sp, f) in o['r']:
                for rec in W.get(sp, ()):
                    if _ovl(rec[0], f):
                        raw.add(rec[1])
            for (sp, f) in o['w']:
                for rec in W.get(sp, ()):
                    if _ovl(rec[0], f):
                        oth.add(rec[1])
                for rec in R.get(sp, ()):
                    if _ovl(rec[0], f):
                        oth.add(rec[1])
            d = set()
            for j in raw | oth:
                pj = ops[j]
                if j not in raw and o['k'] == 'c' and pj['k'] == 'c' and pj['eng'] == o['eng'] == 'pe':
                    continue
                if o['k'] == 'd' and pj['k'] == 'd' and o['sem'] == pj['sem'] and o['grp'] == pj['grp']:
                    continue
                d.add(j)
            d.discard(i)
            deps[i] = d
            for j in d: needed[j] = True
            for (sp, f) in o['w']:
                W[sp] = [rec for rec in W.get(sp, ()) if not _covers(f, rec[0])]
                R[sp] = [rec for rec in R.get(sp, ()) if not _covers(f, rec[0])]
                W[sp].append((f, i))
            for (sp, f) in o['r']:
                lst = R.setdefault(sp, [])
                eng = o['eng']
                if o['k'] == 'c':
                    lst[:] = [rec for rec in lst if not (rec[0] == f and ops[rec[1]]['k'] == 'c' and ops[rec[1]]['eng'] == eng)]
                lst.append((f, i))
        esem = {e: nc.alloc_semaphore("s_" + e) for e in engs}
        dsem = {}
        cnt = {e: 0 for e in engs}
        dcnt = {}
        tok = [None] * n
        grp_final = {}
        for i, o in enumerate(ops):
            if o['k'] == 'c':
                if needed[i]:
                    cnt[o['eng']] += 1
                    tok[i] = (esem[o['eng']], cnt[o['eng']])
            else:
                s = o['sem']
                if s not in dsem:
                    dsem[s] = nc.alloc_semaphore("d_%d" % len(dsem)); dcnt[s] = 0
                dcnt[s] += 16
                grp_final[(s, o['grp'])] = dcnt[s]
        for i, o in enumerate(ops):
            if o['k'] == 'd':
                tok[i] = (dsem[o['sem']], grp_final[(o['sem'], o['grp'])])
        waited = {e: {} for e in engs}
        nwaits = 0
        for i, o in enumerate(ops):
            e = o['eng']; E = engs[e]
            need = {}
            for j in deps[i]:
                s, v = tok[j]
                if need.get(s, 0) < v: need[s] = v
            for s, v in need.items():
                if waited[e].get(s, 0) >= v: continue
                E.wait_ge(s, v); waited[e][s] = v; nwaits += 1
            if o['k'] == 'c':
                ins = o['fn'](E)
                if needed[i]:
                    ins.then_inc(tok[i][0], 1)
            else:
                ins = E.dma_start(out=o['out'], in_=o['in_'], **o['kw'])
                ins.then_inc(dsem[o['sem']], 16)
        E = engs['sp']
        for s, v in dcnt.items():
            E.wait_ge(dsem[s], v)
        self.stats = dict(n=n, nwaits=nwaits, cnt=dict(cnt), nsem=len(esem) + len(dsem))
        return self.stats


D = 1024; T = 2048; KC = 8; DIN = 2576; DFF = 2816; NJ = 22
EPS = 1e-6
QUARTERS = [list(range(0, 6)), list(range(6, 12)), list(range(12, 17)), list(range(17, 22))]

COLS = {}
_o = 0
for _l in range(2):
    for _nm, _n in (("g1", 8), ("adab", 48), ("pscale", 2), ("g2", 8), ("cw", 132), ("cb", 44)):
        COLS[(_nm, _l)] = (_o, _n); _o += _n
COLS["gf"] = (_o, 8); _o += 8
COLS["c"] = (_o, 8); _o += 8
NCOL = _o


def host_cols(inp, b):
    cols = np.zeros((128, NCOL), np.float32)
    def put(key, arr):
        o, n = COLS[key]
        cols[:, o:o + n] = arr
    for l in range(2):
        put(("g1", l), inp["norm1_g"][l].reshape(8, 128).T)
        put(("adab", l), inp["ada_b"][l].reshape(48, 128).T)
        put(("pscale", l), inp["pool_scale"][l].reshape(2, 128).T)
        put(("g2", l), inp["norm2_g"][l].reshape(8, 128).T)
        cw = inp["conv_w"][l].reshape(3, 44, 128).transpose(2, 0, 1).reshape(128, 132)
        put(("cw", l), cw)
        put(("cb", l), inp["conv_b"][l].reshape(44, 128).T)
    put("gf", inp["final_g"].reshape(8, 128).T)
    put("c", inp["c"][b].reshape(8, 128).T)
    return cols


SC = 48.0 ** -0.5
GAM = [1.0 - 2.0 ** (-5.0 - h) for h in range(4)]
GAMC = [g ** 128 for g in GAM]
CT = {}
_o = 0
for _nm, _n in (("tri", 128), ("DT", 512), ("xi", 512), ("zeta", 4), ("invf", 24), ("one", 1)):
    CT[_nm] = (_o, _n); _o += _n
NCT = _o
CBT = {}
_o = 0
for _nm, _n in (("ident", 128), ("causal", 128), ("Pcur", 512), ("Pprev", 512), ("Pfirst", 512), ("onesrow", 128)):
    CBT[_nm] = (_o, _n); _o += _n
NCB = _o
_HC = {}


def host_consts():
    if _HC:
        return _HC["ct"], _HC["cb"]
    ct = np.zeros((128, NCT), np.float64)
    cb = np.zeros((128, NCB), np.float64)
    j = np.arange(128)[:, None]; i = np.arange(128)[None, :]
    tri = (j <= i).astype(np.float64)
    ct[:, CT["tri"][0]:CT["tri"][0] + 128] = -tri / 16.0
    DT = np.zeros((128, 4, 128)); xi = np.zeros((128, 4, 128)); zeta = np.zeros((128, 4))
    for h in range(4):
        DT[:, h, :] = np.where(i >= j, SC * GAM[h] ** np.maximum(i - j, 0), 0.0)
        xi[:, h, :] = SC * GAM[h] ** (i + 1.0)
        zeta[:, h] = GAM[h] ** (127.0 - np.arange(128))
    ct[:, CT["DT"][0]:CT["DT"][0] + 512] = DT.reshape(128, 512)
    ct[:, CT["xi"][0]:CT["xi"][0] + 512] = xi.reshape(128, 512)
    ct[:, CT["zeta"][0]:CT["zeta"][0] + 4] = zeta
    ct[:, CT["invf"][0]:CT["invf"][0] + 24] = (10000.0 ** (-np.arange(0, 48, 2) / 48.0))[None, :]
    ct[:, CT["one"][0]] = -1.0 / 16.0
    cb[:, CBT["ident"][0]:CBT["ident"][0] + 128] = np.eye(128)
    cb[:, CBT["causal"][0]:CBT["causal"][0] + 128] = tri
    Pc = np.zeros((128, 4, 128)); Pp = np.zeros((128, 4, 128)); Pf = np.zeros((128, 4, 128))
    for g, w in enumerate((2, 4, 8, 16)):
        Pc[:, g, :] = np.where((j <= i) & (j > i - w), 1.0 / w, 0.0) - (j == i)
        Pp[:, g, :] = np.where(j - 128 > i - w, 1.0 / w, 0.0)
        cnt = np.minimum(i + 1, w)
        Pf[:, g, :] = np.where((j <= i) & (j > i - w), 1.0 / cnt, 0.0) - (j == i)
    cb[:, CBT["Pcur"][0]:CBT["Pcur"][0] + 512] = Pc.reshape(128, 512)
    cb[:, CBT["Pprev"][0]:CBT["Pprev"][0] + 512] = Pp.reshape(128, 512)
    cb[:, CBT["Pfirst"][0]:CBT["Pfirst"][0] + 512] = Pf.reshape(128, 512)
    cb[:, CBT["onesrow"][0]:CBT["onesrow"][0] + 128] = 1.0
    _HC["ct"] = ct.astype(np.float32); _HC["cb"] = cb.astype(np.float32)
    return _HC["ct"], _HC["cb"]


def host_inputs(inp, b):
    ct, cb = host_consts()
    return {"xT": np.ascontiguousarray(inp['x'][b].T), "cols": host_cols(inp, b),
            "pos": np.ascontiguousarray(inp['positions'][b].reshape(16, 128).T),
            "ctab": ct, "cbt": cb,
            "gng": np.ascontiguousarray(np.broadcast_to(inp['gla_norm_g'][:, None, :], (2, 128, 384))),
            "ada_w": inp['ada_w'], "w_in": inp['w_in'], "w_out": inp['w_out'], "pool_w": inp['pool_w'],
            "gla_wa2": inp['gla_wa2'], "gla_ba": inp['gla_ba'],
            "w_up": inp['w_up'], "w_down": inp['w_down']}


def build(nc, layers=(0, 1), do_mixer=True, do_ffn=True, parts=("pool", "ret", "gla")):
    P = Prog(nc)
    dr = {}
    def din(name, shape, dt=F32):
        dr[name] = nc.dram_tensor(name, list(shape), dt, kind="ExternalInput").ap()
        return dr[name]
    xT = din("xT", [D, T])
    cols_d = din("cols", [128, NCOL])
    ada_w = din("ada_w", [2, D, 6 * D])
    pos_d = din("pos", [128, 16], I32)
    ctab_d = din("ctab", [128, NCT])
    cbt_d = din("cbt", [128, NCB])
    gng_d = din("gng", [2, 128, 384])
    w_in = din("w_in", [2, D, DIN])
    w_out = din("w_out", [2, D, D])
    pool_w = din("pool_w", [2, 4, 64, 64])
    gla_wa2 = din("gla_wa2", [2, 16, 192])
    gla_ba = din("gla_ba", [2, 192])
    w_up = din("w_up", [2, D, 2 * DFF])
    w_down = din("w_down", [2, DFF, D])
    outT = nc.dram_tensor("outT", [D, T], F32, kind="ExternalOutput").ap()

    arena = nc.alloc_sbuf_tensor("arena", [128, 207 * 1024], U8)
    ps = [nc.alloc_psum_tensor("ps%d" % i, [128, 512], F32) for i in range(6)]
    ps67 = nc.alloc_psum_tensor("ps67", [128, 1024], F32)
    ps.append(ps67[:, 0:512]); ps.append(ps67[:, 512:1024])

    class Bump:
        def __init__(self, base, limit): self.o = base; self.base = base; self.limit = limit
        def alloc(self, nbytes, dt, pat=None, **kw):
            assert self.o + nbytes <= self.limit, (self.o, nbytes, self.limit)
            a = arena[:, self.o:self.o + nbytes].bitcast(dt)
            self.o += (nbytes + 31) // 32 * 32
            return a.rearrange(pat, **kw) if pat else a

    X = arena[:, 0:65536].bitcast(F32).rearrange("p (k t) -> p k t", k=KC)
    CB = Bump(65536, 65536 + 18 * 1024)
    cols = CB.alloc(NCOL * 4, F32)
    def col(key, a=0, n=None):
        o, nn = COLS[key]
        n = nn - a if n is None else n
        return cols[:, o + a:o + a + n]
    modT = CB.alloc(2 * 48 * 4, F32, "p (l j) -> p l j", l=2)
    a12 = CB.alloc(2 * 2 * 8 * 4, F32, "p (l s k) -> p l s k", l=2, s=2)
    cact = CB.alloc(8 * 2, BF16)
    cact32 = CB.alloc(8 * 4, F32)
    onesb = CB.alloc(128 * 2, BF16)
    RB0 = CB.limit
    RLIM = 207 * 1024
    modring = arena[:, RLIM - 4096:RLIM].bitcast(BF16).rearrange("p (s k c) -> p s k c", s=2, k=8)
    modring32 = arena[:, RLIM - 8192:RLIM].bitcast(F32).rearrange("p (s k c) -> p s k c", s=2, k=8)

    xv = xT.rearrange("(k p) t -> p k t", p=128)
    ov = outT.rearrange("(k p) t -> p k t", p=128)
    P.dma('sp', cols, cols_d, sem="cols", grp="cols")
    posi = CB.alloc(16 * 4, I32)
    P.dma('sp', posi, pos_d, sem="cols", grp="cols")
    for k in range(KC):
        P.dma('sp' if k % 2 == 0 else 'act', X[:, k, :], xv[:, k, :], sem="xin", grp="xin")
    P.memset('dve', onesb, 1.0 / 1024.0)
    P.act(cact, col("c"), AF.Silu)
    P.act(cact32, col("c"), AF.Silu)

    PS_NORM = ps[6]; PS_MOD = ps[7]

    class ModSched:
        def __init__(self, l, mode='bf16', lo=0, hi=48, ring=None, semp=None):
            self.l = l; self.nd = lo; self.nm = lo; self.mode = mode; self.hi = hi
            self.wv = ada_w[l].rearrange("(k p) c -> p k c", p=128)
            if ring is None:
                ring = [modring[:, i] for i in range(2)] if mode == 'bf16' else [modring32[:, i] for i in range(2)]
            self.ring = ring; self.depth = len(ring)
            self.semp = semp if semp is not None else ("modring%d" if mode == 'bf16' else "modr32_%d")
        def where(self, jc):
            return PS_MOD[:, jc:jc + 1]
        def tick(self):
            if self.nd < self.hi and self.nd - self.nm < self.depth:
                jc = self.nd; self.nd += 1
                sl = jc % self.depth
                P.dma('pool' if self.mode == 'bf16' else 'sp', self.ring[sl], self.wv[:, :, jc * 128:(jc + 1) * 128], sem=self.semp % sl)
                if self.nd - self.nm < self.depth and self.nd < self.hi:
                    return
            if self.nm < self.nd:
                jc = self.nm; self.nm += 1
                sl = jc % self.depth
                out = self.where(jc)
                rhs = cact if self.mode == 'bf16' else cact32
                for k in range(KC):
                    P.mm(out, self.ring[sl][:, k, :], rhs[:, k:k + 1], start=(k == 0), stop=(k == KC - 1))
        def run_until(self, n):
            while self.nm < n:
                self.tick()
        def done(self):
            return self.nm >= self.hi
    def mod_finish(l, j0, j1, src=None):
        src = PS_MOD[:, j0:j1] if src is None else src
        P.tt('dve', modT[:, l, j0:j1], src, col(("adab", l), j0, j1 - j0), ALU.add)
    def mod_derive(l, which):
        sc = modT[:, l, 8:16] if which == 0 else modT[:, l, 32:40]
        g = col(("g1", l)) if which == 0 else col(("g2", l))
        P.stt(a12[:, l, which, :], sc, 1.0, g, ALU.add, ALU.mult)

    def norm_a(t0, n, tmp, psn=None):
        sq, rstd, lnv, tmpf = tmp
        PSN = psn if psn is not None else PS_NORM[:, 0:n]
        xs = X[:, :, t0:t0 + n]
        P.act(sq[:, :, 0:n], xs, AF.Square)
        for k in range(KC):
            P.mm(PSN, onesb, sq[:, k, 0:n], start=(k == 0), stop=(k == KC - 1))
        P.act(lnv[:, 0:n], PSN, AF.Ln, bias=epsc[:, 0:1], scale=1.0)
        P.act(rstd[:, 0:n], lnv[:, 0:n], AF.Exp, scale=-0.5)

    def norm_b(t0, n, dst, tmp, A=None, S=None, G=None, alt='dve'):
        sq, rstd, lnv, tmpf = tmp
        xs = X[:, :, t0:t0 + n]
        rb = rstd[:, 0:n].unsqueeze(1).broadcast_to([128, KC, n])
        P.tt('dve', tmpf[:, :, 0:n], xs, rb, ALU.mult)
        if alt == 'dve2':
            P.tt('dve', tmpf[:, :, 0:n], tmpf[:, :, 0:n], A.unsqueeze(2).broadcast_to([128, KC, n]), ALU.mult)
            P.tt('dve', dst, tmpf[:, :, 0:n], S.unsqueeze(2).broadcast_to([128, KC, n]), ALU.add)
            return
        if alt == 'pool2':
            P.tt('pool', tmpf[:, :, 0:n], tmpf[:, :, 0:n], A.unsqueeze(2).broadcast_to([128, KC, n]), ALU.mult)
            P.tt('pool', dst, tmpf[:, :, 0:n], S.unsqueeze(2).broadcast_to([128, KC, n]), ALU.add)
            return
        for k in range(KC):
            if G is not None:
                if k % 2 == 0 or alt == 'act':
                    P.act(dst[:, k, :], tmpf[:, k, 0:n], AF.Identity, scale=G[:, k:k + 1])
                else:
                    P.ts(alt, dst[:, k, :], tmpf[:, k, 0:n], G[:, k:k + 1], None, ALU.mult)
            else:
                if k % 2 == 0 or alt == 'act':
                    P.act(dst[:, k, :], tmpf[:, k, 0:n], AF.Identity, bias=S[:, k:k + 1], scale=A[:, k:k + 1])
                else:
                    P.ts(alt, dst[:, k, :], tmpf[:, k, 0:n], A[:, k:k + 1], S[:, k:k + 1], ALU.mult, ALU.add)

    def norm(RB, t0, n, dst, A=None, S=None, G=None, tmp=None, psn=None, alt='dve'):
        norm_a(t0, n, tmp, psn=psn)
        norm_b(t0, n, dst, tmp, A=A, S=S, G=G, alt=alt)

    epsc = CB.alloc(4, F32)
    P.memset('dve', epsc, EPS)


    kcol = CB.alloc(8 * 4, F32)
    P.memset('dve', kcol[:, 0:1], EPS)
    P.memset('dve', kcol[:, 1:2], 1.0)
    P.memset('dve', kcol[:, 2:3], math.log(SC))
    ctab = CB.alloc(NCT * 4, F32)
    P.dma('sp', ctab, ctab_d, sem="cols", grp="cols")
    def ct(key):
        o, n = CT[key]
        return ctab[:, o:o + n]
    cbt = CB.alloc(NCB * 2, BF16)
    P.dma('pool', cbt, cbt_d, sem="cbt")
    def cbv(key):
        o, n = CBT[key]
        return cbt[:, o:o + n]
    ident = cbv("ident"); causal = cbv("causal"); onesrow = cbv("onesrow")
    Pcur = cbv("Pcur").rearrange("p (g i) -> p g i", g=4)
    Pprev = cbv("Pprev").rearrange("p (g i) -> p g i", g=4)
    Pfirst = cbv("Pfirst").rearrange("p (g i) -> p g i", g=4)
    tri = ct("tri"); DTm = ct("DT").rearrange("p (h i) -> p h i", h=4)
    xit = ct("xi").rearrange("p (h i) -> p h i", h=4); zeta = ct("zeta"); invf = ct("invf"); one32 = ct("one")
    costab = CB.alloc(16 * 24 * 4, F32, "p (t i) -> p t i", t=16)
    sintab = CB.alloc(16 * 24 * 4, F32, "p (t i) -> p t i", t=16)
    wa2b = CB.alloc(192 * 2, BF16)
    bab = CB.alloc(192 * 2, BF16)
    PWblk = CB.alloc(2 * 128 * 2, BF16, "p (a c) -> p a c", a=2)
    gng = CB.alloc(384 * 4, F32)

    def trig_tables():
        TB_ = Bump(RB0, RLIM)
        posf = TB_.alloc(16 * 4, F32)
        ang = TB_.alloc(384 * 4, F32, "p (t i) -> p t i", t=16)
        a2 = TB_.alloc(384 * 4, F32, "p (t i) -> p t i", t=16)
        kf = TB_.alloc(384 * 4, F32, "p (t i) -> p t i", t=16)
        rr = TB_.alloc(384 * 4, F32, "p (t i) -> p t i", t=16)
        P.copy('dve', posf, posi)
        P.tt('dve', ang, posf.unsqueeze(2).broadcast_to([128, 16, 24]), invf.unsqueeze(1).broadcast_to([128, 16, 24]), ALU.mult)
        MAGIC = 12582912.0
        C1 = 6.28125; C2 = 2.0 * math.pi - 6.28125
        for tab, shift in ((sintab, 0.0), (costab, math.pi / 2)):
            P.ts('dve', a2, ang, shift, None, ALU.add)
            P.ts('dve', kf, a2, 1.0 / (2.0 * math.pi), MAGIC, ALU.mult, ALU.add)
            P.ts('dve', kf, kf, MAGIC, None, ALU.subtract)
            P.stt(rr, kf, -C1, a2, ALU.mult, ALU.add)
            P.stt(rr, kf, -C2, rr, ALU.mult, ALU.add)
            P.ts('dve', rr, rr, math.pi, -math.pi, ALU.min, ALU.max)
            P.act(tab, rr, AF.Sin)
    trig_tables()


    def mixer(l, msch=None, pre=None):
        RB = Bump(RB0, RLIM)
        Win = RB.alloc(KC * DIN * 2, BF16, "p (k c) -> p k c", k=KC)
        Wout = RB.alloc(KC * D * 2, BF16, "p (k c) -> p k c", k=KC)
        hT2 = RB.alloc(2 * KC * 128 * 2, BF16, "p (s k t) -> p s k t", s=2, k=KC)
        yT = RB.alloc(KC * 512 * 2, BF16, "p (k t) -> p k t", k=KC)
        sq = RB.alloc(KC * 128 * 2, BF16, "p (k t) -> p k t", k=KC)
        rstd = RB.alloc(128 * 4, F32); lnv = RB.alloc(128 * 4, F32)
        tmpf = RB.alloc(KC * 128 * 4, F32, "p (k t) -> p k t", k=KC)
        ubf = RB.alloc(3 * 256 * 2, BF16, "p (s c) -> p s c", s=3)
        pooledT = RB.alloc(2 * 128 * 2, BF16, "p (a t) -> p a t", a=2)
        rotA = RB.alloc(384 * 4, F32); rotB = RB.alloc(384 * 4, F32)
        gaT = RB.alloc(128 * 2, BF16)
        e1 = RB.alloc(192 * 4, F32); spb = e1
        eb = RB.alloc(192 * 4, F32); enb = RB.alloc(192 * 4, F32)
        ebl = RB.alloc(2 * 4 * 4, F32, "p (s h) -> p s h", s=2)
        ybuf = RB.alloc(2 * 384 * 2, BF16, "p (m c) -> p m c", m=2)
        MX = []
        for mi in range(2):
            d = {}
            d["qk"] = RB.alloc(2 * 8 * 48 * 2, BF16, "p (s c d) -> p s c d", s=2, c=8)
            d["qkT"] = RB.alloc(2 * 8 * 128 * 2, BF16, "p (s c t) -> p s c t", s=2, c=8)
            if mi == 0:
                d["qxT"] = RB.alloc(2 * 4 * 128 * 2, BF16, "p (s c t) -> p s c t", s=2, c=4)
                d["vz"] = RB.alloc(2 * 384 * 2, BF16, "p (s c) -> p s c", s=2)
            d["v"] = RB.alloc(2 * 384 * 2, BF16, "p (s c) -> p s c", s=2)
            d["STm"] = RB.alloc(4 * 128 * 2, BF16, "p (h t) -> p h t", h=4)
            d["S32"] = RB.alloc(4 * 96 * 4, F32, "p (h v) -> p h v", h=4)
            d["Sbf"] = RB.alloc(2 * 4 * 96 * 2, BF16, "p (s h v) -> p s h v", s=2, h=4)
            if mi == 1:
                d["tmpS"] = RB.alloc(4 * 96 * 4, F32, "p (h v) -> p h v", h=4)
            MX.append(d)

        assert RB.o <= RLIM - 4096, (RB.o, RLIM - 4096)
        sgj = RB.alloc(2 * 2 * 384 * 4, F32, "p (s m c) -> p s m c", s=2, m=2)
        for mi in range(2):
            MX[mi]["sg"] = sgj[:, :, mi, :]
        sq2 = RB.alloc(2 * 384 * 4, F32, "p (m c) -> p m c", m=2)
        ss8 = RB.alloc(8 * 4, F32); ln8 = RB.alloc(8 * 4, F32); rs8 = RB.alloc(8 * 4, F32)
        B3bf = ps[3][:, :].bitcast(BF16)
        B5bf = ps[5][:, :].bitcast(BF16)
        wiv = w_in[l].rearrange("(k p) c -> p k c", p=128)
        for k in range(KC):
            for hf in range(2):
                P.dma('pool', Win[:, k, hf * 1288:(hf + 1) * 1288], wiv[:, k, hf * 1288:(hf + 1) * 1288], sem="win", grp=("win", l))
        wov = w_out[l].rearrange("(k p) c -> p k c", p=128)
        for k in range(KC):
            P.dma('pool', Wout[:, k, :], wov[:, k, :], sem="wout", grp=("wout", l))
        P.memset('dve', PWblk, 0.0)
        for g in range(4):
            gs = g % 2; p_ = g // 2
            P.dma('pool', PWblk[64 * gs:64 * gs + 64, p_, 64 * gs:64 * gs + 64], pool_w[l, g], sem="small", grp=("small", l))
        P.dma('pool', wa2b[0:16, :], gla_wa2[l], sem="small", grp=("small", l))
        P.dma('pool', bab[0:1, :], gla_ba[l:l + 1, :], sem="small", grp=("small", l))
        P.dma('sp', gng, gng_d[l], sem="gng")
        if pre is not None:
            pre()
        for mi in range(2):
            P.memset('dve', MX[mi]["S32"][0:48], 0.0)
            P.memset('dve', MX[mi]["Sbf"][0:48, 0], 0.0)
        mod_derive(l, 0)
        A1 = a12[:, l, 0, :]; S1 = modT[:, l, 0:8]

        def proj(hT, c0, c1, bank):
            n = c1 - c0
            for k in range(KC):
                P.mm(ps[bank][:, 0:n], hT[:, k, :], Win[:, k, c0:c1], start=(k == 0), stop=(k == KC - 1))
            return ps[bank][:, 0:n]

        def modtick(n=1):
            if msch is None:
                return
            for _ in range(n):
                if msch.done():
                    return
                n0 = msch.nm
                msch.tick()
                if n0 < 24 <= msch.nm:
                    mod_finish(l, 16, 24, src=ps[0][:, 384:392])
                if msch.done():
                    mod_finish(l, 24, 48, src=ps[0][:, 392:416])

        def stage0(t):
            norm_a(t * 128, 128, (sq, rstd, lnv, tmpf), psn=ps[2][:, 384:512])
            yield
            modtick(2 if (msch is not None and msch.nm < 24) else 1)
            norm_b(t * 128, 128, hT2[:, t % 2], (sq, rstd, lnv, tmpf), A=A1, S=S1, alt='dve2')
            yield

        def stage1(t):
            par = t % 2
            dr_ = MX[0]; dg = MX[1]
            hT = hT2[:, t % 2]
            modtick(2 if (msch is not None and msch.nm < 24) else 1)
            for k in range(KC):
                P.mm(ps[0][0:16, 0:128], Win[:, k, 2176:2192], hT[:, k, :], start=(k == 0), stop=(k == KC - 1))
            P.copy('act', gaT[0:16, :], ps[0][0:16, 0:128])
            yield
            pq = proj(hT, 256, 640, 1)
            cosb = costab[:, t, :].unsqueeze(1).broadcast_to([128, 16, 24])
            sinb = sintab[:, t, :].unsqueeze(1).broadcast_to([128, 16, 24])
            pq3 = pq.rearrange("p (c i) -> p c i", i=24)
            P.tt('dve', rotA.rearrange("p (c i) -> p c i", i=24), pq3, cosb, ALU.mult)
            P.tt('dve', rotB.rearrange("p (c i) -> p c i", i=24), pq3, sinb, ALU.mult)
            A4 = rotA.rearrange("p (c f i) -> p c f i", f=2, i=24); B4 = rotB.rearrange("p (c f i) -> p c f i", f=2, i=24)
            R4 = dr_["qk"][:, par].rearrange("p c (f i) -> p c f i", f=2)
            P.tt('dve', R4[:, :, 0, :], A4[:, :, 0, :], B4[:, :, 1, :], ALU.subtract)
            P.tt('dve', R4[:, :, 1, :], A4[:, :, 1, :], B4[:, :, 0, :], ALU.add)
            yield
            P.mm(ps[2][:, 0:192], gaT[0:16, :], wa2b[0:16, :], start=True, stop=False)
            P.mm(ps[2][:, 0:192], onesrow[0:1, :], bab[0:1, :], start=False, stop=True)
            P.act(e1, ps[2][:, 0:192], AF.Exp, scale=-1.0)
            P.act(spb, e1, AF.Ln, bias=kcol[:, 1:2], scale=1.0)
            yield
            pu = proj(hT, 0, 256, 0)
            P.copy('act', ubf[:, t % 3, :], pu)
            yield
            P.mm(ps[2][:, 192:384], tri, spb)
            for h in range(4):
                P.mm(ps[0][0:48, 416 + h:417 + h], spb[:, 48 * h:48 * h + 48], one32)
            P.act(eb, ps[2][:, 192:384], AF.Exp, bias=kcol[:, 2:3], scale=1.0)
            P.act(enb, ps[2][:, 192:384], AF.Exp, scale=-1.0)
            P.act(ebl[0:48, par, :], ps[0][0:48, 416:420], AF.Exp)
            yield
            pv = proj(hT, 640, 1024, 1)
            P.copy('act', dr_["v"][:, par], pv)
            P.tt('dve', dr_["vz"][:, par].rearrange("p (h v) -> p h v", h=4), pv.rearrange("p (h v) -> p h v", h=4),
                 zeta.unsqueeze(2).broadcast_to([128, 4, 96]), ALU.mult)
            yield
            pv = proj(hT, 1792, 2176, 0)
            P.copy('act', dg["v"][:, par], pv)
            yield
            pg = proj(hT, 1408, 1792, 1)
            P.tt('dve', dg["qk"][:, par, 0:4, :], pg[:, 0:192].rearrange("p (c d) -> p c d", c=4), eb.rearrange("p (c d) -> p c d", c=4), ALU.mult)
            P.tt('dve', dg["qk"][:, par, 4:8, :], pg[:, 192:384].rearrange("p (c d) -> p c d", c=4), enb.rearrange("p (c d) -> p c d", c=4), ALU.mult)
            yield
            B3v = B3bf[0:48, :].rearrange("p (c t) -> p c t", c=8)
            for c in range(8):
                P.tr(B3bf[0:48, c * 128:(c + 1) * 128], dr_["qk"][:, par, c, :], ident)
            P.copy('act', dr_["qkT"][0:48, par], B3v)
            P.tt('dve', dr_["qxT"][0:48, par], B3v[:, 0:4, :], xit[0:48], ALU.mult)
            yield
            pgt = proj(hT, 1024, 1408, 0)
            P.act(dr_["sg"][:, par], pgt, AF.Silu)
            yield
            pgt = proj(hT, 2192, 2576, 1)
            P.act(dg["sg"][:, par], pgt, AF.Silu)
            P.tt('dve', dg["sg"][:, par], dg["sg"][:, par], gng, ALU.mult)
            yield
            for c in range(8):
                P.tr(B3bf[0:48, c * 128:(c + 1) * 128], dg["qk"][:, par, c, :], ident)
            P.copy('act', dg["qkT"][0:48, par], B3v)
            yield

        def st_mask(t, mi, kind):
            d = MX[mi]; par = t % 2
            qkT = d["qkT"][:, par]
            STp = ps[5][:, :].rearrange("p (h t) -> p h t", h=4)
            for h in range(4):
                P.mm(STp[:, h, :], qkT[0:48, 4 + h, :], qkT[0:48, h, :])
            if kind == "ret":
                P.tt('dve', d["STm"], STp, DTm, ALU.mult)
            else:
                P.tt('dve', d["STm"], STp, causal.unsqueeze(1).broadcast_to([128, 4, 128]), ALU.mult)

        def core(t, mi, kind):
            d = MX[mi]; par = t % 2
            qkT = d["qkT"][:, par]
            qxT = d["qxT"][:, par] if kind == "ret" else qkT
            vv = d["v"][:, par]
            Op = ps[6][:, 0:384] if kind == "ret" else ps[7][:, 0:384]
            dSbank = ps[4]
            kz = d["qk"][:, par, 4:8, :]
            vz = d["vz"][:, par] if kind == "ret" else vv
            for h in range(4):
                P.mm(dSbank[0:48, 96 * h:96 * h + 96], kz[:, h, :], vz[:, 96 * h:96 * h + 96])
            dSv = dSbank[0:48, 0:384].rearrange("p (h v) -> p h v", h=4)
            S32 = d["S32"]
            for h in range(4):
                P.mm(Op[:, 96 * h:96 * h + 96], d["STm"][:, h, :], vv[:, 96 * h:96 * h + 96], start=True, stop=False)
                P.mm(Op[:, 96 * h:96 * h + 96], qxT[0:48, h, :], d["Sbf"][0:48, par, h, :], start=False, stop=True)
            if kind == "ret":
                for h in range(4):
                    P.stt(S32[0:48, h, :], S32[0:48, h, :], GAMC[h], dSv[:, h, :], ALU.mult, ALU.add)
            else:
                P.tt('dve', d["tmpS"][0:48], dSv, S32[0:48], ALU.add)
                P.tt('dve', S32[0:48], d["tmpS"][0:48], ebl[0:48, par, :].unsqueeze(2).broadcast_to([48, 4, 96]), ALU.mult)
            P.copy('act', d["Sbf"][0:48, 1 - par], S32[0:48])

        def rms_both(t):
            par = t % 2
            Ob = ps67[:, :].rearrange("p (m c) -> p m c", m=2)[:, :, 0:384]
            P.act(sq2, Ob, AF.Square)
            P.op('dve', lambda e, o=ss8, i_=sq2.rearrange("p m (h v) -> p (m h) v", h=4): e.tensor_reduce(o, i_, mybir.AxisListType.X, ALU.add),
                 r=[sq2], w=[ss8])
            P.act(ln8, ss8, AF.Ln, bias=kcol[:, 0:1], scale=1.0 / 96.0)
            P.act(rs8, ln8, AF.Exp, scale=-0.5)
            P.tt('dve', sq2.rearrange("p m (h v) -> p m h v", h=4), Ob.rearrange("p m (h v) -> p m h v", h=4),
                 rs8.rearrange("p (m h) -> p m h", m=2).unsqueeze(3).broadcast_to([128, 2, 4, 96]), ALU.mult)
            P.tt('dve', ybuf, sq2, sgj[:, par], ALU.mult)

        def y_tr(t, mi):
            ti = t % 4
            for c in range(3):
                P.tr(B5bf[:, c * 128:(c + 1) * 128], ybuf[:, mi, c * 128:(c + 1) * 128], ident)
            P.copy('dve', yT[:, 2 + 3 * mi:5 + 3 * mi, ti * 128:(ti + 1) * 128], B5bf[:, 0:384].rearrange("p (c t) -> p c t", c=3))

        def stage2(t):
            ti = t % 4
            st_mask(t, 0, "ret")
            st_mask(t, 1, "gla")
            yield
            core(t, 0, "ret")
            core(t, 1, "gla")
            rms_both(t)
            yield
            y_tr(t, 0)
            yield
            y_tr(t, 1)
            yield
            cur = t % 3; prv = (t - 1) % 3
            for p_ in range(2):
                for gs in range(2):
                    g = 2 * p_ + gs
                    pp = ps[4][:, 384:512]
                    if t == 0:
                        P.mm(pp, ubf[:, cur, 128 * p_:128 * p_ + 128], Pfirst[:, g, :])
                    else:
                        P.mm(pp, ubf[:, cur, 128 * p_:128 * p_ + 128], Pcur[:, g, :], start=True, stop=False)
                        P.mm(pp, ubf[:, prv, 128 * p_:128 * p_ + 128], Pprev[:, g, :], start=False, stop=True)
                    P.copy('act', pooledT[64 * gs:64 * gs + 64, p_, :], pp[64 * gs:64 * gs + 64, :])
                    yield
                pm = ps[7][:, 384:512]
                P.mm(pm, PWblk[:, p_, :], pooledT[:, p_, :])
                P.act(yT[:, p_, ti * 128:(ti + 1) * 128], pm, AF.Identity, scale=col(("pscale", l), p_, 1))
            if ti == 3:
                bk = t // 4
                for m in range(KC):
                    wb = ps[(7, 6, 5)[m % 3]]
                    for k in range(KC):
                        P.mm(wb[:, 0:512], Wout[:, k, m * 128:(m + 1) * 128], yT[:, k, :], start=(k == 0), stop=(k == KC - 1))
                    xs = X[:, m, bk * 512:(bk + 1) * 512]
                    P.stt(xs, wb[:, 0:512], modT[:, l, 16 + m:17 + m], xs, ALU.mult, ALU.add)
                    if m % 2 == 1:
                        yield

        def rr(gens):
            while gens:
                for g_ in list(gens):
                    try:
                        next(g_)
                    except StopIteration:
                        gens.remove(g_)
        rr([stage0(0)])
        rr([stage1(0), stage0(1)])
        for t in range(16):
            gens = [stage2(t)]
            if t + 2 < 16:
                gens.append(stage0(t + 2))
            if t + 1 < 16:
                gens.append(stage1(t + 1))
            rr(gens)

    def ffn(l, nextmod=None):
        RB = Bump(RB0, RLIM)
        h2 = RB.alloc(KC * T * 2, BF16, "p (k t) -> p k t", k=KC)
        actb = RB.alloc(6 * T * 2, BF16, "p (j t) -> p j t", j=6)
        wd = RB.alloc(6 * D * 2, BF16, "p (j c) -> p j c", j=6)
        wu = RB.alloc(2 * KC * 256 * 2, BF16, "p (s k c) -> p s k c", s=2, k=KC)
        NT = 128
        ntmps = []
        for i_ in range(2):
            sq_ = RB.alloc(KC * NT * 2, BF16, "p (k t) -> p k t", k=KC)
            rstd_ = RB.alloc(NT * 4, F32)
            lnv_ = RB.alloc(NT * 4, F32)
            tmpf_ = RB.alloc(KC * NT * 4, F32, "p (k t) -> p k t", k=KC)
            ntmps.append((sq_, rstd_, lnv_, tmpf_))
        mod_derive(l, 1)
        U = RB.alloc(3 * 2 * 514 * 4, F32, "p (s q c) -> p s q c", s=3, q=2)
        Y = RB.alloc(3 * 2 * 512 * 4, F32, "p (s q c) -> p s q c", s=3, q=2)
        wuv = w_up[l].rearrange("(k p) c -> p k c", p=128)
        NSB = T // NT
        def normgen():
            norm_a(0, NT, ntmps[0], psn=ps[6][:, 0:NT])
            for i in range(NSB):
                if i + 1 < NSB:
                    norm_a((i + 1) * NT, NT, ntmps[(i + 1) % 2], psn=ps[6][:, ((i + 1) % 2) * NT:((i + 1) % 2 + 1) * NT])
                norm_b(i * NT, NT, h2[:, :, i * NT:(i + 1) * NT], ntmps[i % 2], A=a12[:, l, 1, :], S=modT[:, l, 24:32])
                yield i
        ngen = normgen()
        ndone = [-1]
        def ensure_norm(tb):
            need = (tb + 1) * (512 // NT) - 1
            while ndone[0] < need:
                ndone[0] = next(ngen)
        def load_wu(j):
            slot = j % 2
            P.dma('pool', wu[:, slot, :, 0:128], wuv[:, :, j * 128:(j + 1) * 128], sem="wu%d" % slot, grp=("wu", l, j))
            P.dma('pool', wu[:, slot, :, 128:256], wuv[:, :, DFF + j * 128:DFF + (j + 1) * 128], sem="wu%d" % slot, grp=("wu", l, j))
        def load_wd(J):
            for jl, j in enumerate(J):
                P.dma('pool', wd[:, jl, :], w_down[l][j * 128:(j + 1) * 128, :], sem="wd%d" % jl)
        gstep = [0]
        ring4 = [arena[:, RLIM - 2048 * (i + 1):RLIM - 2048 * i].bitcast(BF16).rearrange("p (k c) -> p k c", k=8) for i in range(4)]
        msch = ModSched(nextmod, mode='bf16', ring=ring4, semp="modr4_%d") if nextmod is not None else None
        def stageA(st):
            jl, j, tb, g = st
            par = g % 2
            ensure_norm(tb)
            if msch is not None and not msch.done():
                msch.tick()
            slot = j % 2
            for qi in range(2):
                pst = ps[2 * par + qi]
                for k in range(KC):
                    P.mm(pst[:, :], wu[:, slot, k, qi * 128:(qi + 1) * 128], h2[:, k, tb * 512:(tb + 1) * 512],
                         start=(k == 0), stop=(k == KC - 1))
        def cw(j, qi):
            ch = qi * NJ + j
            return (col(("cw", l), 0 * 44 + ch, 1), col(("cw", l), 1 * 44 + ch, 1), col(("cw", l), 2 * 44 + ch, 1), col(("cb", l), ch, 1))
        def stageB(st):
            jl, j, tb, g = st
            par = g % 2; p3 = g % 3
            for qi in range(2):
                pst = ps[2 * par + qi]
                w0, w1, w2, bb = cw(j, qi)
                Ub = U[:, p3, qi, :]; Yb = Y[:, p3, qi, :]
                if tb == 0:
                    P.memset('pool', Ub[:, 0:2], 0.0)
                else:
                    P.copy('pool', Ub[:, 0:2], U[:, (g - 1) % 3, qi, 512:514])
                P.copy('act', Ub[:, 2:514], pst[:, :])
                P.act(Yb, pst[:, :], AF.Identity, bias=bb, scale=w2)
        def stageC(st):
            jl, j, tb, g = st
            p3 = g % 3
            for qi in range(2):
                w0, w1, w2, bb = cw(j, qi)
                Ub = U[:, p3, qi, :]; Yb = Y[:, p3, qi, :]
                P.stt(Yb, Ub[:, 1:513], w1, Yb, ALU.mult, ALU.add)
                P.stt(Yb, Ub[:, 0:512], w0, Yb, ALU.mult, ALU.add)
        def stageD1(st):
            jl, j, tb, g = st
            p3 = g % 3
            P.act(Y[:, p3, 1, :], Y[:, p3, 1, :], AF.Silu)
        def stageD2(st):
            jl, j, tb, g = st
            p3 = g % 3
            P.tt('dve', actb[:, jl, tb * 512:(tb + 1) * 512], Y[:, p3, 0, :], Y[:, p3, 1, :], ALU.mult)
        ycnt = 0
        load_wd(QUARTERS[0])
        load_wu(QUARTERS[0][0])
        def make_steps(q):
            J = QUARTERS[q]
            steps = []
            for jl, j in enumerate(J):
                for tb in range(4):
                    steps.append((jl, j, tb, gstep[0])); gstep[0] += 1
            return steps
        def prefetch_for(q, steps, idx):
            J = QUARTERS[q]
            jl, j, tb, g = steps[idx]
            if tb == 0:
                if jl + 1 < len(J):
                    load_wu(J[jl + 1])
                elif q + 1 < len(QUARTERS):
                    load_wu(QUARTERS[q + 1][0])
        def prologue(q, steps):
            prefetch_for(q, steps, 0)
            stageA(steps[0]); stageB(steps[0])
            prefetch_for(q, steps, 1)
            stageA(steps[1]); stageB(steps[1])
        cur_steps = make_steps(0)
        prologue(0, cur_steps)
        for q, J in enumerate(QUARTERS):
            steps = cur_steps
            n_ = len(steps)
            stageC(steps[0])
            for i in range(n_):
                if i + 2 < n_:
                    prefetch_for(q, steps, i + 2)
                    stageA(steps[i + 2]); stageB(steps[i + 2])
                if i + 1 < n_:
                    stageC(steps[i + 1])
                stageD1(steps[i]); stageD2(steps[i])
            if q + 1 < len(QUARTERS):
                cur_steps = make_steps(q + 1)
                prologue(q + 1, cur_steps)
            for tb in range(4):
                for m in range(KC):
                    pst = ps[4 + ycnt % 2]; ycnt += 1
                    for jl in range(len(J)):
                        P.mm(pst[:, :], wd[:, jl, m * 128:(m + 1) * 128], actb[:, jl, tb * 512:(tb + 1) * 512],
                             start=(jl == 0), stop=(jl == len(J) - 1))
                    xs = X[:, m, tb * 512:(tb + 1) * 512]
                    P.stt(xs, pst[:, :], modT[:, l, 40 + m:41 + m], xs, ALU.mult, ALU.add)
            if q + 1 < len(QUARTERS):
                load_wd(QUARTERS[q + 1])
        if msch is not None:
            msch.run_until(48)
            mod_finish(nextmod, 0, 48)

    def final():
        RB = Bump(RB0, RLIM)
        NT = 256
        tmps = []
        for i in range(2):
            sq = RB.alloc(KC * NT * 2, BF16, "p (k t) -> p k t", k=KC)
            rstd = RB.alloc(NT * 4, F32)
            lnv = RB.alloc(NT * 4, F32)
            tmpf = RB.alloc(KC * NT * 4, F32, "p (k t) -> p k t", k=KC)
            tmps.append((sq, rstd, lnv, tmpf))
        ob = RB.alloc(2 * KC * NT * 4, F32, "p (s k t) -> p s k t", s=2, k=KC)
        nb = T // NT
        norm_a(0, NT, tmps[0], psn=ps[6][:, 0:NT])
        for i in range(nb):
            t0 = i * NT
            if i + 1 < nb:
                norm_a(t0 + NT, NT, tmps[(i + 1) % 2], psn=ps[6 + (i + 1) % 2][:, 0:NT])
            norm_b(t0, NT, ob[:, i % 2], tmps[i % 2], G=col("gf"))
            P.dma('sp' if i % 2 == 0 else 'act', ov[:, :, t0:t0 + NT], ob[:, i % 2], sem="out%d" % (i % 2))

    layers = list(layers)
    for li, l in enumerate(layers):
        nxt = layers[li + 1] if li + 1 < len(layers) else None
        m0 = None
        pre = None
        if li == 0 or not do_ffn:
            if do_mixer:
                def pre():
                    ma = ModSched(l, mode='f32', lo=0, hi=16)
                    ma.run_until(16)
                    mod_finish(l, 0, 16)
                m0 = ModSched(l, mode='bf16', lo=16, hi=48)
                m0.where = lambda jc: ps[0][:, 384 + jc - 16:385 + jc - 16]
            else:
                m0 = ModSched(l)
                m0.run_until(48)
                mod_finish(l, 0, 48)
                m0 = None
        if do_mixer:
            mixer(l, msch=m0, pre=pre)
        if do_ffn:
            ffn(l, nextmod=nxt)
    final()
    st = P.emit()
    return st


def kernel(**inputs):
    inp = {k: np.asarray(v) for k, v in inputs.items()}
    nc = bass.Bass("TRN2", target_bir_lowering=False)
    build(nc)
    in_maps = [host_inputs(inp, b) for b in range(8)]
    res = run_bass_kernel_spmd(nc, in_maps, core_ids=list(range(8)))
    out = np.stack([np.asarray(res.results[b]["outT"]).T for b in range(8)], axis=0)
    return np.ascontiguousarray(out, dtype=np.float32)
```

```python
import math
import numpy as np
import concourse.bass as bass
import concourse.mybir as mybir
from concourse.bass_utils import run_bass_kernel_spmd

F32 = mybir.dt.float32; BF16 = mybir.dt.bfloat16; I32 = mybir.dt.int32; U8 = mybir.dt.uint8
ALU = mybir.AluOpType; AF = mybir.ActivationFunctionType


def _rng(ap):
    sp = str(ap.space)
    if 'SB' not in sp and 'PSUM' not in sp:
        return None
    if 'PSUM' in sp:
        pat = ap.ap
        ds = mybir.dt.size(ap.dtype)
        pstep = pat[0][0]
        col0 = ap.offset % pstep if pstep > 0 else ap.offset
        ext = 1
        for st, cn in pat[1:]:
            ext += (cn - 1) * abs(st)
        b0 = (col0 * ds) // 2048; b1 = ((col0 + ext) * ds - 1) // 2048
        return [("%s#%d" % (ap.tensor.name, b), (0, 2048, ((0, 2048),))) for b in range(b0, b1 + 1)]
    pat = ap.ap
    ds = mybir.dt.size(ap.dtype)
    pstep = pat[0][0]
    col0 = ap.offset % pstep if pstep > 0 else ap.offset
    dims = sorted([(abs(st), cn) for st, cn in pat[1:] if cn > 1 and st != 0])
    run = 1
    rest = []
    for st, cn in dims:
        if st <= run:
            run = max(run, (cn - 1) * st + run)
        else:
            rest.append((st, cn))
    starts = [0]
    nint = 1
    for st, cn in rest:
        nint *= cn
    if nint > 64:
        ext = run
        for st, cn in rest:
            ext += (cn - 1) * st
        ivs = [(col0 * ds, (col0 + ext) * ds)]
    else:
        for st, cn in rest:
            starts = [a + i * st for a in starts for i in range(cn)]
        starts.sort()
        ivs = []
        for a in starts:
            s0 = (col0 + a) * ds; e0 = (col0 + a + run) * ds
            if ivs and s0 <= ivs[-1][1]:
                ivs[-1] = (ivs[-1][0], max(ivs[-1][1], e0))
            else:
                ivs.append((s0, e0))
    return (ap.tensor.name, (ivs[0][0], ivs[-1][1], tuple(ivs)))


def _ovl(a, b):
    if not (a[0] < b[1] and b[0] < a[1]):
        return False
    ia = a[2]; ib = b[2]
    if len(ia) == 1 and len(ib) == 1:
        return True
    i = j = 0
    while i < len(ia) and j < len(ib):
        if ia[i][0] < ib[j][1] and ib[j][0] < ia[i][1]:
            return True
        if ia[i][1] <= ib[j][1]:
            i += 1
        else:
            j += 1
    return False


def _covers(a, b):
    if not (a[0] <= b[0] and b[1] <= a[1]):
        return False
    ia = a[2]
    for (s, e) in b[2]:
        ok = False
        for (s2, e2) in ia:
            if s2 <= s and e <= e2:
                ok = True; break
        if not ok:
            return False
    return True


class Prog:
    ENG = ('pe', 'act', 'dve', 'pool', 'sp')

    def __init__(self, nc):
        self.nc = nc
        self.ops = []
        self.grp_ctr = 0

    def op(self, eng, fn, r=(), w=()):
        def flat(lst):
            out = []
            for x in lst:
                if not x:
                    continue
                if isinstance(x, list):
                    out.extend(x)
                else:
                    out.append(x)
            return out
        rr = flat(_rng(a) for a in r if a is not None and not isinstance(a, (int, float)))
        ww = flat(_rng(a) for a in w)
        ww = ww + [x for x in rr if x[0].startswith('ps') and x not in ww]
        self.ops.append(dict(k='c', eng=eng, fn=fn, r=rr, w=ww))

    def dma(self, eng, out, in_, sem, grp=None, **kw):
        if grp is None:
            self.grp_ctr += 1
            grp = ('_g', self.grp_ctr)
        rr = [x for x in (_rng(in_),) if x and not isinstance(x, list)]
        ww = [x for x in (_rng(out),) if x and not isinstance(x, list)]
        self.ops.append(dict(k='d', eng=eng, out=out, in_=in_, r=rr, w=ww, sem=sem, grp=grp, kw=kw))

    def mm(self, out, lhsT, rhs, start=True, stop=True):
        self.op('pe', lambda e: e.matmul(out, lhsT, rhs, start=start, stop=stop), r=[lhsT, rhs], w=[out])

    def tr(self, out, in_, ident):
        self.op('pe', lambda e: e.transpose(out, in_, ident), r=[in_, ident], w=[out])

    def act(self, out, in_, func, bias=None, scale=None, eng='act'):
        kw = {}
        if bias is not None: kw['bias'] = bias
        if scale is not None: kw['scale'] = scale
        self.op(eng, lambda e: e.activation(out, in_, func, **kw), r=[in_, bias, scale], w=[out])

    def tt(self, eng, out, in0, in1, op):
        self.op(eng, lambda e: e.tensor_tensor(out, in0, in1, op), r=[in0, in1], w=[out])

    def ts(self, eng, out, in0, s1, s2, op0, op1=None):
        if op1 is None:
            self.op(eng, lambda e: e.tensor_scalar(out, in0, s1, None, op0), r=[in0, s1], w=[out])
        else:
            self.op(eng, lambda e: e.tensor_scalar(out, in0, s1, s2, op0, op1), r=[in0, s1, s2], w=[out])

    def stt(self, out, in0, scalar, in1, op0, op1, eng='dve'):
        self.op(eng, lambda e: e.scalar_tensor_tensor(out, in0, scalar, in1, op0, op1), r=[in0, scalar, in1], w=[out])

    def copy(self, eng, out, in_):
        if eng == 'act':
            self.op(eng, lambda e: e.copy(out, in_), r=[in_], w=[out])
        else:
            self.op(eng, lambda e: e.tensor_copy(out, in_), r=[in_], w=[out])

    def memset(self, eng, out, val):
        self.op(eng, lambda e: e.memset(out, val), r=[], w=[out])

    def emit(self):
        nc = self.nc
        engs = {'pe': nc.tensor, 'act': nc.scalar, 'dve': nc.vector, 'pool': nc.gpsimd, 'sp': nc.sync}
        ops = self.ops
        n = len(ops)
        W = {}
        R = {}
        deps = [None] * n
        needed = [False] * n
        for i, o in enumerate(ops):
            raw = set(); oth = set()
            for (sp, f) in o['r']:
                for rec in W.get(sp, ()):
                    if _ovl(rec[0], f):
                        raw.add(rec[1])
            for (sp, f) in o['w']:
                for rec in W.get(sp, ()):
                    if _ovl(rec[0], f):
                        oth.add(rec[1])
                for rec in R.get(sp, ()):
                    if _ovl(rec[0], f):
                        oth.add(rec[1])
            d = set()
            for j in raw | oth:
                pj = ops[j]
                if j not in raw and o['k'] == 'c' and pj['k'] == 'c' and pj['eng'] == o['eng'] == 'pe':
                    continue
                if o['k'] == 'd' and pj['k'] == 'd' and o['sem'] == pj['sem'] and o['grp'] == pj['grp']:
                    continue
                d.add(j)
            d.discard(i)
            deps[i] = d
            for j in d: needed[j] = True
            for (sp, f) in o['w']:
                W[sp] = [rec for rec in W.get(sp, ()) if not _covers(f, rec[0])]
                R[sp] = [rec for rec in R.get(sp, ()) if not _covers(f, rec[0])]
                W[sp].append((f, i))
            for (sp, f) in o['r']:
                lst = R.setdefault(sp, [])
                eng = o['eng']
                if o['k'] == 'c':
                    lst[:] = [rec for rec in lst if not (rec[0] == f and ops[rec[1]]['k'] == 'c' and ops[rec[1]]['eng'] == eng)]
                lst.append((f, i))
        esem = {e: nc.alloc_semaphore("s_" + e) for e in engs}
        dsem = {}
        cnt = {e: 0 for e in engs}
        dcnt = {}
        tok = [None] * n
        grp_final = {}
        for i, o in enumerate(ops):
            if o['k'] == 'c':
                if needed[i]:
                    cnt[o['eng']] += 1
                    tok[i] = (esem[o['eng']], cnt[o['eng']])
            else:
                s = o['sem']
                if s not in dsem:
                    dsem[s] = nc.alloc_semaphore("d_%d" % len(dsem)); dcnt[s] = 0
                dcnt[s] += 16
                grp_final[(s, o['grp'])] = dcnt[s]
        for i, o in enumerate(ops):
            if o['k'] == 'd':
                tok[i] = (dsem[o['sem']], grp_final[(o['sem'], o['grp'])])
        waited = {e: {} for e in engs}
        nwaits = 0
        for i, o in enumerate(ops):
            e = o['eng']; E = engs[e]
            need = {}
            for j in deps[i]:
                s, v = tok[j]
                if need.get(s, 0) < v: need[s] = v
            for s, v in need.items():
                if waited[e].get(s, 0) >= v: continue
                E.wait_ge(s, v); waited[e][s] = v; nwaits += 1
            if o['k'] == 'c':
                ins = o['fn'](E)
                if needed[i]:
                    ins.then_inc(tok[i][0], 1)
            else:
                ins = E.dma_start(out=o['out'], in_=o['in_'], **o['kw'])
                ins.then_inc(dsem[o['sem']], 16)
        E = engs['sp']
        for s, v in dcnt.items():
            E.wait_ge(dsem[s], v)
        self.stats = dict(n=n, nwaits=nwaits, cnt=dict(cnt), nsem=len(esem) + len(dsem))
        return self.stats


D = 1024; T = 2048; KC = 8; DIN = 2576; DFF = 2816; NJ = 22
EPS = 1e-6
QUARTERS = [list(range(0, 6)), list(range(6, 12)), list(range(12, 17)), list(range(17, 22))]

COLS = {}
_o = 0
for _l in range(2):
    for _nm, _n in (("g1", 8), ("adab", 48), ("pscale", 2), ("g2", 8), ("cw", 132), ("cb", 44)):
        COLS[(_nm, _l)] = (_o, _n); _o += _n
COLS["gf"] = (_o, 8); _o += 8
COLS["c"] = (_o, 8); _o += 8
NCOL = _o


def host_cols(inp, b):
    cols = np.zeros((128, NCOL), np.float32)
    def put(key, arr):
        o, n = COLS[key]
        cols[:, o:o + n] = arr
    for l in range(2):
        put(("g1", l), inp["norm1_g"][l].reshape(8, 128).T)
        put(("adab", l), inp["ada_b"][l].reshape(48, 128).T)
        put(("pscale", l), inp["pool_scale"][l].reshape(2, 128).T)
        put(("g2", l), inp["norm2_g"][l].reshape(8, 128).T)
        cw = inp["conv_w"][l].reshape(3, 44, 128).transpose(2, 0, 1).reshape(128, 132)
        put(("cw", l), cw)
        put(("cb", l), inp["conv_b"][l].reshape(44, 128).T)
    put("gf", inp["final_g"].reshape(8, 128).T)
    put("c", inp["c"][b].reshape(8, 128).T)
    return cols


SC = 48.0 ** -0.5
GAM = [1.0 - 2.0 ** (-5.0 - h) for h in range(4)]
GAMC = [g ** 128 for g in GAM]
CT = {}
_o = 0
for _nm, _n in (("tri", 128), ("DT", 512), ("xi", 512), ("zeta", 4), ("invf", 24), ("one", 1)):
    CT[_nm] = (_o, _n); _o += _n
NCT = _o
CBT = {}
_o = 0
for _nm, _n in (("ident", 128), ("causal", 128), ("Pcur", 512), ("Pprev", 512), ("Pfirst", 512), ("onesrow", 128)):
    CBT[_nm] = (_o, _n); _o += _n
NCB = _o
_HC = {}


def host_consts():
    if _HC:
        return _HC["ct"], _HC["cb"]
    ct = np.zeros((128, NCT), np.float64)
    cb = np.zeros((128, NCB), np.float64)
    j = np.arange(128)[:, None]; i = np.arange(128)[None, :]
    tri = (j <= i).astype(np.float64)
    ct[:, CT["tri"][0]:CT["tri"][0] + 128] = -tri / 16.0
    DT = np.zeros((128, 4, 128)); xi = np.zeros((128, 4, 128)); zeta = np.zeros((128, 4))
    for h in range(4):
        DT[:, h, :] = np.where(i >= j, SC * GAM[h] ** np.maximum(i - j, 0), 0.0)
        xi[:, h, :] = SC * GAM[h] ** (i + 1.0)
        zeta[:, h] = GAM[h] ** (127.0 - np.arange(128))
    ct[:, CT["DT"][0]:CT["DT"][0] + 512] = DT.reshape(128, 512)
    ct[:, CT["xi"][0]:CT["xi"][0] + 512] = xi.reshape(128, 512)
    ct[:, CT["zeta"][0]:CT["zeta"][0] + 4] = zeta
    ct[:, CT["invf"][0]:CT["invf"][0] + 24] = (10000.0 ** (-np.arange(0, 48, 2) / 48.0))[None, :]
    ct[:, CT["one"][0]] = -1.0 / 16.0
    cb[:, CBT["ident"][0]:CBT["ident"][0] + 128] = np.eye(128)
    cb[:, CBT["causal"][0]:CBT["causal"][0] + 128] = tri
    Pc = np.zeros((128, 4, 128)); Pp = np.zeros((128, 4, 128)); Pf = np.zeros((128, 4, 128))
    for g, w in enumerate((2, 4, 8, 16)):
        Pc[:, g, :] = np.where((j <= i) & (j > i - w), 1.0 / w, 0.0) - (j == i)
        Pp[:, g, :] = np.where(j - 128 > i - w, 1.0 / w, 0.0)
        cnt = np.minimum(i + 1, w)
        Pf[:, g, :] = np.where((j <= i) & (j > i - w), 1.0 / cnt, 0.0) - (j == i)
    cb[:, CBT["Pcur"][0]:CBT["Pcur"][0] + 512] = Pc.reshape(128, 512)
    cb[:, CBT["Pprev"][0]:CBT["Pprev"][0] + 512] = Pp.reshape(128, 512)
    cb[:, CBT["Pfirst"][0]:CBT["Pfirst"][0] + 512] = Pf.reshape(128, 512)
    cb[:, CBT["onesrow"][0]:CBT["onesrow"][0] + 128] = 1.0
    _HC["ct"] = ct.astype(np.float32); _HC["cb"] = cb.astype(np.float32)
    return _HC["ct"], _HC["cb"]


def host_inputs(inp, b):
    ct, cb = host_consts()
    return {"xT": np.ascontiguousarray(inp['x'][b].T), "cols": host_cols(inp, b),
            "pos": np.ascontiguousarray(inp['positions'][b].reshape(16, 128).T),
            "ctab": ct, "cbt": cb,
            "gng": np.ascontiguousarray(np.broadcast_to(inp['gla_norm_g'][:, None, :], (2, 128, 384))),
            "ada_w": inp['ada_w'], "w_in": inp['w_in'], "w_out": inp['w_out'], "pool_w": inp['pool_w'],
            "gla_wa2": inp['gla_wa2'], "gla_ba": inp['gla_ba'],
            "w_up": inp['w_up'], "w_down": inp['w_down']}


def build(nc, layers=(0, 1), do_mixer=True, do_ffn=True, parts=("pool", "ret", "gla")):
    P = Prog(nc)
    dr = {}
    def din(name, shape, dt=F32):
        dr[name] = nc.dram_tensor(name, list(shape), dt, kind="ExternalInput").ap()
        return dr[name]
    xT = din("xT", [D, T])
    cols_d = din("cols", [128, NCOL])
    ada_w = din("ada_w", [2, D, 6 * D])
    pos_d = din("pos", [128, 16], I32)
    ctab_d = din("ctab", [128, NCT])
    cbt_d = din("cbt", [128, NCB])
    gng_d = din("gng", [2, 128, 384])
    w_in = din("w_in", [2, D, DIN])
    w_out = din("w_out", [2, D, D])
    pool_w = din("pool_w", [2, 4, 64, 64])
    gla_wa2 = din("gla_wa2", [2, 16, 192])
    gla_ba = din("gla_ba", [2, 192])
    w_up = din("w_up", [2, D, 2 * DFF])
    w_down = din("w_down", [2, DFF, D])
    outT = nc.dram_tensor("outT", [D, T], F32, kind="ExternalOutput").ap()

    arena = nc.alloc_sbuf_tensor("arena", [128, 207 * 1024], U8)
    ps = [nc.alloc_psum_tensor("ps%d" % i, [128, 512], F32) for i in range(6)]
    ps67 = nc.alloc_psum_tensor("ps67", [128, 1024], F32)
    ps.append(ps67[:, 0:512]); ps.append(ps67[:, 512:1024])

    class Bump:
        def __init__(self, base, limit): self.o = base; self.base = base; self.limit = limit
        def alloc(self, nbytes, dt, pat=None, **kw):
            assert self.o + nbytes <= self.limit, (self.o, nbytes, self.limit)
            a = arena[:, self.o:self.o + nbytes].bitcast(dt)
            self.o += (nbytes + 31) // 32 * 32
            return a.rearrange(pat, **kw) if pat else a

    X = arena[:, 0:65536].bitcast(F32).rearrange("p (k t) -> p k t", k=KC)
    CB = Bump(65536, 65536 + 18 * 1024)
    cols = CB.alloc(NCOL * 4, F32)
    def col(key, a=0, n=None):
        o, nn = COLS[key]
        n = nn - a if n is None else n
        return cols[:, o + a:o + a + n]
    modT = CB.alloc(2 * 48 * 4, F32, "p (l j) -> p l j", l=2)
    a12 = CB.alloc(2 * 2 * 8 * 4, F32, "p (l s k) -> p l s k", l=2, s=2)
    cact = CB.alloc(8 * 2, BF16)
    cact32 = CB.alloc(8 * 4, F32)
    onesb = CB.alloc(128 * 2, BF16)
    RB0 = CB.limit
    RLIM = 207 * 1024
    modring = arena[:, RLIM - 4096:RLIM].bitcast(BF16).rearrange("p (s k c) -> p s k c", s=2, k=8)
    modring32 = arena[:, RLIM - 8192:RLIM].bitcast(F32).rearrange("p (s k c) -> p s k c", s=2, k=8)

    xv = xT.rearrange("(k p) t -> p k t", p=128)
    ov = outT.rearrange("(k p) t -> p k t", p=128)
    P.dma('sp', cols, cols_d, sem="cols", grp="cols")
    posi = CB.alloc(16 * 4, I32)
    P.dma('sp', posi, pos_d, sem="cols", grp="cols")
    for k in range(KC):
        P.dma('sp' if k % 2 == 0 else 'act', X[:, k, :], xv[:, k, :], sem="xin", grp="xin")
    P.memset('dve', onesb, 1.0 / 1024.0)
    P.act(cact, col("c"), AF.Silu)
    P.act(cact32, col("c"), AF.Silu)

    PS_NORM = ps[6]; PS_MOD = ps[7]

    class ModSched:
        def __init__(self, l, mode='bf16', lo=0, hi=48, ring=None, semp=None):
            self.l = l; self.nd = lo; self.nm = lo; self.mode = mode; self.hi = hi
            self.wv = ada_w[l].rearrange("(k p) c -> p k c", p=128)
            if ring is None:
                ring = [modring[:, i] for i in range(2)] if mode == 'bf16' else [modring32[:, i] for i in range(2)]
            self.ring = ring; self.depth = len(ring)
            self.semp = semp if semp is not None else ("modring%d" if mode == 'bf16' else "modr32_%d")
        def where(self, jc):
            return PS_MOD[:, jc:jc + 1]
        def tick(self):
            if self.nd < self.hi and self.nd - self.nm < self.depth:
                jc = self.nd; self.nd += 1
                sl = jc % self.depth
                P.dma('pool' if self.mode == 'bf16' else 'sp', self.ring[sl], self.wv[:, :, jc * 128:(jc + 1) * 128], sem=self.semp % sl)
                if self.nd - self.nm < self.depth and self.nd < self.hi:
                    return
            if self.nm < self.nd:
                jc = self.nm; self.nm += 1
                sl = jc % self.depth
                out = self.where(jc)
                rhs = cact if self.mode == 'bf16' else cact32
                for k in range(KC):
                    P.mm(out, self.ring[sl][:, k, :], rhs[:, k:k + 1], start=(k == 0), stop=(k == KC - 1))
        def run_until(self, n):
            while self.nm < n:
                self.tick()
        def done(self):
            return self.nm >= self.hi
    def mod_finish(l, j0, j1, src=None):
        src = PS_MOD[:, j0:j1] if src is None else src
        P.tt('dve', modT[:, l, j0:j1], src, col(("adab", l), j0, j1 - j0), ALU.add)
    def mod_derive(l, which):
        sc = modT[:, l, 8:16] if which == 0 else modT[:, l, 32:40]
        g = col(("g1", l)) if which == 0 else col(("g2", l))
        P.stt(a12[:, l, which, :], sc, 1.0, g, ALU.add, ALU.mult)

    def norm_a(t0, n, tmp, psn=None):
        sq, rstd, lnv, tmpf = tmp
        PSN = psn if psn is not None else PS_NORM[:, 0:n]
        xs = X[:, :, t0:t0 + n]
        P.act(sq[:, :, 0:n], xs, AF.Square)
        for k in range(KC):
            P.mm(PSN, onesb, sq[:, k, 0:n], start=(k == 0), stop=(k == KC - 1))
        P.act(lnv[:, 0:n], PSN, AF.Ln, bias=epsc[:, 0:1], scale=1.0)
        P.act(rstd[:, 0:n], lnv[:, 0:n], AF.Exp, scale=-0.5)

    def norm_b(t0, n, dst, tmp, A=None, S=None, G=None, alt='dve'):
        sq, rstd, lnv, tmpf = tmp
        xs = X[:, :, t0:t0 + n]
        rb = rstd[:, 0:n].unsqueeze(1).broadcast_to([128, KC, n])
        P.tt('dve', tmpf[:, :, 0:n], xs, rb, ALU.mult)
        if alt == 'dve2':
            P.tt('dve', tmpf[:, :, 0:n], tmpf[:, :, 0:n], A.unsqueeze(2).broadcast_to([128, KC, n]), ALU.mult)
            P.tt('dve', dst, tmpf[:, :, 0:n], S.unsqueeze(2).broadcast_to([128, KC, n]), ALU.add)
            return
        if alt == 'pool2':
            P.tt('pool', tmpf[:, :, 0:n], tmpf[:, :, 0:n], A.unsqueeze(2).broadcast_to([128, KC, n]), ALU.mult)
            P.tt('pool', dst, tmpf[:, :, 0:n], S.unsqueeze(2).broadcast_to([128, KC, n]), ALU.add)
            return
        for k in range(KC):
            if G is not None:
                if k % 2 == 0 or alt == 'act':
                    P.act(dst[:, k, :], tmpf[:, k, 0:n], AF.Identity, scale=G[:, k:k + 1])
                else:
                    P.ts(alt, dst[:, k, :], tmpf[:, k, 0:n], G[:, k:k + 1], None, ALU.mult)
            else:
                if k % 2 == 0 or alt == 'act':
                    P.act(dst[:, k, :], tmpf[:, k, 0:n], AF.Identity, bias=S[:, k:k + 1], scale=A[:, k:k + 1])
                else:
                    P.ts(alt, dst[:, k, :], tmpf[:, k, 0:n], A[:, k:k + 1], S[:, k:k + 1], ALU.mult, ALU.add)

    def norm(RB, t0, n, dst, A=None, S=None, G=None, tmp=None, psn=None, alt='dve'):
        norm_a(t0, n, tmp, psn=psn)
        norm_b(t0, n, dst, tmp, A=A, S=S, G=G, alt=alt)

    epsc = CB.alloc(4, F32)
    P.memset('dve', epsc, EPS)


    kcol = CB.alloc(8 * 4, F32)
    P.memset('dve', kcol[:, 0:1], EPS)
    P.memset('dve', kcol[:, 1:2], 1.0)
    P.memset('dve', kcol[:, 2:3], math.log(SC))
    ctab = CB.alloc(NCT * 4, F32)
    P.dma('sp', ctab, ctab_d, sem="cols", grp="cols")
    def ct(key):
        o, n = CT[key]
        return ctab[:, o:o + n]
    cbt = CB.alloc(NCB * 2, BF16)
    P.dma('pool', cbt, cbt_d, sem="cbt")
    def cbv(key):
        o, n = CBT[key]
        return cbt[:, o:o + n]
    ident = cbv("ident"); causal = cbv("causal"); onesrow = cbv("onesrow")
    Pcur = cbv("Pcur").rearrange("p (g i) -> p g i", g=4)
    Pprev = cbv("Pprev").rearrange("p (g i) -> p g i", g=4)
    Pfirst = cbv("Pfirst").rearrange("p (g i) -> p g i", g=4)
    tri = ct("tri"); DTm = ct("DT").rearrange("p (h i) -> p h i", h=4)
    xit = ct("xi").rearrange("p (h i) -> p h i", h=4); zeta = ct("zeta"); invf = ct("invf"); one32 = ct("one")
    costab = CB.alloc(16 * 24 * 4, F32, "p (t i) -> p t i", t=16)
    sintab = CB.alloc(16 * 24 * 4, F32, "p (t i) -> p t i", t=16)
    wa2b = CB.alloc(192 * 2, BF16)
    bab = CB.alloc(192 * 2, BF16)
    PWblk = CB.alloc(2 * 128 * 2, BF16, "p (a c) -> p a c", a=2)
    gng = CB.alloc(384 * 4, F32)

    def trig_tables():
        TB_ = Bump(RB0, RLIM)
        posf = TB_.alloc(16 * 4, F32)
        ang = TB_.alloc(384 * 4, F32, "p (t i) -> p t i", t=16)
        a2 = TB_.alloc(384 * 4, F32, "p (t i) -> p t i", t=16)
        kf = TB_.alloc(384 * 4, F32, "p (t i) -> p t i", t=16)
        rr = TB_.alloc(384 * 4, F32, "p (t i) -> p t i", t=16)
        P.copy('dve', posf, posi)
        P.tt('dve', ang, posf.unsqueeze(2).broadcast_to([128, 16, 24]), invf.unsqueeze(1).broadcast_to([128, 16, 24]), ALU.mult)
        MAGIC = 12582912.0
        C1 = 6.28125; C2 = 2.0 * math.pi - 6.28125
        for tab, shift in ((sintab, 0.0), (costab, math.pi / 2)):
            P.ts('dve', a2, ang, shift, None, ALU.add)
            P.ts('dve', kf, a2, 1.0 / (2.0 * math.pi), MAGIC, ALU.mult, ALU.add)
            P.ts('dve', kf, kf, MAGIC, None, ALU.subtract)
            P.stt(rr, kf, -C1, a2, ALU.mult, ALU.add)
            P.stt(rr, kf, -C2, rr, ALU.mult, ALU.add)
            P.ts('dve', rr, rr, math.pi, -math.pi, ALU.min, ALU.max)
            P.act(tab, rr, AF.Sin)
    trig_tables()


    def mixer(l, msch=None, pre=None):
        RB = Bump(RB0, RLIM)
        Win = RB.alloc(KC * DIN * 2, BF16, "p (k c) -> p k c", k=KC)
        Wout = RB.alloc(KC * D * 2, BF16, "p (k c) -> p k c", k=KC)
        hT2 = RB.alloc(2 * KC * 128 * 2, BF16, "p (s k t) -> p s k t", s=2, k=KC)
        yT = RB.alloc(KC * 512 * 2, BF16, "p (k t) -> p k t", k=KC)
        sq = RB.alloc(KC * 128 * 2, BF16, "p (k t) -> p k t", k=KC)
        rstd = RB.alloc(128 * 4, F32); lnv = RB.alloc(128 * 4, F32)
        tmpf = RB.alloc(KC * 128 * 4, F32, "p (k t) -> p k t", k=KC)
        ubf = RB.alloc(3 * 256 * 2, BF16, "p (s c) -> p s c", s=3)
        pooledT = RB.alloc(2 * 128 * 2, BF16, "p (a t) -> p a t", a=2)
        rotA = RB.alloc(384 * 4, F32); rotB = RB.alloc(384 * 4, F32)
        gaT = RB.alloc(128 * 2, BF16)
        e1 = RB.alloc(192 * 4, F32); spb = e1
        eb = RB.alloc(192 * 4, F32); enb = RB.alloc(192 * 4, F32)
        ebl = RB.alloc(2 * 4 * 4, F32, "p (s h) -> p s h", s=2)
        ybuf = RB.alloc(2 * 384 * 2, BF16, "p (m c) -> p m c", m=2)
        MX = []
        for mi in range(2):
            d = {}
            d["qk"] = RB.alloc(2 * 8 * 48 * 2, BF16, "p (s c d) -> p s c d", s=2, c=8)
            d["qkT"] = RB.alloc(2 * 8 * 128 * 2, BF16, "p (s c t) -> p s c t", s=2, c=8)
            if mi == 0:
                d["qxT"] = RB.alloc(2 * 4 * 128 * 2, BF16, "p (s c t) -> p s c t", s=2, c=4)
                d["vz"] = RB.alloc(2 * 384 * 2, BF16, "p (s c) -> p s c", s=2)
            d["v"] = RB.alloc(2 * 384 * 2, BF16, "p (s c) -> p s c", s=2)
            d["STm"] = RB.alloc(4 * 128 * 2, BF16, "p (h t) -> p h t", h=4)
            d["S32"] = RB.alloc(4 * 96 * 4, F32, "p (h v) -> p h v", h=4)
            d["Sbf"] = RB.alloc(2 * 4 * 96 * 2, BF16, "p (s h v) -> p s h v", s=2, h=4)
            if mi == 1:
                d["tmpS"] = RB.alloc(4 * 96 * 4, F32, "p (h v) -> p h v", h=4)
            MX.append(d)

        assert RB.o <= RLIM - 4096, (RB.o, RLIM - 4096)
        sgj = RB.alloc(2 * 2 * 384 * 4, F32, "p (s m c) -> p s m c", s=2, m=2)
        for mi in range(2):
            MX[mi]["sg"] = sgj[:, :, mi, :]
        sq2 = RB.alloc(2 * 384 * 4, F32, "p (m c) -> p m c", m=2)
        ss8 = RB.alloc(8 * 4, F32); ln8 = RB.alloc(8 * 4, F32); rs8 = RB.alloc(8 * 4, F32)
        B3bf = ps[3][:, :].bitcast(BF16)
        B5bf = ps[5][:, :].bitcast(BF16)
        wiv = w_in[l].rearrange("(k p) c -> p k c", p=128)
        for k in range(KC):
            for hf in range(2):
                P.dma('pool', Win[:, k, hf * 1288:(hf + 1) * 1288], wiv[:, k, hf * 1288:(hf + 1) * 1288], sem="win", grp=("win", l))
        wov = w_out[l].rearrange("(k p) c -> p k c", p=128)
        for k in range(KC):
            P.dma('pool', Wout[:, k, :], wov[:, k, :], sem="wout", grp=("wout", l))
        P.memset('dve', PWblk, 0.0)
        for g in range(4):
            gs = g % 2; p_ = g // 2
            P.dma('pool', PWblk[64 * gs:64 * gs + 64, p_, 64 * gs:64 * gs + 64], pool_w[l, g], sem="small", grp=("small", l))
        P.dma('pool', wa2b[0:16, :], gla_wa2[l], sem="small", grp=("small", l))
        P.dma('pool', bab[0:1, :], gla_ba[l:l + 1, :], sem="small", grp=("small", l))
        P.dma('sp', gng, gng_d[l], sem="gng")
        if pre is not None:
            pre()
        for mi in range(2):
            P.memset('dve', MX[mi]["S32"][0:48], 0.0)
            P.memset('dve', MX[mi]["Sbf"][0:48, 0], 0.0)
        mod_derive(l, 0)
        A1 = a12[:, l, 0, :]; S1 = modT[:, l, 0:8]

        def proj(hT, c0, c1, bank):
            n = c1 - c0
            for k in range(KC):
                P.mm(ps[bank][:, 0:n], hT[:, k, :], Win[:, k, c0:c1], start=(k == 0), stop=(k == KC - 1))
            return ps[bank][:, 0:n]

        def modtick(n=1):
            if msch is None:
                return
            for _ in range(n):
                if msch.done():
                    return
                n0 = msch.nm
                msch.tick()
                if n0 < 24 <= msch.nm:
                    mod_finish(l, 16, 24, src=ps[0][:, 384:392])
                if msch.done():
                    mod_finish(l, 24, 48, src=ps[0][:, 392:416])

        def stage0(t):
            norm_a(t * 128, 128, (sq, rstd, lnv, tmpf), psn=ps[2][:, 384:512])
            yield
            modtick(2 if (msch is not None and msch.nm < 24) else 1)
            norm_b(t * 128, 128, hT2[:, t % 2], (sq, rstd, lnv, tmpf), A=A1, S=S1, alt='dve2')
            yield

        def stage1(t):
            par = t % 2
            dr_ = MX[0]; dg = MX[1]
            hT = hT2[:, t % 2]
            modtick(2 if (msch is not None and msch.nm < 24) else 1)
            for k in range(KC):
                P.mm(ps[0][0:16, 0:128], Win[:, k, 2176:2192], hT[:, k, :], start=(k == 0), stop=(k == KC - 1))
            P.copy('act', gaT[0:16, :], ps[0][0:16, 0:128])
            yield
            pq = proj(hT, 256, 640, 1)
            cosb = costab[:, t, :].unsqueeze(1).broadcast_to([128, 16, 24])
            sinb = sintab[:, t, :].unsqueeze(1).broadcast_to([128, 16, 24])
            pq3 = pq.rearrange("p (c i) -> p c i", i=24)
            P.tt('dve', rotA.rearrange("p (c i) -> p c i", i=24), pq3, cosb, ALU.mult)
            P.tt('dve', rotB.rearrange("p (c i) -> p c i", i=24), pq3, sinb, ALU.mult)
            A4 = rotA.rearrange("p (c f i) -> p c f i", f=2, i=24); B4 = rotB.rearrange("p (c f i) -> p c f i", f=2, i=24)
            R4 = dr_["qk"][:, par].rearrange("p c (f i) -> p c f i", f=2)
            P.tt('dve', R4[:, :, 0, :], A4[:, :, 0, :], B4[:, :, 1, :], ALU.subtract)
            P.tt('dve', R4[:, :, 1, :], A4[:, :, 1, :], B4[:, :, 0, :], ALU.add)
            yield
            P.mm(ps[2][:, 0:192], gaT[0:16, :], wa2b[0:16, :], start=True, stop=False)
            P.mm(ps[2][:, 0:192], onesrow[0:1, :], bab[0:1, :], start=False, stop=True)
            P.act(e1, ps[2][:, 0:192], AF.Exp, scale=-1.0)
            P.act(spb, e1, AF.Ln, bias=kcol[:, 1:2], scale=1.0)
            yield
            pu = proj(hT, 0, 256, 0)
            P.copy('act', ubf[:, t % 3, :], pu)
            yield
            P.mm(ps[2][:, 192:384], tri, spb)
            for h in range(4):
                P.mm(ps[0][0:48, 416 + h:417 + h], spb[:, 48 * h:48 * h + 48], one32)
            P.act(eb, ps[2][:, 192:384], AF.Exp, bias=kcol[:, 2:3], scale=1.0)
            P.act(enb, ps[2][:, 192:384], AF.Exp, scale=-1.0)
            P.act(ebl[0:48, par, :], ps[0][0:48, 416:420], AF.Exp)
            yield
            pv = proj(hT, 640, 1024, 1)
            P.copy('act', dr_["v"][:, par], pv)
            P.tt('dve', dr_["vz"][:, par].rearrange("p (h v) -> p h v", h=4), pv.rearrange("p (h v) -> p h v", h=4),
                 zeta.unsqueeze(2).broadcast_to([128, 4, 96]), ALU.mult)
            yield
            pv = proj(hT, 1792, 2176, 0)
            P.copy('act', dg["v"][:, par], pv)
            yield
            pg = proj(hT, 1408, 1792, 1)
            P.tt('dve', dg["qk"][:, par, 0:4, :], pg[:, 0:192].rearrange("p (c d) -> p c d", c=4), eb.rearrange("p (c d) -> p c d", c=4), ALU.mult)
            P.tt('dve', dg["qk"][:, par, 4:8, :], pg[:, 192:384].rearrange("p (c d) -> p c d", c=4), enb.rearrange("p (c d) -> p c d", c=4), ALU.mult)
            yield
            B3v = B3bf[0:48, :].rearrange("p (c t) -> p c t", c=8)
            for c in range(8):
                P.tr(B3bf[0:48, c * 128:(c + 1) * 128], dr_["qk"][:, par, c, :], ident)
            P.copy('act', dr_["qkT"][0:48, par], B3v)
            P.tt('dve', dr_["qxT"][0:48, par], B3v[:, 0:4, :], xit[0:48], ALU.mult)
            yield
            for c in range(8):
                P.tr(B3bf[0:48, c * 128:(c + 1) * 128], dg["qk"][:, par, c, :], ident)
            P.copy('act', dg["qkT"][0:48, par], B3v)
            yield
            pgt = proj(hT, 1024, 1408, 0)
            P.act(dr_["sg"][:, par], pgt, AF.Silu)
            yield
            pgt = proj(hT, 2192, 2576, 1)
            P.act(dg["sg"][:, par], pgt, AF.Silu)
            P.tt('dve', dg["sg"][:, par], dg["sg"][:, par], gng, ALU.mult)
            yield

        def st_mask(t, mi, kind):
            d = MX[mi]; par = t % 2
            qkT = d["qkT"][:, par]
            STp = ps[5][:, :].rearrange("p (h t) -> p h t", h=4)
            for h in range(4):
                P.mm(STp[:, h, :], qkT[0:48, 4 + h, :], qkT[0:48, h, :])
            if kind == "ret":
                P.tt('dve', d["STm"], STp, DTm, ALU.mult)
            else:
                P.tt('dve', d["STm"], STp, causal.unsqueeze(1).broadcast_to([128, 4, 128]), ALU.mult)

        def core(t, mi, kind):
            d = MX[mi]; par = t % 2
            qkT = d["qkT"][:, par]
            qxT = d["qxT"][:, par] if kind == "ret" else qkT
            vv = d["v"][:, par]
            Op = ps[6][:, 0:384] if kind == "ret" else ps[7][:, 0:384]
            dSbank = ps[4]
            kz = d["qk"][:, par, 4:8, :]
            vz = d["vz"][:, par] if kind == "ret" else vv
            for h in range(4):
                P.mm(dSbank[0:48, 96 * h:96 * h + 96], kz[:, h, :], vz[:, 96 * h:96 * h + 96])
            dSv = dSbank[0:48, 0:384].rearrange("p (h v) -> p h v", h=4)
            S32 = d["S32"]
            for h in range(4):
                P.mm(Op[:, 96 * h:96 * h + 96], d["STm"][:, h, :], vv[:, 96 * h:96 * h + 96], start=True, stop=False)
                P.mm(Op[:, 96 * h:96 * h + 96], qxT[0:48, h, :], d["Sbf"][0:48, par, h, :], start=False, stop=True)
            if kind == "ret":
                for h in range(4):
                    P.stt(S32[0:48, h, :], S32[0:48, h, :], GAMC[h], dSv[:, h, :], ALU.mult, ALU.add)
            else:
                P.tt('dve', d["tmpS"][0:48], dSv, S32[0:48], ALU.add)
                P.tt('dve', S32[0:48], d["tmpS"][0:48], ebl[0:48, par, :].unsqueeze(2).broadcast_to([48, 4, 96]), ALU.mult)
            P.copy('act', d["Sbf"][0:48, 1 - par], S32[0:48])

        def rms_both(t):
            par = t % 2
            Ob = ps67[:, :].rearrange("p (m c) -> p m c", m=2)[:, :, 0:384]
            P.act(sq2, Ob, AF.Square)
            P.op('dve', lambda e, o=ss8, i_=sq2.rearrange("p m (h v) -> p (m h) v", h=4): e.tensor_reduce(o, i_, mybir.AxisListType.X, ALU.add),
                 r=[sq2], w=[ss8])
            P.act(ln8, ss8, AF.Ln, bias=kcol[:, 0:1], scale=1.0 / 96.0)
            P.act(rs8, ln8, AF.Exp, scale=-0.5)
            P.tt('dve', sq2.rearrange("p m (h v) -> p m h v", h=4), Ob.rearrange("p m (h v) -> p m h v", h=4),
                 rs8.rearrange("p (m h) -> p m h", m=2).unsqueeze(3).broadcast_to([128, 2, 4, 96]), ALU.mult)
            P.tt('dve', ybuf, sq2, sgj[:, par], ALU.mult)

        def y_tr(t, mi):
            ti = t % 4
            for c in range(3):
                P.tr(B5bf[:, c * 128:(c + 1) * 128], ybuf[:, mi, c * 128:(c + 1) * 128], ident)
            P.copy('dve', yT[:, 2 + 3 * mi:5 + 3 * mi, ti * 128:(ti + 1) * 128], B5bf[:, 0:384].rearrange("p (c t) -> p c t", c=3))

        def stage2(t):
            ti = t % 4
            st_mask(t, 0, "ret")
            st_mask(t, 1, "gla")
            yield
            core(t, 0, "ret")
            core(t, 1, "gla")
            rms_both(t)
            yield
            cur = t % 3; prv = (t - 1) % 3
            for p_ in range(2):
                for gs in range(2):
                    g = 2 * p_ + gs
                    pp = ps[4][:, 384:512]
                    if t == 0:
                        P.mm(pp, ubf[:, cur, 128 * p_:128 * p_ + 128], Pfirst[:, g, :])
                    else:
                        P.mm(pp, ubf[:, cur, 128 * p_:128 * p_ + 128], Pcur[:, g, :], start=True, stop=False)
                        P.mm(pp, ubf[:, prv, 128 * p_:128 * p_ + 128], Pprev[:, g, :], start=False, stop=True)
                    P.copy('act', pooledT[64 * gs:64 * gs + 64, p_, :], pp[64 * gs:64 * gs + 64, :])
                    yield
                pm = ps[7][:, 384:512]
                P.mm(pm, PWblk[:, p_, :], pooledT[:, p_, :])
                P.act(yT[:, p_, ti * 128:(ti + 1) * 128], pm, AF.Identity, scale=col(("pscale", l), p_, 1))
            y_tr(t, 0)
            yield
            y_tr(t, 1)
            yield
            if ti == 3:
                bk = t // 4
                for m in range(KC):
                    wb = ps[(7, 6, 5)[m % 3]]
                    for k in range(KC):
                        P.mm(wb[:, 0:512], Wout[:, k, m * 128:(m + 1) * 128], yT[:, k, :], start=(k == 0), stop=(k == KC - 1))
                    xs = X[:, m, bk * 512:(bk + 1) * 512]
                    P.stt(xs, wb[:, 0:512], modT[:, l, 16 + m:17 + m], xs, ALU.mult, ALU.add)
                    if m % 2 == 1:
                        yield

        def rr(gens):
            while gens:
                for g_ in list(gens):
                    try:
                        next(g_)
                    except StopIteration:
                        gens.remove(g_)
        rr([stage0(0)])
        rr([stage1(0), stage0(1)])
        for t in range(16):
            gens = [stage2(t)]
            if t + 2 < 16:
                gens.append(stage0(t + 2))
            if t + 1 < 16:
                gens.append(stage1(t + 1))
            rr(gens)

    def ffn(l, nextmod=None):
        RB = Bump(RB0, RLIM)
        h2 = RB.alloc(KC * T * 2, BF16, "p (k t) -> p k t", k=KC)
        actb = RB.alloc(6 * T * 2, BF16, "p (j t) -> p j t", j=6)
        wd = RB.alloc(6 * D * 2, BF16, "p (j c) -> p j c", j=6)
        wu = RB.alloc(2 * KC * 256 * 2, BF16, "p (s k c) -> p s k c", s=2, k=KC)
        NT = 128
        ntmps = []
        for i_ in range(2):
            sq_ = RB.alloc(KC * NT * 2, BF16, "p (k t) -> p k t", k=KC)
            rstd_ = RB.alloc(NT * 4, F32)
            lnv_ = RB.alloc(NT * 4, F32)
            tmpf_ = RB.alloc(KC * NT * 4, F32, "p (k t) -> p k t", k=KC)
            ntmps.append((sq_, rstd_, lnv_, tmpf_))
        mod_derive(l, 1)
        U = RB.alloc(3 * 2 * 514 * 4, F32, "p (s q c) -> p s q c", s=3, q=2)
        Y = RB.alloc(3 * 2 * 512 * 4, F32, "p (s q c) -> p s q c", s=3, q=2)
        wuv = w_up[l].rearrange("(k p) c -> p k c", p=128)
        NSB = T // NT
        def normgen():
            norm_a(0, NT, ntmps[0], psn=ps[6][:, 0:NT])
            for i in range(NSB):
                if i + 1 < NSB:
                    norm_a((i + 1) * NT, NT, ntmps[(i + 1) % 2], psn=ps[6][:, ((i + 1) % 2) * NT:((i + 1) % 2 + 1) * NT])
                norm_b(i * NT, NT, h2[:, :, i * NT:(i + 1) * NT], ntmps[i % 2], A=a12[:, l, 1, :], S=modT[:, l, 24:32])
                yield i
        ngen = normgen()
        ndone = [-1]
        def ensure_norm(tb):
            need = (tb + 1) * (512 // NT) - 1
            while ndone[0] < need:
                ndone[0] = next(ngen)
        def load_wu(j):
            slot = j % 2
            P.dma('pool', wu[:, slot, :, 0:128], wuv[:, :, j * 128:(j + 1) * 128], sem="wu%d" % slot, grp=("wu", l, j))
            P.dma('pool', wu[:, slot, :, 128:256], wuv[:, :, DFF + j * 128:DFF + (j + 1) * 128], sem="wu%d" % slot, grp=("wu", l, j))
        def load_wd(J):
            for jl, j in enumerate(J):
                P.dma('pool', wd[:, jl, :], w_down[l][j * 128:(j + 1) * 128, :], sem="wd%d" % jl)
        gstep = [0]
        ring4 = [arena[:, RLIM - 2048 * (i + 1):RLIM - 2048 * i].bitcast(BF16).rearrange("p (k c) -> p k c", k=8) for i in range(4)]
        msch = ModSched(nextmod, mode='bf16', ring=ring4, semp="modr4_%d") if nextmod is not None else None
        def stageA(st):
            jl, j, tb, g = st
            par = g % 2
            ensure_norm(tb)
            if msch is not None and not msch.done():
                msch.tick()
            slot = j % 2
            for qi in range(2):
                pst = ps[2 * par + qi]
                for k in range(KC):
                    P.mm(pst[:, :], wu[:, slot, k, qi * 128:(qi + 1) * 128], h2[:, k, tb * 512:(tb + 1) * 512],
                         start=(k == 0), stop=(k == KC - 1))
        def cw(j, qi):
            ch = qi * NJ + j
            return (col(("cw", l), 0 * 44 + ch, 1), col(("cw", l), 1 * 44 + ch, 1), col(("cw", l), 2 * 44 + ch, 1), col(("cb", l), ch, 1))
        def stageB(st):
            jl, j, tb, g = st
            par = g % 2; p3 = g % 3
            for qi in range(2):
                pst = ps[2 * par + qi]
                w0, w1, w2, bb = cw(j, qi)
                Ub = U[:, p3, qi, :]; Yb = Y[:, p3, qi, :]
                if tb == 0:
                    P.memset('pool', Ub[:, 0:2], 0.0)
                else:
                    P.copy('pool', Ub[:, 0:2], U[:, (g - 1) % 3, qi, 512:514])
                P.copy('act', Ub[:, 2:514], pst[:, :])
                P.act(Yb, pst[:, :], AF.Identity, bias=bb, scale=w2)
        def stageC(st):
            jl, j, tb, g = st
            p3 = g % 3
            for qi in range(2):
                w0, w1, w2, bb = cw(j, qi)
                Ub = U[:, p3, qi, :]; Yb = Y[:, p3, qi, :]
                P.stt(Yb, Ub[:, 1:513], w1, Yb, ALU.mult, ALU.add)
                P.stt(Yb, Ub[:, 0:512], w0, Yb, ALU.mult, ALU.add)
        def stageD1(st):
            jl, j, tb, g = st
            p3 = g % 3
            P.act(Y[:, p3, 1, :], Y[:, p3, 1, :], AF.Silu)
        def stageD2(st):
            jl, j, tb, g = st
            p3 = g % 3
            P.tt('dve', actb[:, jl, tb * 512:(tb + 1) * 512], Y[:, p3, 0, :], Y[:, p3, 1, :], ALU.mult)
        ycnt = 0
        load_wd(QUARTERS[0])
        load_wu(QUARTERS[0][0])
        def make_steps(q):
            J = QUARTERS[q]
            steps = []
            for jl, j in enumerate(J):
                for tb in range(4):
                    steps.append((jl, j, tb, gstep[0])); gstep[0] += 1
            return steps
        def prefetch_for(q, steps, idx):
            J = QUARTERS[q]
            jl, j, tb, g = steps[idx]
            if tb == 0:
                if jl + 1 < len(J):
                    load_wu(J[jl + 1])
                elif q + 1 < len(QUARTERS):
                    load_wu(QUARTERS[q + 1][0])
        def prologue(q, steps):
            prefetch_for(q, steps, 0)
            stageA(steps[0]); stageB(steps[0])
            prefetch_for(q, steps, 1)
            stageA(steps[1]); stageB(steps[1])
        cur_steps = make_steps(0)
        prologue(0, cur_steps)
        for q, J in enumerate(QUARTERS):
            steps = cur_steps
            n_ = len(steps)
            stageC(steps[0])
            for i in range(n_):
                if i + 2 < n_:
                    prefetch_for(q, steps, i + 2)
                    stageA(steps[i + 2]); stageB(steps[i + 2])
                if i + 1 < n_:
                    stageC(steps[i + 1])
                stageD1(steps[i]); stageD2(steps[i])
            if q + 1 < len(QUARTERS):
                cur_steps = make_steps(q + 1)
                prologue(q + 1, cur_steps)
            for tb in range(4):
                for m in range(KC):
                    pst = ps[4 + ycnt % 2]; ycnt += 1
                    for jl in range(len(J)):
                        P.mm(pst[:, :], wd[:, jl, m * 128:(m + 1) * 128], actb[:, jl, tb * 512:(tb + 1) * 512],
                             start=(jl == 0), stop=(jl == len(J) - 1))
                    xs = X[:, m, tb * 512:(tb + 1) * 512]
                    P.stt(xs, pst[:, :], modT[:, l, 40 + m:41 + m], xs, ALU.mult, ALU.add)
            if q + 1 < len(QUARTERS):
                load_wd(QUARTERS[q + 1])
        if msch is not None:
            msch.run_until(48)
            mod_finish(nextmod, 0, 48)

    def final():
        RB = Bump(RB0, RLIM)
        NT = 256
        tmps = []
        for i in range(2):
            sq = RB.alloc(KC * NT * 2, BF16, "p (k t) -> p k t", k=KC)
            rstd = RB.alloc(NT * 4, F32)
            lnv = RB.alloc(NT * 4, F32)
            tmpf = RB.alloc(KC * NT * 4, F32, "p (k t) -> p k t", k=KC)
            tmps.append((sq, rstd, lnv, tmpf))
        ob = RB.alloc(2 * KC * NT * 4, F32, "p (s k t) -> p s k t", s=2, k=KC)
        nb = T // NT
        norm_a(0, NT, tmps[0], psn=ps[6][:, 0:NT])
        for i in range(nb):
            t0 = i * NT
            if i + 1 < nb:
                norm_a(t0 + NT, NT, tmps[(i + 1) % 2], psn=ps[6 + (i + 1) % 2][:, 0:NT])
            norm_b(t0, NT, ob[:, i % 2], tmps[i % 2], G=col("gf"))
            P.dma('sp' if i % 2 == 0 else 'act', ov[:, :, t0:t0 + NT], ob[:, i % 2], sem="out%d" % (i % 2))

    layers = list(layers)
    for li, l in enumerate(layers):
        nxt = layers[li + 1] if li + 1 < len(layers) else None
        m0 = None
        pre = None
        if li == 0 or not do_ffn:
            if do_mixer:
                def pre():
                    ma = ModSched(l, mode='f32', lo=0, hi=16)
                    ma.run_until(16)
                    mod_finish(l, 0, 16)
                m0 = ModSched(l, mode='bf16', lo=16, hi=48)
                m0.where = lambda jc: ps[0][:, 384 + jc - 16:385 + jc - 16]
            else:
                m0 = ModSched(l)
                m0.run_until(48)
                mod_finish(l, 0, 48)
                m0 = None
        if do_mixer:
            mixer(l, msch=m0, pre=pre)
        if do_ffn:
            ffn(l, nextmod=nxt)
    final()
    st = P.emit()
    return st


def kernel(**inputs):
    inp = {k: np.asarray(v) for k, v in inputs.items()}
    nc = bass.Bass("TRN2", target_bir_lowering=False)
    build(nc)
    in_maps = [host_inputs(inp, b) for b in range(8)]
    res = run_bass_kernel_spmd(nc, in_maps, core_ids=list(range(8)))
    out = np.stack([np.asarray(res.results[b]["outT"]).T for b in range(8)], axis=0)
    return np.ascontiguousarray(out, dtype=np.float32)
```

```python
import math
import numpy as np
import concourse.bass as bass
import concourse.mybir as mybir
from concourse.bass_utils import run_bass_kernel_spmd

F32 = mybir.dt.float32; BF16 = mybir.dt.bfloat16; I32 = mybir.dt.int32; U8 = mybir.dt.uint8
ALU = mybir.AluOpType; AF = mybir.ActivationFunctionType


def _rng(ap):
    sp = str(ap.space)
    if 'SB' not in sp and 'PSUM' not in sp:
        return None
    if 'PSUM' in sp:
        pat = ap.ap
        ds = mybir.dt.size(ap.dtype)
        pstep = pat[0][0]
        col0 = ap.offset % pstep if pstep > 0 else ap.offset
        ext = 1
        for st, cn in pat[1:]:
            ext += (cn - 1) * abs(st)
        b0 = (col0 * ds) // 2048; b1 = ((col0 + ext) * ds - 1) // 2048
        return [("%s#%d" % (ap.tensor.name, b), (0, 2048, ((0, 2048),))) for b in range(b0, b1 + 1)]
    pat = ap.ap
    ds = mybir.dt.size(ap.dtype)
    pstep = pat[0][0]
    col0 = ap.offset % pstep if pstep > 0 else ap.offset
    dims = sorted([(abs(st), cn) for st, cn in pat[1:] if cn > 1 and st != 0])
    run = 1
    rest = []
    for st, cn in dims:
        if st <= run:
            run = max(run, (cn - 1) * st + run)
        else:
            rest.append((st, cn))
    starts = [0]
    nint = 1
    for st, cn in rest:
        nint *= cn
    if nint > 64:
        ext = run
        for st, cn in rest:
            ext += (cn - 1) * st
        ivs = [(col0 * ds, (col0 + ext) * ds)]
    else:
        for st, cn in rest:
            starts = [a + i * st for a in starts for i in range(cn)]
        starts.sort()
        ivs = []
        for a in starts:
            s0 = (col0 + a) * ds; e0 = (col0 + a + run) * ds
            if ivs and s0 <= ivs[-1][1]:
                ivs[-1] = (ivs[-1][0], max(ivs[-1][1], e0))
            else:
                ivs.append((s0, e0))
    return (ap.tensor.name, (ivs[0][0], ivs[-1][1], tuple(ivs)))


def _ovl(a, b):
    if not (a[0] < b[1] and b[0] < a[1]):
        return False
    ia = a[2]; ib = b[2]
    if len(ia) == 1 and len(ib) == 1:
        return True
    i = j = 0
    while i < len(ia) and j < len(ib):
        if ia[i][0] < ib[j][1] and ib[j][0] < ia[i][1]:
            return True
        if ia[i][1] <= ib[j][1]:
            i += 1
        else:
            j += 1
    return False


def _covers(a, b):
    if not (a[0] <= b[0] and b[1] <= a[1]):
        return False
    ia = a[2]
    for (s, e) in b[2]:
        ok = False
        for (s2, e2) in ia:
            if s2 <= s and e <= e2:
                ok = True; break
        if not ok:
            return False
    return True


class Prog:
    ENG = ('pe', 'act', 'dve', 'pool', 'sp')

    def __init__(self, nc):
        self.nc = nc
        self.ops = []
        self.grp_ctr = 0

    def op(self, eng, fn, r=(), w=()):
        def flat(lst):
            out = []
            for x in lst:
                if not x:
                    continue
                if isinstance(x, list):
                    out.extend(x)
                else:
                    out.append(x)
            return out
        rr = flat(_rng(a) for a in r if a is not None and not isinstance(a, (int, float)))
        ww = flat(_rng(a) for a in w)
        ww = ww + [x for x in rr if x[0].startswith('ps') and x not in ww]
        self.ops.append(dict(k='c', eng=eng, fn=fn, r=rr, w=ww))

    def dma(self, eng, out, in_, sem, grp=None, **kw):
        if grp is None:
            self.grp_ctr += 1
            grp = ('_g', self.grp_ctr)
        rr = [x for x in (_rng(in_),) if x and not isinstance(x, list)]
        ww = [x for x in (_rng(out),) if x and not isinstance(x, list)]
        self.ops.append(dict(k='d', eng=eng, out=out, in_=in_, r=rr, w=ww, sem=sem, grp=grp, kw=kw))

    def mm(self, out, lhsT, rhs, start=True, stop=True):
        self.op('pe', lambda e: e.matmul(out, lhsT, rhs, start=start, stop=stop), r=[lhsT, rhs], w=[out])

    def tr(self, out, in_, ident):
        self.op('pe', lambda e: e.transpose(out, in_, ident), r=[in_, ident], w=[out])

    def act(self, out, in_, func, bias=None, scale=None, eng='act'):
        kw = {}
        if bias is not None: kw['bias'] = bias
        if scale is not None: kw['scale'] = scale
        self.op(eng, lambda e: e.activation(out, in_, func, **kw), r=[in_, bias, scale], w=[out])

    def tt(self, eng, out, in0, in1, op):
        self.op(eng, lambda e: e.tensor_tensor(out, in0, in1, op), r=[in0, in1], w=[out])

    def ts(self, eng, out, in0, s1, s2, op0, op1=None):
        if op1 is None:
            self.op(eng, lambda e: e.tensor_scalar(out, in0, s1, None, op0), r=[in0, s1], w=[out])
        else:
            self.op(eng, lambda e: e.tensor_scalar(out, in0, s1, s2, op0, op1), r=[in0, s1, s2], w=[out])

    def stt(self, out, in0, scalar, in1, op0, op1, eng='dve'):
        self.op(eng, lambda e: e.scalar_tensor_tensor(out, in0, scalar, in1, op0, op1), r=[in0, scalar, in1], w=[out])

    def copy(self, eng, out, in_):
        if eng == 'act':
            self.op(eng, lambda e: e.copy(out, in_), r=[in_], w=[out])
        else:
            self.op(eng, lambda e: e.tensor_copy(out, in_), r=[in_], w=[out])

    def memset(self, eng, out, val):
        self.op(eng, lambda e: e.memset(out, val), r=[], w=[out])

    def emit(self):
        nc = self.nc
        engs = {'pe': nc.tensor, 'act': nc.scalar, 'dve': nc.vector, 'pool': nc.gpsimd, 'sp': nc.sync}
        ops = self.ops
        n = len(ops)
        W = {}
        R = {}
        deps = [None] * n
        needed = [False] * n
        for i, o in enumerate(ops):
            raw = set(); oth = set()
            for (sp, f) in o['r']:
                for rec in W.get(sp, ()):
                    if _ovl(rec[0], f):
                        raw.add(rec[1])
            for (sp, f) in o['w']:
                for rec in W.get(sp, ()):
                    if _ovl(rec[0], f):
                        oth.add(rec[1])
                for rec in R.get(sp, ()):
                    if _ovl(rec[0], f):
                        oth.add(rec[1])
            d = set()
            for j in raw | oth:
                pj = ops[j]
                if j not in raw and o['k'] == 'c' and pj['k'] == 'c' and pj['eng'] == o['eng'] == 'pe':
                    continue
                if o['k'] == 'd' and pj['k'] == 'd' and o['sem'] == pj['sem'] and o['grp'] == pj['grp']:
                    continue
                d.add(j)
            d.discard(i)
            deps[i] = d
            for j in d: needed[j] = True
            for (sp, f) in o['w']:
                W[sp] = [rec for rec in W.get(sp, ()) if not _covers(f, rec[0])]
                R[sp] = [rec for rec in R.get(sp, ()) if not _covers(f, rec[0])]
                W[sp].append((f, i))
            for (sp, f) in o['r']:
                lst = R.setdefault(sp, [])
                eng = o['eng']
                if o['k'] == 'c':
                    lst[:] = [rec for rec in lst if not (rec[0] == f and ops[rec[1]]['k'] == 'c' and ops[rec[1]]['eng'] == eng)]
                lst.append((f, i))
        esem = {e: nc.alloc_semaphore("s_" + e) for e in engs}
        dsem = {}
        cnt = {e: 0 for e in engs}
        dcnt = {}
        tok = [None] * n
        grp_final = {}
        for i, o in enumerate(ops):
            if o['k'] == 'c':
                if needed[i]:
                    cnt[o['eng']] += 1
                    tok[i] = (esem[o['eng']], cnt[o['eng']])
            else:
                s = o['sem']
                if s not in dsem:
                    dsem[s] = nc.alloc_semaphore("d_%d" % len(dsem)); dcnt[s] = 0
                dcnt[s] += 16
                grp_final[(s, o['grp'])] = dcnt[s]
        for i, o in enumerate(ops):
            if o['k'] == 'd':
                tok[i] = (dsem[o['sem']], grp_final[(o['sem'], o['grp'])])
        waited = {e: {} for e in engs}
        nwaits = 0
        for i, o in enumerate(ops):
            e = o['eng']; E = engs[e]
            need = {}
            for j in deps[i]:
                s, v = tok[j]
                if need.get(s, 0) < v: need[s] = v
            for s, v in need.items():
                if waited[e].get(s, 0) >= v: continue
                E.wait_ge(s, v); waited[e][s] = v; nwaits += 1
            if o['k'] == 'c':
                ins = o['fn'](E)
                if needed[i]:
                    ins.then_inc(tok[i][0], 1)
            else:
                ins = E.dma_start(out=o['out'], in_=o['in_'], **o['kw'])
                ins.then_inc(dsem[o['sem']], 16)
        E = engs['sp']
        for s, v in dcnt.items():
            E.wait_ge(dsem[s], v)
        self.stats = dict(n=n, nwaits=nwaits, cnt=dict(cnt), nsem=len(esem) + len(dsem))
        return self.stats


D = 1024; T = 2048; KC = 8; DIN = 2576; DFF = 2816; NJ = 22
EPS = 1e-6
QUARTERS = [list(range(0, 6)), list(range(6, 12)), list(range(12, 17)), list(range(17, 22))]

COLS = {}
_o = 0
for _l in range(2):
    for _nm, _n in (("g1", 8), ("adab", 48), ("pscale", 2), ("g2", 8), ("cw", 132), ("cb", 44)):
        COLS[(_nm, _l)] = (_o, _n); _o += _n
COLS["gf"] = (_o, 8); _o += 8
COLS["c"] = (_o, 8); _o += 8
NCOL = _o


def host_cols(inp, b):
    cols = np.zeros((128, NCOL), np.float32)
    def put(key, arr):
        o, n = COLS[key]
        cols[:, o:o + n] = arr
    for l in range(2):
        put(("g1", l), inp["norm1_g"][l].reshape(8, 128).T)
        put(("adab", l), inp["ada_b"][l].reshape(48, 128).T)
        put(("pscale", l), inp["pool_scale"][l].reshape(2, 128).T)
        put(("g2", l), inp["norm2_g"][l].reshape(8, 128).T)
        cw = inp["conv_w"][l].reshape(3, 44, 128).transpose(2, 0, 1).reshape(128, 132)
        put(("cw", l), cw)
        put(("cb", l), inp["conv_b"][l].reshape(44, 128).T)
    put("gf", inp["final_g"].reshape(8, 128).T)
    put("c", inp["c"][b].reshape(8, 128).T)
    return cols


SC = 48.0 ** -0.5
GAM = [1.0 - 2.0 ** (-5.0 - h) for h in range(4)]
GAMC = [g ** 128 for g in GAM]
CT = {}
_o = 0
for _nm, _n in (("tri", 128), ("DT", 512), ("xi", 512), ("zeta", 4), ("invf", 24), ("one", 1)):
    CT[_nm] = (_o, _n); _o += _n
NCT = _o
CBT = {}
_o = 0
for _nm, _n in (("ident", 128), ("causal", 128), ("Pcur", 512), ("Pprev", 512), ("Pfirst", 512), ("onesrow", 128)):
    CBT[_nm] = (_o, _n); _o += _n
NCB = _o
_HC = {}


def host_consts():
    if _HC:
        return _HC["ct"], _HC["cb"]
    ct = np.zeros((128, NCT), np.float64)
    cb = np.zeros((128, NCB), np.float64)
    j = np.arange(128)[:, None]; i = np.arange(128)[None, :]
    tri = (j <= i).astype(np.float64)
    ct[:, CT["tri"][0]:CT["tri"][0] + 128] = -tri / 16.0
    DT = np.zeros((128, 4, 128)); xi = np.zeros((128, 4, 128)); zeta = np.zeros((128, 4))
    for h in range(4):
        DT[:, h, :] = np.where(i >= j, SC * GAM[h] ** np.maximum(i - j, 0), 0.0)
        xi[:, h, :] = SC * GAM[h] ** (i + 1.0)
        zeta[:, h] = GAM[h] ** (127.0 - np.arange(128))
    ct[:, CT["DT"][0]:CT["DT"][0] + 512] = DT.reshape(128, 512)
    ct[:, CT["xi"][0]:CT["xi"][0] + 512] = xi.reshape(128, 512)
    ct[:, CT["zeta"][0]:CT["zeta"][0] + 4] = zeta
    ct[:, CT["invf"][0]:CT["invf"][0] + 24] = (10000.0 ** (-np.arange(0, 48, 2) / 48.0))[None, :]
    ct[:, CT["one"][0]] = -1.0 / 16.0
    cb[:, CBT["ident"][0]:CBT["ident"][0] + 128] = np.eye(128)
    cb[:, CBT["causal"][0]:CBT["causal"][0] + 128] = tri
    Pc = np.zeros((128, 4, 128)); Pp = np.zeros((128, 4, 128)); Pf = np.zeros((128, 4, 128))
    for g, w in enumerate((2, 4, 8, 16)):
        Pc[:, g, :] = np.where((j <= i) & (j > i - w), 1.0 / w, 0.0) - (j == i)
        Pp[:, g, :] = np.where(j - 128 > i - w, 1.0 / w, 0.0)
        cnt = np.minimum(i + 1, w)
        Pf[:, g, :] = np.where((j <= i) & (j > i - w), 1.0 / cnt, 0.0) - (j == i)
    cb[:, CBT["Pcur"][0]:CBT["Pcur"][0] + 512] = Pc.reshape(128, 512)
    cb[:, CBT["Pprev"][0]:CBT["Pprev"][0] + 512] = Pp.reshape(128, 512)
    cb[:, CBT["Pfirst"][0]:CBT["Pfirst"][0] + 512] = Pf.reshape(128, 512)
    cb[:, CBT["onesrow"][0]:CBT["onesrow"][0] + 128] = 1.0
    _HC["ct"] = ct.astype(np.float32); _HC["cb"] = cb.astype(np.float32)
    return _HC["ct"], _HC["cb"]


def host_inputs(inp, b):
    ct, cb = host_consts()
    return {"xT": np.ascontiguousarray(inp['x'][b].T), "cols": host_cols(inp, b),
            "pos": np.ascontiguousarray(inp['positions'][b].reshape(16, 128).T),
            "ctab": ct, "cbt": cb,
            "gng": np.ascontiguousarray(np.broadcast_to(inp['gla_norm_g'][:, None, :], (2, 128, 384))),
            "ada_w": inp['ada_w'], "w_in": inp['w_in'], "w_out": inp['w_out'], "pool_w": inp['pool_w'],
            "gla_wa2": inp['gla_wa2'], "gla_ba": inp['gla_ba'],
            "w_up": inp['w_up'], "w_down": inp['w_down']}


def build(nc, layers=(0, 1), do_mixer=True, do_ffn=True, parts=("pool", "ret", "gla")):
    P = Prog(nc)
    dr = {}
    def din(name, shape, dt=F32):
        dr[name] = nc.dram_tensor(name, list(shape), dt, kind="ExternalInput").ap()
        return dr[name]
    xT = din("xT", [D, T])
    cols_d = din("cols", [128, NCOL])
    ada_w = din("ada_w", [2, D, 6 * D])
    pos_d = din("pos", [128, 16], I32)
    ctab_d = din("ctab", [128, NCT])
    cbt_d = din("cbt", [128, NCB])
    gng_d = din("gng", [2, 128, 384])
    w_in = din("w_in", [2, D, DIN])
    w_out = din("w_out", [2, D, D])
    pool_w = din("pool_w", [2, 4, 64, 64])
    gla_wa2 = din("gla_wa2", [2, 16, 192])
    gla_ba = din("gla_ba", [2, 192])
    w_up = din("w_up", [2, D, 2 * DFF])
    w_down = din("w_down", [2, DFF, D])
    outT = nc.dram_tensor("outT", [D, T], F32, kind="ExternalOutput").ap()

    arena = nc.alloc_sbuf_tensor("arena", [128, 207 * 1024], U8)
    ps = [nc.alloc_psum_tensor("ps%d" % i, [128, 512], F32) for i in range(6)]
    ps67 = nc.alloc_psum_tensor("ps67", [128, 1024], F32)
    ps.append(ps67[:, 0:512]); ps.append(ps67[:, 512:1024])

    class Bump:
        def __init__(self, base, limit): self.o = base; self.base = base; self.limit = limit
        def alloc(self, nbytes, dt, pat=None, **kw):
            assert self.o + nbytes <= self.limit, (self.o, nbytes, self.limit)
            a = arena[:, self.o:self.o + nbytes].bitcast(dt)
            self.o += (nbytes + 31) // 32 * 32
            return a.rearrange(pat, **kw) if pat else a

    X = arena[:, 0:65536].bitcast(F32).rearrange("p (k t) -> p k t", k=KC)
    CB = Bump(65536, 65536 + 18 * 1024)
    cols = CB.alloc(NCOL * 4, F32)
    def col(key, a=0, n=None):
        o, nn = COLS[key]
        n = nn - a if n is None else n
        return cols[:, o + a:o + a + n]
    modT = CB.alloc(2 * 48 * 4, F32, "p (l j) -> p l j", l=2)
    a12 = CB.alloc(2 * 2 * 8 * 4, F32, "p (l s k) -> p l s k", l=2, s=2)
    cact = CB.alloc(8 * 2, BF16)
    cact32 = CB.alloc(8 * 4, F32)
    onesb = CB.alloc(128 * 2, BF16)
    RB0 = CB.limit
    RLIM = 207 * 1024
    modring = arena[:, RLIM - 4096:RLIM].bitcast(BF16).rearrange("p (s k c) -> p s k c", s=2, k=8)
    modring32 = arena[:, RLIM - 8192:RLIM].bitcast(F32).rearrange("p (s k c) -> p s k c", s=2, k=8)

    xv = xT.rearrange("(k p) t -> p k t", p=128)
    ov = outT.rearrange("(k p) t -> p k t", p=128)
    P.dma('sp', cols, cols_d, sem="cols", grp="cols")
    posi = CB.alloc(16 * 4, I32)
    P.dma('sp', posi, pos_d, sem="cols", grp="cols")
    for k in range(KC):
        P.dma('sp' if k % 2 == 0 else 'act', X[:, k, :], xv[:, k, :], sem="xin", grp="xin")
    P.memset('dve', onesb, 1.0 / 1024.0)
    P.act(cact, col("c"), AF.Silu)
    P.act(cact32, col("c"), AF.Silu)

    PS_NORM = ps[6]; PS_MOD = ps[7]

    class ModSched:
        def __init__(self, l, mode='bf16', lo=0, hi=48, ring=None, semp=None):
            self.l = l; self.nd = lo; self.nm = lo; self.mode = mode; self.hi = hi
            self.wv = ada_w[l].rearrange("(k p) c -> p k c", p=128)
            if ring is None:
                ring = [modring[:, i] for i in range(2)] if mode == 'bf16' else [modring32[:, i] for i in range(2)]
            self.ring = ring; self.depth = len(ring)
            self.semp = semp if semp is not None else ("modring%d" if mode == 'bf16' else "modr32_%d")
        def where(self, jc):
            return PS_MOD[:, jc:jc + 1]
        def tick(self):
            if self.nd < self.hi and self.nd - self.nm < self.depth:
                jc = self.nd; self.nd += 1
                sl = jc % self.depth
                P.dma('pool' if self.mode == 'bf16' else 'sp', self.ring[sl], self.wv[:, :, jc * 128:(jc + 1) * 128], sem=self.semp % sl)
                if self.nd - self.nm < self.depth and self.nd < self.hi:
                    return
            if self.nm < self.nd:
                jc = self.nm; self.nm += 1
                sl = jc % self.depth
                out = self.where(jc)
                rhs = cact if self.mode == 'bf16' else cact32
                for k in range(KC):
                    P.mm(out, self.ring[sl][:, k, :], rhs[:, k:k + 1], start=(k == 0), stop=(k == KC - 1))
        def run_until(self, n):
            while self.nm < n:
                self.tick()
        def done(self):
            return self.nm >= self.hi
    def mod_finish(l, j0, j1, src=None):
        src = PS_MOD[:, j0:j1] if src is None else src
        P.tt('dve', modT[:, l, j0:j1], src, col(("adab", l), j0, j1 - j0), ALU.add)
    def mod_derive(l, which):
        sc = modT[:, l, 8:16] if which == 0 else modT[:, l, 32:40]
        g = col(("g1", l)) if which == 0 else col(("g2", l))
        P.stt(a12[:, l, which, :], sc, 1.0, g, ALU.add, ALU.mult)

    def norm_a(t0, n, tmp, psn=None):
        sq, rstd, lnv, tmpf = tmp
        PSN = psn if psn is not None else PS_NORM[:, 0:n]
        xs = X[:, :, t0:t0 + n]
        P.act(sq[:, :, 0:n], xs, AF.Square)
        for k in range(KC):
            P.mm(PSN, onesb, sq[:, k, 0:n], start=(k == 0), stop=(k == KC - 1))
        P.act(lnv[:, 0:n], PSN, AF.Ln, bias=epsc[:, 0:1], scale=1.0)
        P.act(rstd[:, 0:n], lnv[:, 0:n], AF.Exp, scale=-0.5)

    def norm_b(t0, n, dst, tmp, A=None, S=None, G=None, alt='dve'):
        sq, rstd, lnv, tmpf = tmp
        xs = X[:, :, t0:t0 + n]
        rb = rstd[:, 0:n].unsqueeze(1).broadcast_to([128, KC, n])
        P.tt('dve', tmpf[:, :, 0:n], xs, rb, ALU.mult)
        if alt == 'dve2':
            P.tt('dve', tmpf[:, :, 0:n], tmpf[:, :, 0:n], A.unsqueeze(2).broadcast_to([128, KC, n]), ALU.mult)
            P.tt('dve', dst, tmpf[:, :, 0:n], S.unsqueeze(2).broadcast_to([128, KC, n]), ALU.add)
            return
        if alt == 'pool2':
            P.tt('pool', tmpf[:, :, 0:n], tmpf[:, :, 0:n], A.unsqueeze(2).broadcast_to([128, KC, n]), ALU.mult)
            P.tt('pool', dst, tmpf[:, :, 0:n], S.unsqueeze(2).broadcast_to([128, KC, n]), ALU.add)
            return
        for k in range(KC):
            if G is not None:
                if k % 2 == 0 or alt == 'act':
                    P.act(dst[:, k, :], tmpf[:, k, 0:n], AF.Identity, scale=G[:, k:k + 1])
                else:
                    P.ts(alt, dst[:, k, :], tmpf[:, k, 0:n], G[:, k:k + 1], None, ALU.mult)
            else:
                if k % 2 == 0 or alt == 'act':
                    P.act(dst[:, k, :], tmpf[:, k, 0:n], AF.Identity, bias=S[:, k:k + 1], scale=A[:, k:k + 1])
                else:
                    P.ts(alt, dst[:, k, :], tmpf[:, k, 0:n], A[:, k:k + 1], S[:, k:k + 1], ALU.mult, ALU.add)

    def norm(RB, t0, n, dst, A=None, S=None, G=None, tmp=None, psn=None, alt='dve'):
        norm_a(t0, n, tmp, psn=psn)
        norm_b(t0, n, dst, tmp, A=A, S=S, G=G, alt=alt)

    epsc = CB.alloc(4, F32)
    P.memset('dve', epsc, EPS)


    kcol = CB.alloc(8 * 4, F32)
    P.memset('dve', kcol[:, 0:1], EPS)
    P.memset('dve', kcol[:, 1:2], 1.0)
    P.memset('dve', kcol[:, 2:3], math.log(SC))
    ctab = CB.alloc(NCT * 4, F32)
    P.dma('sp', ctab, ctab_d, sem="cols", grp="cols")
    def ct(key):
        o, n = CT[key]
        return ctab[:, o:o + n]
    cbt = CB.alloc(NCB * 2, BF16)
    P.dma('pool', cbt, cbt_d, sem="cbt")
    def cbv(key):
        o, n = CBT[key]
        return cbt[:, o:o + n]
    ident = cbv("ident"); causal = cbv("causal"); onesrow = cbv("onesrow")
    Pcur = cbv("Pcur").rearrange("p (g i) -> p g i", g=4)
    Pprev = cbv("Pprev").rearrange("p (g i) -> p g i", g=4)
    Pfirst = cbv("Pfirst").rearrange("p (g i) -> p g i", g=4)
    tri = ct("tri"); DTm = ct("DT").rearrange("p (h i) -> p h i", h=4)
    xit = ct("xi").rearrange("p (h i) -> p h i", h=4); zeta = ct("zeta"); invf = ct("invf"); one32 = ct("one")
    costab = CB.alloc(16 * 24 * 4, F32, "p (t i) -> p t i", t=16)
    sintab = CB.alloc(16 * 24 * 4, F32, "p (t i) -> p t i", t=16)
    wa2b = CB.alloc(192 * 2, BF16)
    bab = CB.alloc(192 * 2, BF16)
    PWblk = CB.alloc(2 * 128 * 2, BF16, "p (a c) -> p a c", a=2)
    gng = CB.alloc(384 * 4, F32)

    def trig_tables():
        TB_ = Bump(RB0, RLIM)
        posf = TB_.alloc(16 * 4, F32)
        ang = TB_.alloc(384 * 4, F32, "p (t i) -> p t i", t=16)
        a2 = TB_.alloc(384 * 4, F32, "p (t i) -> p t i", t=16)
        kf = TB_.alloc(384 * 4, F32, "p (t i) -> p t i", t=16)
        rr = TB_.alloc(384 * 4, F32, "p (t i) -> p t i", t=16)
        P.copy('dve', posf, posi)
        P.tt('dve', ang, posf.unsqueeze(2).broadcast_to([128, 16, 24]), invf.unsqueeze(1).broadcast_to([128, 16, 24]), ALU.mult)
        MAGIC = 12582912.0
        C1 = 6.28125; C2 = 2.0 * math.pi - 6.28125
        for tab, shift in ((sintab, 0.0), (costab, math.pi / 2)):
            P.ts('dve', a2, ang, shift, None, ALU.add)
            P.ts('dve', kf, a2, 1.0 / (2.0 * math.pi), MAGIC, ALU.mult, ALU.add)
            P.ts('dve', kf, kf, MAGIC, None, ALU.subtract)
            P.stt(rr, kf, -C1, a2, ALU.mult, ALU.add)
            P.stt(rr, kf, -C2, rr, ALU.mult, ALU.add)
            P.ts('dve', rr, rr, math.pi, -math.pi, ALU.min, ALU.max)
            P.act(tab, rr, AF.Sin)
    trig_tables()


    def mixer(l, msch=None, pre=None):
        RB = Bump(RB0, RLIM)
        Win = RB.alloc(KC * DIN * 2, BF16, "p (k c) -> p k c", k=KC)
        Wout = RB.alloc(KC * D * 2, BF16, "p (k c) -> p k c", k=KC)
        hT2 = RB.alloc(2 * KC * 128 * 2, BF16, "p (s k t) -> p s k t", s=2, k=KC)
        yT = RB.alloc(KC * 512 * 2, BF16, "p (k t) -> p k t", k=KC)
        sq = RB.alloc(KC * 128 * 2, BF16, "p (k t) -> p k t", k=KC)
        rstd = RB.alloc(128 * 4, F32); lnv = RB.alloc(128 * 4, F32)
        tmpf = RB.alloc(KC * 128 * 4, F32, "p (k t) -> p k t", k=KC)
        ubf = RB.alloc(3 * 256 * 2, BF16, "p (s c) -> p s c", s=3)
        pooledT = RB.alloc(2 * 128 * 2, BF16, "p (a t) -> p a t", a=2)
        rotA = RB.alloc(384 * 4, F32); rotB = RB.alloc(384 * 4, F32)
        gaT = RB.alloc(128 * 2, BF16)
        e1 = RB.alloc(192 * 4, F32); spb = e1
        eb = RB.alloc(192 * 4, F32); enb = RB.alloc(192 * 4, F32)
        ebl = RB.alloc(2 * 4 * 4, F32, "p (s h) -> p s h", s=2)
        ybuf = RB.alloc(2 * 384 * 2, BF16, "p (m c) -> p m c", m=2)
        MX = []
        for mi in range(2):
            d = {}
            d["qk"] = RB.alloc(2 * 8 * 48 * 2, BF16, "p (s c d) -> p s c d", s=2, c=8)
            d["qkT"] = RB.alloc(2 * 8 * 128 * 2, BF16, "p (s c t) -> p s c t", s=2, c=8)
            if mi == 0:
                d["qxT"] = RB.alloc(2 * 4 * 128 * 2, BF16, "p (s c t) -> p s c t", s=2, c=4)
                d["vz"] = RB.alloc(2 * 384 * 2, BF16, "p (s c) -> p s c", s=2)
            d["v"] = RB.alloc(2 * 384 * 2, BF16, "p (s c) -> p s c", s=2)
            d["STm"] = RB.alloc(4 * 128 * 2, BF16, "p (h t) -> p h t", h=4)
            d["S32"] = RB.alloc(4 * 96 * 4, F32, "p (h v) -> p h v", h=4)
            d["Sbf"] = RB.alloc(2 * 4 * 96 * 2, BF16, "p (s h v) -> p s h v", s=2, h=4)
            if mi == 1:
                d["tmpS"] = RB.alloc(4 * 96 * 4, F32, "p (h v) -> p h v", h=4)
            MX.append(d)

        assert RB.o <= RLIM - 4096, (RB.o, RLIM - 4096)
        sgj = RB.alloc(2 * 2 * 384 * 4, F32, "p (s m c) -> p s m c", s=2, m=2)
        for mi in range(2):
            MX[mi]["sg"] = sgj[:, :, mi, :]
        sq2 = RB.alloc(2 * 384 * 4, F32, "p (m c) -> p m c", m=2)
        ss8 = RB.alloc(8 * 4, F32); ln8 = RB.alloc(8 * 4, F32); rs8 = RB.alloc(8 * 4, F32)
        B3bf = ps[3][:, :].bitcast(BF16)
        B5bf = ps[5][:, :].bitcast(BF16)
        wiv = w_in[l].rearrange("(k p) c -> p k c", p=128)
        for k in range(KC):
            for hf in range(2):
                P.dma('pool', Win[:, k, hf * 1288:(hf + 1) * 1288], wiv[:, k, hf * 1288:(hf + 1) * 1288], sem="win", grp=("win", l))
        wov = w_out[l].rearrange("(k p) c -> p k c", p=128)
        for k in range(KC):
            P.dma('pool', Wout[:, k, :], wov[:, k, :], sem="wout", grp=("wout", l))
        P.memset('dve', PWblk, 0.0)
        for g in range(4):
            gs = g % 2; p_ = g // 2
            P.dma('pool', PWblk[64 * gs:64 * gs + 64, p_, 64 * gs:64 * gs + 64], pool_w[l, g], sem="small", grp=("small", l))
        P.dma('pool', wa2b[0:16, :], gla_wa2[l], sem="small", grp=("small", l))
        P.dma('pool', bab[0:1, :], gla_ba[l:l + 1, :], sem="small", grp=("small", l))
        P.dma('sp', gng, gng_d[l], sem="gng")
        if pre is not None:
            pre()
        for mi in range(2):
            P.memset('dve', MX[mi]["S32"][0:48], 0.0)
            P.memset('dve', MX[mi]["Sbf"][0:48, 0], 0.0)
        mod_derive(l, 0)
        A1 = a12[:, l, 0, :]; S1 = modT[:, l, 0:8]

        def proj(hT, c0, c1, bank):
            n = c1 - c0
            for k in range(KC):
                P.mm(ps[bank][:, 0:n], hT[:, k, :], Win[:, k, c0:c1], start=(k == 0), stop=(k == KC - 1))
            return ps[bank][:, 0:n]

        def modtick(n=1):
            if msch is None:
                return
            for _ in range(n):
                if msch.done():
                    return
                n0 = msch.nm
                msch.tick()
                if n0 < 24 <= msch.nm:
                    mod_finish(l, 16, 24, src=ps[0][:, 384:392])
                if msch.done():
                    mod_finish(l, 24, 48, src=ps[0][:, 392:416])

        def stage0(t):
            norm_a(t * 128, 128, (sq, rstd, lnv, tmpf), psn=ps[2][:, 384:512])
            yield
            modtick(2 if (msch is not None and msch.nm < 24) else 1)
            norm_b(t * 128, 128, hT2[:, t % 2], (sq, rstd, lnv, tmpf), A=A1, S=S1, alt='dve2')
            yield

        def stage1(t):
            par = t % 2
            dr_ = MX[0]; dg = MX[1]
            hT = hT2[:, t % 2]
            modtick(2 if (msch is not None and msch.nm < 24) else 1)
            for k in range(KC):
                P.mm(ps[0][0:16, 0:128], Win[:, k, 2176:2192], hT[:, k, :], start=(k == 0), stop=(k == KC - 1))
            P.copy('act', gaT[0:16, :], ps[0][0:16, 0:128])
            yield
            pq = proj(hT, 256, 640, 1)
            cosb = costab[:, t, :].unsqueeze(1).broadcast_to([128, 16, 24])
            sinb = sintab[:, t, :].unsqueeze(1).broadcast_to([128, 16, 24])
            pq3 = pq.rearrange("p (c i) -> p c i", i=24)
            P.tt('dve', rotA.rearrange("p (c i) -> p c i", i=24), pq3, cosb, ALU.mult)
            P.tt('dve', rotB.rearrange("p (c i) -> p c i", i=24), pq3, sinb, ALU.mult)
            A4 = rotA.rearrange("p (c f i) -> p c f i", f=2, i=24); B4 = rotB.rearrange("p (c f i) -> p c f i", f=2, i=24)
            R4 = dr_["qk"][:, par].rearrange("p c (f i) -> p c f i", f=2)
            P.tt('dve', R4[:, :, 0, :], A4[:, :, 0, :], B4[:, :, 1, :], ALU.subtract)
            P.tt('dve', R4[:, :, 1, :], A4[:, :, 1, :], B4[:, :, 0, :], ALU.add)
            yield
            P.mm(ps[2][:, 0:192], gaT[0:16, :], wa2b[0:16, :], start=True, stop=False)
            P.mm(ps[2][:, 0:192], onesrow[0:1, :], bab[0:1, :], start=False, stop=True)
            P.act(e1, ps[2][:, 0:192], AF.Exp, scale=-1.0)
            P.act(spb, e1, AF.Ln, bias=kcol[:, 1:2], scale=1.0)
            yield
            pu = proj(hT, 0, 256, 0)
            P.copy('act', ubf[:, t % 3, :], pu)
            yield
            P.mm(ps[2][:, 192:384], tri, spb)
            for h in range(4):
                P.mm(ps[0][0:48, 416 + h:417 + h], spb[:, 48 * h:48 * h + 48], one32)
            P.act(eb, ps[2][:, 192:384], AF.Exp, bias=kcol[:, 2:3], scale=1.0)
            P.act(enb, ps[2][:, 192:384], AF.Exp, scale=-1.0)
            P.act(ebl[0:48, par, :], ps[0][0:48, 416:420], AF.Exp)
            yield
            pv = proj(hT, 640, 1024, 1)
            P.copy('act', dr_["v"][:, par], pv)
            P.tt('dve', dr_["vz"][:, par].rearrange("p (h v) -> p h v", h=4), pv.rearrange("p (h v) -> p h v", h=4),
                 zeta.unsqueeze(2).broadcast_to([128, 4, 96]), ALU.mult)
            yield
            pv = proj(hT, 1792, 2176, 0)
            P.copy('act', dg["v"][:, par], pv)
            yield
            pg = proj(hT, 1408, 1792, 1)
            P.tt('dve', dg["qk"][:, par, 0:4, :], pg[:, 0:192].rearrange("p (c d) -> p c d", c=4), eb.rearrange("p (c d) -> p c d", c=4), ALU.mult)
            P.tt('dve', dg["qk"][:, par, 4:8, :], pg[:, 192:384].rearrange("p (c d) -> p c d", c=4), enb.rearrange("p (c d) -> p c d", c=4), ALU.mult)
            yield
            B3v = B3bf[0:48, :].rearrange("p (c t) -> p c t", c=8)
            for c in range(8):
                P.tr(B3bf[0:48, c * 128:(c + 1) * 128], dr_["qk"][:, par, c, :], ident)
            P.copy('act', dr_["qkT"][0:48, par], B3v)
            P.tt('dve', dr_["qxT"][0:48, par], B3v[:, 0:4, :], xit[0:48], ALU.mult)
            yield
            pgt = proj(hT, 1024, 1408, 0)
            P.act(dr_["sg"][:, par], pgt, AF.Silu)
            yield
            pgt = proj(hT, 2192, 2576, 1)
            P.act(dg["sg"][:, par], pgt, AF.Silu)
            P.tt('dve', dg["sg"][:, par], dg["sg"][:, par], gng, ALU.mult)
            yield
            for c in range(8):
                P.tr(B3bf[0:48, c * 128:(c + 1) * 128], dg["qk"][:, par, c, :], ident)
            P.copy('act', dg["qkT"][0:48, par], B3v)
            yield

        def st_mask(t, mi, kind):
            d = MX[mi]; par = t % 2
            qkT = d["qkT"][:, par]
            STp = ps[5][:, :].rearrange("p (h t) -> p h t", h=4)
            for h in range(4):
                P.mm(STp[:, h, :], qkT[0:48, 4 + h, :], qkT[0:48, h, :])
            if kind == "ret":
                P.tt('dve', d["STm"], STp, DTm, ALU.mult)
            else:
                P.tt('dve', d["STm"], STp, causal.unsqueeze(1).broadcast_to([128, 4, 128]), ALU.mult)

        def core(t, mi, kind):
            d = MX[mi]; par = t % 2
            qkT = d["qkT"][:, par]
            qxT = d["qxT"][:, par] if kind == "ret" else qkT
            vv = d["v"][:, par]
            Op = ps[6][:, 0:384] if kind == "ret" else ps[7][:, 0:384]
            dSbank = ps[4]
            kz = d["qk"][:, par, 4:8, :]
            vz = d["vz"][:, par] if kind == "ret" else vv
            for h in range(4):
                P.mm(dSbank[0:48, 96 * h:96 * h + 96], kz[:, h, :], vz[:, 96 * h:96 * h + 96])
            dSv = dSbank[0:48, 0:384].rearrange("p (h v) -> p h v", h=4)
            S32 = d["S32"]
            for h in range(4):
                P.mm(Op[:, 96 * h:96 * h + 96], d["STm"][:, h, :], vv[:, 96 * h:96 * h + 96], start=True, stop=False)
                P.mm(Op[:, 96 * h:96 * h + 96], qxT[0:48, h, :], d["Sbf"][0:48, par, h, :], start=False, stop=True)
            if kind == "ret":
                for h in range(4):
                    P.stt(S32[0:48, h, :], S32[0:48, h, :], GAMC[h], dSv[:, h, :], ALU.mult, ALU.add)
            else:
                P.tt('dve', d["tmpS"][0:48], dSv, S32[0:48], ALU.add)
                P.tt('dve', S32[0:48], d["tmpS"][0:48], ebl[0:48, par, :].unsqueeze(2).broadcast_to([48, 4, 96]), ALU.mult)
            P.copy('act', d["Sbf"][0:48, 1 - par], S32[0:48])

        def rms_both(t):
            par = t % 2
            Ob = ps67[:, :].rearrange("p (m c) -> p m c", m=2)[:, :, 0:384]
            P.act(sq2, Ob, AF.Square)
            P.op('dve', lambda e, o=ss8, i_=sq2.rearrange("p m (h v) -> p (m h) v", h=4): e.tensor_reduce(o, i_, mybir.AxisListType.X, ALU.add),
                 r=[sq2], w=[ss8])
            P.act(ln8, ss8, AF.Ln, bias=kcol[:, 0:1], scale=1.0 / 96.0)
            P.act(rs8, ln8, AF.Exp, scale=-0.5)
            P.tt('dve', sq2.rearrange("p m (h v) -> p m h v", h=4), Ob.rearrange("p m (h v) -> p m h v", h=4),
                 rs8.rearrange("p (m h) -> p m h", m=2).unsqueeze(3).broadcast_to([128, 2, 4, 96]), ALU.mult)
            P.tt('dve', ybuf, sq2, sgj[:, par], ALU.mult)

        def y_tr(t, mi):
            ti = t % 4
            for c in range(3):
                P.tr(B5bf[:, c * 128:(c + 1) * 128], ybuf[:, mi, c * 128:(c + 1) * 128], ident)
            P.copy('dve', yT[:, 2 + 3 * mi:5 + 3 * mi, ti * 128:(ti + 1) * 128], B5bf[:, 0:384].rearrange("p (c t) -> p c t", c=3))

        def stage2(t):
            ti = t % 4
            st_mask(t, 0, "ret")
            st_mask(t, 1, "gla")
            yield
            core(t, 0, "ret")
            core(t, 1, "gla")
            rms_both(t)
            yield
            cur = t % 3; prv = (t - 1) % 3
            for p_ in range(2):
                for gs in range(2):
                    g = 2 * p_ + gs
                    pp = ps[4][:, 384:512]
                    if t == 0:
                        P.mm(pp, ubf[:, cur, 128 * p_:128 * p_ + 128], Pfirst[:, g, :])
                    else:
                        P.mm(pp, ubf[:, cur, 128 * p_:128 * p_ + 128], Pcur[:, g, :], start=True, stop=False)
                        P.mm(pp, ubf[:, prv, 128 * p_:128 * p_ + 128], Pprev[:, g, :], start=False, stop=True)
                    P.copy('act', pooledT[64 * gs:64 * gs + 64, p_, :], pp[64 * gs:64 * gs + 64, :])
                    yield
                pm = ps[7][:, 384:512]
                P.mm(pm, PWblk[:, p_, :], pooledT[:, p_, :])
                P.act(yT[:, p_, ti * 128:(ti + 1) * 128], pm, AF.Identity, scale=col(("pscale", l), p_, 1))
            y_tr(t, 0)
            yield
            y_tr(t, 1)
            yield
            if ti == 3:
                bk = t // 4
                for m in range(KC):
                    wb = ps[(7, 6)[m % 2]]
                    for k in range(KC):
                        P.mm(wb[:, 0:512], Wout[:, k, m * 128:(m + 1) * 128], yT[:, k, :], start=(k == 0), stop=(k == KC - 1))
                    xs = X[:, m, bk * 512:(bk + 1) * 512]
                    P.stt(xs, wb[:, 0:512], modT[:, l, 16 + m:17 + m], xs, ALU.mult, ALU.add)
                    if m % 2 == 1:
                        yield

        def rr(gens):
            while gens:
                for g_ in list(gens):
                    try:
                        next(g_)
                    except StopIteration:
                        gens.remove(g_)
        rr([stage0(0)])
        rr([stage1(0), stage0(1)])
        for t in range(16):
            gens = [stage2(t)]
            if t + 2 < 16:
                gens.append(stage0(t + 2))
            if t + 1 < 16:
                gens.append(stage1(t + 1))
            rr(gens)

    def ffn(l, nextmod=None):
        RB = Bump(RB0, RLIM)
        h2 = RB.alloc(KC * T * 2, BF16, "p (k t) -> p k t", k=KC)
        actb = RB.alloc(6 * T * 2, BF16, "p (j t) -> p j t", j=6)
        wd = RB.alloc(6 * D * 2, BF16, "p (j c) -> p j c", j=6)
        wu = RB.alloc(2 * KC * 256 * 2, BF16, "p (s k c) -> p s k c", s=2, k=KC)
        NT = 128
        ntmps = []
        for i_ in range(2):
            sq_ = RB.alloc(KC * NT * 2, BF16, "p (k t) -> p k t", k=KC)
            rstd_ = RB.alloc(NT * 4, F32)
            lnv_ = RB.alloc(NT * 4, F32)
            tmpf_ = RB.alloc(KC * NT * 4, F32, "p (k t) -> p k t", k=KC)
            ntmps.append((sq_, rstd_, lnv_, tmpf_))
        mod_derive(l, 1)
        U = RB.alloc(3 * 2 * 514 * 4, F32, "p (s q c) -> p s q c", s=3, q=2)
        Y = RB.alloc(3 * 2 * 512 * 4, F32, "p (s q c) -> p s q c", s=3, q=2)
        wuv = w_up[l].rearrange("(k p) c -> p k c", p=128)
        NSB = T // NT
        def normgen():
            norm_a(0, NT, ntmps[0], psn=ps[6][:, 0:NT])
            for i in range(NSB):
                if i + 1 < NSB:
                    norm_a((i + 1) * NT, NT, ntmps[(i + 1) % 2], psn=ps[6][:, ((i + 1) % 2) * NT:((i + 1) % 2 + 1) * NT])
                norm_b(i * NT, NT, h2[:, :, i * NT:(i + 1) * NT], ntmps[i % 2], A=a12[:, l, 1, :], S=modT[:, l, 24:32])
                yield i
        ngen = normgen()
        ndone = [-1]
        def ensure_norm(tb):
            need = (tb + 1) * (512 // NT) - 1
            while ndone[0] < need:
                ndone[0] = next(ngen)
        def load_wu(j):
            slot = j % 2
            P.dma('pool', wu[:, slot, :, 0:128], wuv[:, :, j * 128:(j + 1) * 128], sem="wu%d" % slot, grp=("wu", l, j))
            P.dma('pool', wu[:, slot, :, 128:256], wuv[:, :, DFF + j * 128:DFF + (j + 1) * 128], sem="wu%d" % slot, grp=("wu", l, j))
        def load_wd(J):
            for jl, j in enumerate(J):
                P.dma('pool', wd[:, jl, :], w_down[l][j * 128:(j + 1) * 128, :], sem="wd%d" % jl)
        gstep = [0]
        ring4 = [arena[:, RLIM - 2048 * (i + 1):RLIM - 2048 * i].bitcast(BF16).rearrange("p (k c) -> p k c", k=8) for i in range(4)]
        msch = ModSched(nextmod, mode='bf16', ring=ring4, semp="modr4_%d") if nextmod is not None else None
        def stageA(st):
            jl, j, tb, g = st
            par = g % 2
            ensure_norm(tb)
            if msch is not None and not msch.done():
                msch.tick()
            slot = j % 2
            for qi in range(2):
                pst = ps[2 * par + qi]
                for k in range(KC):
                    P.mm(pst[:, :], wu[:, slot, k, qi * 128:(qi + 1) * 128], h2[:, k, tb * 512:(tb + 1) * 512],
                         start=(k == 0), stop=(k == KC - 1))
        def cw(j, qi):
            ch = qi * NJ + j
            return (col(("cw", l), 0 * 44 + ch, 1), col(("cw", l), 1 * 44 + ch, 1), col(("cw", l), 2 * 44 + ch, 1), col(("cb", l), ch, 1))
        def stageB(st):
            jl, j, tb, g = st
            par = g % 2; p3 = g % 3
            for qi in range(2):
                pst = ps[2 * par + qi]
                w0, w1, w2, bb = cw(j, qi)
                Ub = U[:, p3, qi, :]; Yb = Y[:, p3, qi, :]
                if tb == 0:
                    P.memset('pool', Ub[:, 0:2], 0.0)
                else:
                    P.copy('pool', Ub[:, 0:2], U[:, (g - 1) % 3, qi, 512:514])
                P.copy('act', Ub[:, 2:514], pst[:, :])
                P.act(Yb, pst[:, :], AF.Identity, bias=bb, scale=w2)
        def stageC(st):
            jl, j, tb, g = st
            p3 = g % 3
            for qi in range(2):
                w0, w1, w2, bb = cw(j, qi)
                Ub = U[:, p3, qi, :]; Yb = Y[:, p3, qi, :]
                P.stt(Yb, Ub[:, 1:513], w1, Yb, ALU.mult, ALU.add)
                P.stt(Yb, Ub[:, 0:512], w0, Yb, ALU.mult, ALU.add)
        def stageD1(st):
            jl, j, tb, g = st
            p3 = g % 3
            P.act(Y[:, p3, 1, :], Y[:, p3, 1, :], AF.Silu)
        def stageD2(st):
            jl, j, tb, g = st
            p3 = g % 3
            P.tt('dve', actb[:, jl, tb * 512:(tb + 1) * 512], Y[:, p3, 0, :], Y[:, p3, 1, :], ALU.mult)
        ycnt = 0
        load_wd(QUARTERS[0])
        load_wu(QUARTERS[0][0])
        def make_steps(q):
            J = QUARTERS[q]
            steps = []
            for jl, j in enumerate(J):
                for tb in range(4):
                    steps.append((jl, j, tb, gstep[0])); gstep[0] += 1
            return steps
        def prefetch_for(q, steps, idx):
            J = QUARTERS[q]
            jl, j, tb, g = steps[idx]
            if tb == 0:
                if jl + 1 < len(J):
                    load_wu(J[jl + 1])
                elif q + 1 < len(QUARTERS):
                    load_wu(QUARTERS[q + 1][0])
        def prologue(q, steps):
            prefetch_for(q, steps, 0)
            stageA(steps[0]); stageB(steps[0])
            prefetch_for(q, steps, 1)
            stageA(steps[1]); stageB(steps[1])
        cur_steps = make_steps(0)
        prologue(0, cur_steps)
        for q, J in enumerate(QUARTERS):
            steps = cur_steps
            n_ = len(steps)
            stageC(steps[0])
            for i in range(n_):
                if i + 2 < n_:
                    prefetch_for(q, steps, i + 2)
                    stageA(steps[i + 2]); stageB(steps[i + 2])
                if i + 1 < n_:
                    stageC(steps[i + 1])
                stageD1(steps[i]); stageD2(steps[i])
            if q + 1 < len(QUARTERS):
                cur_steps = make_steps(q + 1)
                prologue(q + 1, cur_steps)
            for tb in range(4):
                for m in range(KC):
                    pst = ps[4 + ycnt % 2]; ycnt += 1
                    for jl in range(len(J)):
                        P.mm(pst[:, :], wd[:, jl, m * 128:(m + 1) * 128], actb[:, jl, tb * 512:(tb + 1) * 512],
                             start=(jl == 0), stop=(jl == len(J) - 1))
                    xs = X[:, m, tb * 512:(tb + 1) * 512]
                    P.stt(xs, pst[:, :], modT[:, l, 40 + m:41 + m], xs, ALU.mult, ALU.add)
            if q + 1 < len(QUARTERS):
                load_wd(QUARTERS[q + 1])
        if msch is not None:
            msch.run_until(48)
            mod_finish(nextmod, 0, 48)

    def final():
        RB = Bump(RB0, RLIM)
        NT = 256
        tmps = []
        for i in range(2):
            sq = RB.alloc(KC * NT * 2, BF16, "p (k t) -> p k t", k=KC)
            rstd = RB.alloc(NT * 4, F32)
            lnv = RB.alloc(NT * 4, F32)
            tmpf = RB.alloc(KC * NT * 4, F32, "p (k t) -> p k t", k=KC)
            tmps.append((sq, rstd, lnv, tmpf))
        ob = RB.alloc(2 * KC * NT * 4, F32, "p (s k t) -> p s k t", s=2, k=KC)
        nb = T // NT
        norm_a(0, NT, tmps[0], psn=ps[6][:, 0:NT])
        for i in range(nb):
            t0 = i * NT
            if i + 1 < nb:
                norm_a(t0 + NT, NT, tmps[(i + 1) % 2], psn=ps[6 + (i + 1) % 2][:, 0:NT])
            norm_b(t0, NT, ob[:, i % 2], tmps[i % 2], G=col("gf"))
            P.dma('sp' if i % 2 == 0 else 'act', ov[:, :, t0:t0 + NT], ob[:, i % 2], sem="out%d" % (i % 2))

    layers = list(layers)
    for li, l in enumerate(layers):
        nxt = layers[li + 1] if li + 1 < len(layers) else None
        m0 = None
        pre = None
        if li == 0 or not do_ffn:
            if do_mixer:
                def pre():
                    ma = ModSched(l, mode='f32', lo=0, hi=16)
                    ma.run_until(16)
                    mod_finish(l, 0, 16)
                m0 = ModSched(l, mode='bf16', lo=16, hi=48)
                m0.where = lambda jc: ps[0][:, 384 + jc - 16:385 + jc - 16]
            else:
                m0 = ModSched(l)
                m0.run_until(48)
                mod_finish(l, 0, 48)
                m0 = None
        if do_mixer:
            mixer(l, msch=m0, pre=pre)
        if do_ffn:
            ffn(l, nextmod=nxt)
    final()
    st = P.emit()
    return st


def kernel(**inputs):
    inp = {k: np.asarray(v) for k, v in inputs.items()}
    nc = bass.Bass("TRN2", target_bir_lowering=False)
    build(nc)
    in_maps = [host_inputs(inp, b) for b in range(8)]
    res = run_bass_kernel_spmd(nc, in_maps, core_ids=list(range(8)))
    out = np.stack([np.asarray(res.results[b]["outT"]).T for b in range(8)], axis=0)
    return np.ascontiguousarray(out, dtype=np.float32)
```

```python
import math
import numpy as np
import concourse.bass as bass
import concourse.mybir as mybir
from concourse.bass_utils import run_bass_kernel_spmd

F32 = mybir.dt.float32; BF16 = mybir.dt.bfloat16; I32 = mybir.dt.int32; U8 = mybir.dt.uint8
ALU = mybir.AluOpType; AF = mybir.ActivationFunctionType


def _rng(ap):
    sp = str(ap.space)
    if 'SB' not in sp and 'PSUM' not in sp:
        return None
    if 'PSUM' in sp:
        pat = ap.ap
        ds = mybir.dt.size(ap.dtype)
        pstep = pat[0][0]
        col0 = ap.offset % pstep if pstep > 0 else ap.offset
        ext = 1
        for st, cn in pat[1:]:
            ext += (cn - 1) * abs(st)
        b0 = (col0 * ds) // 2048; b1 = ((col0 + ext) * ds - 1) // 2048
        return [("%s#%d" % (ap.tensor.name, b), (0, 2048, ((0, 2048),))) for b in range(b0, b1 + 1)]
    pat = ap.ap
    ds = mybir.dt.size(ap.dtype)
    pstep = pat[0][0]
    col0 = ap.offset % pstep if pstep > 0 else ap.offset
    dims = sorted([(abs(st), cn) for st, cn in pat[1:] if cn > 1 and st != 0])
    run = 1
    rest = []
    for st, cn in dims:
        if st <= run:
            run = max(run, (cn - 1) * st + run)
        else:
            rest.append((st, cn))
    starts = [0]
    nint = 1
    for st, cn in rest:
        nint *= cn
    if nint > 64:
        ext = run
        for st, cn in rest:
            ext += (cn - 1) * st
        ivs = [(col0 * ds, (col0 + ext) * ds)]
    else:
        for st, cn in rest:
            starts = [a + i * st for a in starts for i in range(cn)]
        starts.sort()
        ivs = []
        for a in starts:
            s0 = (col0 + a) * ds; e0 = (col0 + a + run) * ds
            if ivs and s0 <= ivs[-1][1]:
                ivs[-1] = (ivs[-1][0], max(ivs[-1][1], e0))
            else:
                ivs.append((s0, e0))
    return (ap.tensor.name, (ivs[0][0], ivs[-1][1], tuple(ivs)))


def _ovl(a, b):
    if not (a[0] < b[1] and b[0] < a[1]):
        return False
    ia = a[2]; ib = b[2]
    if len(ia) == 1 and len(ib) == 1:
        return True
    i = j = 0
    while i < len(ia) and j < len(ib):
        if ia[i][0] < ib[j][1] and ib[j][0] < ia[i][1]:
            return True
        if ia[i][1] <= ib[j][1]:
            i += 1
        else:
            j += 1
    return False


def _covers(a, b):
    if not (a[0] <= b[0] and b[1] <= a[1]):
        return False
    ia = a[2]
    for (s, e) in b[2]:
        ok = False
        for (s2, e2) in ia:
            if s2 <= s and e <= e2:
                ok = True; break
        if not ok:
            return False
    return True


class Prog:
    ENG = ('pe', 'act', 'dve', 'pool', 'sp')

    def __init__(self, nc):
        self.nc = nc
        self.ops = []
        self.grp_ctr = 0

    def op(self, eng, fn, r=(), w=()):
        def flat(lst):
            out = []
            for x in lst:
                if not x:
                    continue
                if isinstance(x, list):
                    out.extend(x)
                else:
                    out.append(x)
            return out
        rr = flat(_rng(a) for a in r if a is not None and not isinstance(a, (int, float)))
        ww = flat(_rng(a) for a in w)
        ww = ww + [x for x in rr if x[0].startswith('ps') and x not in ww]
        self.ops.append(dict(k='c', eng=eng, fn=fn, r=rr, w=ww))

    def dma(self, eng, out, in_, sem, grp=None, **kw):
        if grp is None:
            self.grp_ctr += 1
            grp = ('_g', self.grp_ctr)
        rr = [x for x in (_rng(in_),) if x and not isinstance(x, list)]
        ww = [x for x in (_rng(out),) if x and not isinstance(x, list)]
        self.ops.append(dict(k='d', eng=eng, out=out, in_=in_, r=rr, w=ww, sem=sem, grp=grp, kw=kw))

    def mm(self, out, lhsT, rhs, start=True, stop=True):
        self.op('pe', lambda e: e.matmul(out, lhsT, rhs, start=start, stop=stop), r=[lhsT, rhs], w=[out])

    def tr(self, out, in_, ident):
        self.op('pe', lambda e: e.transpose(out, in_, ident), r=[in_, ident], w=[out])

    def act(self, out, in_, func, bias=None, scale=None, eng='act'):
        kw = {}
        if bias is not None: kw['bias'] = bias
        if scale is not None: kw['scale'] = scale
        self.op(eng, lambda e: e.activation(out, in_, func, **kw), r=[in_, bias, scale], w=[out])

    def tt(self, eng, out, in0, in1, op):
        self.op(eng, lambda e: e.tensor_tensor(out, in0, in1, op), r=[in0, in1], w=[out])

    def ts(self, eng, out, in0, s1, s2, op0, op1=None):
        if op1 is None:
            self.op(eng, lambda e: e.tensor_scalar(out, in0, s1, None, op0), r=[in0, s1], w=[out])
        else:
            self.op(eng, lambda e: e.tensor_scalar(out, in0, s1, s2, op0, op1), r=[in0, s1, s2], w=[out])

    def stt(self, out, in0, scalar, in1, op0, op1, eng='dve'):
        self.op(eng, lambda e: e.scalar_tensor_tensor(out, in0, scalar, in1, op0, op1), r=[in0, scalar, in1], w=[out])

    def copy(self, eng, out, in_):
        if eng == 'act':
            self.op(eng, lambda e: e.copy(out, in_), r=[in_], w=[out])
        else:
            self.op(eng, lambda e: e.tensor_copy(out, in_), r=[in_], w=[out])

    def memset(self, eng, out, val):
        self.op(eng, lambda e: e.memset(out, val), r=[], w=[out])

    def emit(self):
        nc = self.nc
        engs = {'pe': nc.tensor, 'act': nc.scalar, 'dve': nc.vector, 'pool': nc.gpsimd, 'sp': nc.sync}
        ops = self.ops
        n = len(ops)
        W = {}
        R = {}
        deps = [None] * n
        needed = [False] * n
        for i, o in enumerate(ops):
            raw = set(); oth = set()
            for (sp, f) in o['r']:
                for rec in W.get(sp, ()):
                    if _ovl(rec[0], f):
                        raw.add(rec[1])
            for (sp, f) in o['w']:
                for rec in W.get(sp, ()):
                    if _ovl(rec[0], f):
                        oth.add(rec[1])
                for rec in R.get(sp, ()):
                    if _ovl(rec[0], f):
                        oth.add(rec[1])
            d = set()
            for j in raw | oth:
                pj = ops[j]
                if j not in raw and o['k'] == 'c' and pj['k'] == 'c' and pj['eng'] == o['eng'] == 'pe':
                    continue
                if o['k'] == 'd' and pj['k'] == 'd' and o['sem'] == pj['sem'] and o['grp'] == pj['grp']:
                    continue
                d.add(j)
            d.discard(i)
            deps[i] = d
            for j in d: needed[j] = True
            for (sp, f) in o['w']:
                W[sp] = [rec for rec in W.get(sp, ()) if not _covers(f, rec[0])]
                R[sp] = [rec for rec in R.get(sp, ()) if not _covers(f, rec[0])]
                W[sp].append((f, i))
            for (sp, f) in o['r']:
                lst = R.setdefault(sp, [])
                eng = o['eng']
                if o['k'] == 'c':
                    lst[:] = [rec for rec in lst if not (rec[0] == f and ops[rec[1]]['k'] == 'c' and ops[rec[1]]['eng'] == eng)]
                lst.append((f, i))
        esem = {e: nc.alloc_semaphore("s_" + e) for e in engs}
        dsem = {}
        cnt = {e: 0 for e in engs}
        dcnt = {}
        tok = [None] * n
        grp_final = {}
        for i, o in enumerate(ops):
            if o['k'] == 'c':
                if needed[i]:
                    cnt[o['eng']] += 1
                    tok[i] = (esem[o['eng']], cnt[o['eng']])
            else:
                s = o['sem']
                if s not in dsem:
                    dsem[s] = nc.alloc_semaphore("d_%d" % len(dsem)); dcnt[s] = 0
                dcnt[s] += 16
                grp_final[(s, o['grp'])] = dcnt[s]
        for i, o in enumerate(ops):
            if o['k'] == 'd':
                tok[i] = (dsem[o['sem']], grp_final[(o['sem'], o['grp'])])
        waited = {e: {} for e in engs}
        nwaits = 0
        for i, o in enumerate(ops):
            e = o['eng']; E = engs[e]
            need = {}
            for j in deps[i]:
                s, v = tok[j]
                if need.get(s, 0) < v: need[s] = v
            for s, v in need.items():
                if waited[e].get(s, 0) >= v: continue
                E.wait_ge(s, v); waited[e][s] = v; nwaits += 1
            if o['k'] == 'c':
                ins = o['fn'](E)
                if needed[i]:
                    ins.then_inc(tok[i][0], 1)
            else:
                ins = E.dma_start(out=o['out'], in_=o['in_'], **o['kw'])
                ins.then_inc(dsem[o['sem']], 16)
        E = engs['sp']
        for s, v in dcnt.items():
            E.wait_ge(dsem[s], v)
        self.stats = dict(n=n, nwaits=nwaits, cnt=dict(cnt), nsem=len(esem) + len(dsem))
        return self.stats


D = 1024; T = 2048; KC = 8; DIN = 2576; DFF = 2816; NJ = 22
EPS = 1e-6
QUARTERS = [list(range(0, 6)), list(range(6, 12)), list(range(12, 17)), list(range(17, 22))]

COLS = {}
_o = 0
for _l in range(2):
    for _nm, _n in (("g1", 8), ("adab", 48), ("pscale", 2), ("g2", 8), ("cw", 132), ("cb", 44)):
        COLS[(_nm, _l)] = (_o, _n); _o += _n
COLS["gf"] = (_o, 8); _o += 8
COLS["c"] = (_o, 8); _o += 8
NCOL = _o


def host_cols(inp, b):
    cols = np.zeros((128, NCOL), np.float32)
    def put(key, arr):
        o, n = COLS[key]
        cols[:, o:o + n] = arr
    for l in range(2):
        put(("g1", l), inp["norm1_g"][l].reshape(8, 128).T)
        put(("adab", l), inp["ada_b"][l].reshape(48, 128).T)
        put(("pscale", l), inp["pool_scale"][l].reshape(2, 128).T)
        put(("g2", l), inp["norm2_g"][l].reshape(8, 128).T)
        cw = inp["conv_w"][l].reshape(3, 44, 128).transpose(2, 0, 1).reshape(128, 132)
        put(("cw", l), cw)
        put(("cb", l), inp["conv_b"][l].reshape(44, 128).T)
    put("gf", inp["final_g"].reshape(8, 128).T)
    put("c", inp["c"][b].reshape(8, 128).T)
    return cols


SC = 48.0 ** -0.5
GAM = [1.0 - 2.0 ** (-5.0 - h) for h in range(4)]
GAMC = [g ** 128 for g in GAM]
CT = {}
_o = 0
for _nm, _n in (("tri", 128), ("DT", 512), ("xi", 512), ("zeta", 4), ("invf", 24), ("one", 1)):
    CT[_nm] = (_o, _n); _o += _n
NCT = _o
CBT = {}
_o = 0
for _nm, _n in (("ident", 128), ("causal", 128), ("Pcur", 512), ("Pprev", 512), ("Pfirst", 512), ("onesrow", 128)):
    CBT[_nm] = (_o, _n); _o += _n
NCB = _o
_HC = {}


def host_consts():
    if _HC:
        return _HC["ct"], _HC["cb"]
    ct = np.zeros((128, NCT), np.float64)
    cb = np.zeros((128, NCB), np.float64)
    j = np.arange(128)[:, None]; i = np.arange(128)[None, :]
    tri = (j <= i).astype(np.float64)
    ct[:, CT["tri"][0]:CT["tri"][0] + 128] = -tri / 16.0
    DT = np.zeros((128, 4, 128)); xi = np.zeros((128, 4, 128)); zeta = np.zeros((128, 4))
    for h in range(4):
        DT[:, h, :] = np.where(i >= j, SC * GAM[h] ** np.maximum(i - j, 0), 0.0)
        xi[:, h, :] = SC * GAM[h] ** (i + 1.0)
        zeta[:, h] = GAM[h] ** (127.0 - np.arange(128))
    ct[:, CT["DT"][0]:CT["DT"][0] + 512] = DT.reshape(128, 512)
    ct[:, CT["xi"][0]:CT["xi"][0] + 512] = xi.reshape(128, 512)
    ct[:, CT["zeta"][0]:CT["zeta"][0] + 4] = zeta
    ct[:, CT["invf"][0]:CT["invf"][0] + 24] = (10000.0 ** (-np.arange(0, 48, 2) / 48.0))[None, :]
    ct[:, CT["one"][0]] = -1.0 / 16.0
    cb[:, CBT["ident"][0]:CBT["ident"][0] + 128] = np.eye(128)
    cb[:, CBT["causal"][0]:CBT["causal"][0] + 128] = tri
    Pc = np.zeros((128, 4, 128)); Pp = np.zeros((128, 4, 128)); Pf = np.zeros((128, 4, 128))
    for g, w in enumerate((2, 4, 8, 16)):
        Pc[:, g, :] = np.where((j <= i) & (j > i - w), 1.0 / w, 0.0) - (j == i)
        Pp[:, g, :] = np.where(j - 128 > i - w, 1.0 / w, 0.0)
        cnt = np.minimum(i + 1, w)
        Pf[:, g, :] = np.where((j <= i) & (j > i - w), 1.0 / cnt, 0.0) - (j == i)
    cb[:, CBT["Pcur"][0]:CBT["Pcur"][0] + 512] = Pc.reshape(128, 512)
    cb[:, CBT["Pprev"][0]:CBT["Pprev"][0] + 512] = Pp.reshape(128, 512)
    cb[:, CBT["Pfirst"][0]:CBT["Pfirst"][0] + 512] = Pf.reshape(128, 512)
    cb[:, CBT["onesrow"][0]:CBT["onesrow"][0] + 128] = 1.0
    _HC["ct"] = ct.astype(np.float32); _HC["cb"] = cb.astype(np.float32)
    return _HC["ct"], _HC["cb"]


def host_inputs(inp, b):
    ct, cb = host_consts()
    return {"xT": np.ascontiguousarray(inp['x'][b].T), "cols": host_cols(inp, b),
            "pos": np.ascontiguousarray(inp['positions'][b].reshape(16, 128).T),
            "ctab": ct, "cbt": cb,
            "gng": np.ascontiguousarray(np.broadcast_to(inp['gla_norm_g'][:, None, :], (2, 128, 384))),
            "ada_w": inp['ada_w'], "w_in": inp['w_in'], "w_out": inp['w_out'], "pool_w": inp['pool_w'],
            "gla_wa2": inp['gla_wa2'], "gla_ba": inp['gla_ba'],
            "w_up": inp['w_up'], "w_down": inp['w_down']}


def build(nc, layers=(0, 1), do_mixer=True, do_ffn=True, parts=("pool", "ret", "gla")):
    P = Prog(nc)
    dr = {}
    def din(name, shape, dt=F32):
        dr[name] = nc.dram_tensor(name, list(shape), dt, kind="ExternalInput").ap()
        return dr[name]
    xT = din("xT", [D, T])
    cols_d = din("cols", [128, NCOL])
    ada_w = din("ada_w", [2, D, 6 * D])
    pos_d = din("pos", [128, 16], I32)
    ctab_d = din("ctab", [128, NCT])
    cbt_d = din("cbt", [128, NCB])
    gng_d = din("gng", [2, 128, 384])
    w_in = din("w_in", [2, D, DIN])
    w_out = din("w_out", [2, D, D])
    pool_w = din("pool_w", [2, 4, 64, 64])
    gla_wa2 = din("gla_wa2", [2, 16, 192])
    gla_ba = din("gla_ba", [2, 192])
    w_up = din("w_up", [2, D, 2 * DFF])
    w_down = din("w_down", [2, DFF, D])
    outT = nc.dram_tensor("outT", [D, T], F32, kind="ExternalOutput").ap()

    arena = nc.alloc_sbuf_tensor("arena", [128, 207 * 1024], U8)
    ps = [nc.alloc_psum_tensor("ps%d" % i, [128, 512], F32) for i in range(6)]
    ps67 = nc.alloc_psum_tensor("ps67", [128, 1024], F32)
    ps.append(ps67[:, 0:512]); ps.append(ps67[:, 512:1024])

    class Bump:
        def __init__(self, base, limit): self.o = base; self.base = base; self.limit = limit
        def alloc(self, nbytes, dt, pat=None, **kw):
            assert self.o + nbytes <= self.limit, (self.o, nbytes, self.limit)
            a = arena[:, self.o:self.o + nbytes].bitcast(dt)
            self.o += (nbytes + 31) // 32 * 32
            return a.rearrange(pat, **kw) if pat else a

    X = arena[:, 0:65536].bitcast(F32).rearrange("p (k t) -> p k t", k=KC)
    CB = Bump(65536, 65536 + 18 * 1024)
    cols = CB.alloc(NCOL * 4, F32)
    def col(key, a=0, n=None):
        o, nn = COLS[key]
        n = nn - a if n is None else n
        return cols[:, o + a:o + a + n]
    modT = CB.alloc(2 * 48 * 4, F32, "p (l j) -> p l j", l=2)
    a12 = CB.alloc(2 * 2 * 8 * 4, F32, "p (l s k) -> p l s k", l=2, s=2)
    cact = CB.alloc(8 * 2, BF16)
    cact32 = CB.alloc(8 * 4, F32)
    onesb = CB.alloc(128 * 2, BF16)
    RB0 = CB.limit
    RLIM = 207 * 1024
    modring = arena[:, RLIM - 4096:RLIM].bitcast(BF16).rearrange("p (s k c) -> p s k c", s=2, k=8)
    modring32 = arena[:, RLIM - 8192:RLIM].bitcast(F32).rearrange("p (s k c) -> p s k c", s=2, k=8)

    xv = xT.rearrange("(k p) t -> p k t", p=128)
    ov = outT.rearrange("(k p) t -> p k t", p=128)
    P.dma('sp', cols, cols_d, sem="cols", grp="cols")
    posi = CB.alloc(16 * 4, I32)
    P.dma('sp', posi, pos_d, sem="cols", grp="cols")
    for k in range(KC):
        P.dma('sp' if k % 2 == 0 else 'act', X[:, k, :], xv[:, k, :], sem="xin", grp="xin")
    P.memset('dve', onesb, 1.0 / 1024.0)
    P.act(cact, col("c"), AF.Silu)
    P.act(cact32, col("c"), AF.Silu)

    PS_NORM = ps[6]; PS_MOD = ps[7]

    class ModSched:
        def __init__(self, l, mode='bf16', lo=0, hi=48, ring=None, semp=None):
            self.l = l; self.nd = lo; self.nm = lo; self.mode = mode; self.hi = hi
            self.wv = ada_w[l].rearrange("(k p) c -> p k c", p=128)
            if ring is None:
                ring = [modring[:, i] for i in range(2)] if mode == 'bf16' else [modring32[:, i] for i in range(2)]
            self.ring = ring; self.depth = len(ring)
            self.semp = semp if semp is not None else ("modring%d" if mode == 'bf16' else "modr32_%d")
        def where(self, jc):
            return PS_MOD[:, jc:jc + 1]
        def tick(self):
            if self.nd < self.hi and self.nd - self.nm < self.depth:
                jc = self.nd; self.nd += 1
                sl = jc % self.depth
                P.dma('pool' if self.mode == 'bf16' else 'sp', self.ring[sl], self.wv[:, :, jc * 128:(jc + 1) * 128], sem=self.semp % sl)
                if self.nd - self.nm < self.depth and self.nd < self.hi:
                    return
            if self.nm < self.nd:
                jc = self.nm; self.nm += 1
                sl = jc % self.depth
                out = self.where(jc)
                rhs = cact if self.mode == 'bf16' else cact32
                for k in range(KC):
                    P.mm(out, self.ring[sl][:, k, :], rhs[:, k:k + 1], start=(k == 0), stop=(k == KC - 1))
        def run_until(self, n):
            while self.nm < n:
                self.tick()
        def done(self):
            return self.nm >= self.hi
    def mod_finish(l, j0, j1, src=None):
        src = PS_MOD[:, j0:j1] if src is None else src
        P.tt('dve', modT[:, l, j0:j1], src, col(("adab", l), j0, j1 - j0), ALU.add)
    def mod_derive(l, which):
        sc = modT[:, l, 8:16] if which == 0 else modT[:, l, 32:40]
        g = col(("g1", l)) if which == 0 else col(("g2", l))
        P.stt(a12[:, l, which, :], sc, 1.0, g, ALU.add, ALU.mult)

    def norm_a(t0, n, tmp, psn=None):
        sq, rstd, lnv, tmpf = tmp
        PSN = psn if psn is not None else PS_NORM[:, 0:n]
        xs = X[:, :, t0:t0 + n]
        P.act(sq[:, :, 0:n], xs, AF.Square)
        for k in range(KC):
            P.mm(PSN, onesb, sq[:, k, 0:n], start=(k == 0), stop=(k == KC - 1))
        P.act(lnv[:, 0:n], PSN, AF.Ln, bias=epsc[:, 0:1], scale=1.0)
        P.act(rstd[:, 0:n], lnv[:, 0:n], AF.Exp, scale=-0.5)

    def norm_b(t0, n, dst, tmp, A=None, S=None, G=None, alt='dve'):
        sq, rstd, lnv, tmpf = tmp
        xs = X[:, :, t0:t0 + n]
        rb = rstd[:, 0:n].unsqueeze(1).broadcast_to([128, KC, n])
        P.tt('dve', tmpf[:, :, 0:n], xs, rb, ALU.mult)
        if alt == 'dve2':
            P.tt('dve', tmpf[:, :, 0:n], tmpf[:, :, 0:n], A.unsqueeze(2).broadcast_to([128, KC, n]), ALU.mult)
            P.tt('dve', dst, tmpf[:, :, 0:n], S.unsqueeze(2).broadcast_to([128, KC, n]), ALU.add)
            return
        if alt == 'pool2':
            P.tt('pool', tmpf[:, :, 0:n], tmpf[:, :, 0:n], A.unsqueeze(2).broadcast_to([128, KC, n]), ALU.mult)
            P.tt('pool', dst, tmpf[:, :, 0:n], S.unsqueeze(2).broadcast_to([128, KC, n]), ALU.add)
            return
        for k in range(KC):
            if G is not None:
                if k % 2 == 0 or alt == 'act':
                    P.act(dst[:, k, :], tmpf[:, k, 0:n], AF.Identity, scale=G[:, k:k + 1])
                else:
                    P.ts(alt, dst[:, k, :], tmpf[:, k, 0:n], G[:, k:k + 1], None, ALU.mult)
            else:
                if k % 2 == 0 or alt == 'act':
                    P.act(dst[:, k, :], tmpf[:, k, 0:n], AF.Identity, bias=S[:, k:k + 1], scale=A[:, k:k + 1])
                else:
                    P.ts(alt, dst[:, k, :], tmpf[:, k, 0:n], A[:, k:k + 1], S[:, k:k + 1], ALU.mult, ALU.add)

    def norm(RB, t0, n, dst, A=None, S=None, G=None, tmp=None, psn=None, alt='dve'):
        norm_a(t0, n, tmp, psn=psn)
        norm_b(t0, n, dst, tmp, A=A, S=S, G=G, alt=alt)

    epsc = CB.alloc(4, F32)
    P.memset('dve', epsc, EPS)


    kcol = CB.alloc(8 * 4, F32)
    P.memset('dve', kcol[:, 0:1], EPS)
    P.memset('dve', kcol[:, 1:2], 1.0)
    P.memset('dve', kcol[:, 2:3], math.log(SC))
    ctab = CB.alloc(NCT * 4, F32)
    P.dma('sp', ctab, ctab_d, sem="cols", grp="cols")
    def ct(key):
        o, n = CT[key]
        return ctab[:, o:o + n]
    cbt = CB.alloc(NCB * 2, BF16)
    P.dma('pool', cbt, cbt_d, sem="cbt")
    def cbv(key):
        o, n = CBT[key]
        return cbt[:, o:o + n]
    ident = cbv("ident"); causal = cbv("causal"); onesrow = cbv("onesrow")
    Pcur = cbv("Pcur").rearrange("p (g i) -> p g i", g=4)
    Pprev = cbv("Pprev").rearrange("p (g i) -> p g i", g=4)
    Pfirst = cbv("Pfirst").rearrange("p (g i) -> p g i", g=4)
    tri = ct("tri"); DTm = ct("DT").rearrange("p (h i) -> p h i", h=4)
    xit = ct("xi").rearrange("p (h i) -> p h i", h=4); zeta = ct("zeta"); invf = ct("invf"); one32 = ct("one")
    costab = CB.alloc(16 * 24 * 4, F32, "p (t i) -> p t i", t=16)
    sintab = CB.alloc(16 * 24 * 4, F32, "p (t i) -> p t i", t=16)
    wa2b = CB.alloc(192 * 2, BF16)
    bab = CB.alloc(192 * 2, BF16)
    PWblk = CB.alloc(2 * 128 * 2, BF16, "p (a c) -> p a c", a=2)
    gng = CB.alloc(384 * 4, F32)

    def trig_tables():
        TB_ = Bump(RB0, RLIM)
        posf = TB_.alloc(16 * 4, F32)
        ang = TB_.alloc(384 * 4, F32, "p (t i) -> p t i", t=16)
        a2 = TB_.alloc(384 * 4, F32, "p (t i) -> p t i", t=16)
        kf = TB_.alloc(384 * 4, F32, "p (t i) -> p t i", t=16)
        rr = TB_.alloc(384 * 4, F32, "p (t i) -> p t i", t=16)
        P.copy('dve', posf, posi)
        P.tt('dve', ang, posf.unsqueeze(2).broadcast_to([128, 16, 24]), invf.unsqueeze(1).broadcast_to([128, 16, 24]), ALU.mult)
        MAGIC = 12582912.0
        C1 = 6.28125; C2 = 2.0 * math.pi - 6.28125
        for tab, shift in ((sintab, 0.0), (costab, math.pi / 2)):
            P.ts('dve', a2, ang, shift, None, ALU.add)
            P.ts('dve', kf, a2, 1.0 / (2.0 * math.pi), MAGIC, ALU.mult, ALU.add)
            P.ts('dve', kf, kf, MAGIC, None, ALU.subtract)
            P.stt(rr, kf, -C1, a2, ALU.mult, ALU.add)
            P.stt(rr, kf, -C2, rr, ALU.mult, ALU.add)
            P.ts('dve', rr, rr, math.pi, -math.pi, ALU.min, ALU.max)
            P.act(tab, rr, AF.Sin)
    trig_tables()


    def mixer(l, msch=None, pre=None):
        RB = Bump(RB0, RLIM)
        Win = RB.alloc(KC * DIN * 2, BF16, "p (k c) -> p k c", k=KC)
        Wout = RB.alloc(KC * D * 2, BF16, "p (k c) -> p k c", k=KC)
        hT2 = RB.alloc(2 * KC * 128 * 2, BF16, "p (s k t) -> p s k t", s=2, k=KC)
        yT = RB.alloc(KC * 512 * 2, BF16, "p (k t) -> p k t", k=KC)
        sq = RB.alloc(KC * 128 * 2, BF16, "p (k t) -> p k t", k=KC)
        rstd = RB.alloc(128 * 4, F32); lnv = RB.alloc(128 * 4, F32)
        tmpf = RB.alloc(KC * 128 * 4, F32, "p (k t) -> p k t", k=KC)
        ubf = RB.alloc(3 * 256 * 2, BF16, "p (s c) -> p s c", s=3)
        pooledT = RB.alloc(2 * 128 * 2, BF16, "p (a t) -> p a t", a=2)
        rotA = RB.alloc(384 * 4, F32); rotB = RB.alloc(384 * 4, F32)
        gaT = RB.alloc(128 * 2, BF16)
        e1 = RB.alloc(192 * 4, F32); spb = e1
        eb = RB.alloc(192 * 4, F32); enb = RB.alloc(192 * 4, F32)
        ebl = RB.alloc(2 * 4 * 4, F32, "p (s h) -> p s h", s=2)
        ybuf = RB.alloc(2 * 384 * 2, BF16, "p (m c) -> p m c", m=2)
        MX = []
        for mi in range(2):
            d = {}
            d["qk"] = RB.alloc(2 * 8 * 48 * 2, BF16, "p (s c d) -> p s c d", s=2, c=8)
            d["qkT"] = RB.alloc(2 * 8 * 128 * 2, BF16, "p (s c t) -> p s c t", s=2, c=8)
            if mi == 0:
                d["qxT"] = RB.alloc(2 * 4 * 128 * 2, BF16, "p (s c t) -> p s c t", s=2, c=4)
                d["vz"] = RB.alloc(2 * 384 * 2, BF16, "p (s c) -> p s c", s=2)
            d["v"] = RB.alloc(2 * 384 * 2, BF16, "p (s c) -> p s c", s=2)
            d["STm"] = RB.alloc(4 * 128 * 2, BF16, "p (h t) -> p h t", h=4)
            d["S32"] = RB.alloc(4 * 96 * 4, F32, "p (h v) -> p h v", h=4)
            d["Sbf"] = RB.alloc(2 * 4 * 96 * 2, BF16, "p (s h v) -> p s h v", s=2, h=4)
            if mi == 1:
                d["tmpS"] = RB.alloc(4 * 96 * 4, F32, "p (h v) -> p h v", h=4)
            MX.append(d)

        assert RB.o <= RLIM - 4096, (RB.o, RLIM - 4096)
        sgj = RB.alloc(2 * 2 * 384 * 4, F32, "p (s m c) -> p s m c", s=2, m=2)
        for mi in range(2):
            MX[mi]["sg"] = sgj[:, :, mi, :]
        sq2 = RB.alloc(2 * 384 * 4, F32, "p (m c) -> p m c", m=2)
        ss8 = RB.alloc(8 * 4, F32); ln8 = RB.alloc(8 * 4, F32); rs8 = RB.alloc(8 * 4, F32)
        B3bf = ps[3][:, :].bitcast(BF16)
        B5bf = ps[5][:, :].bitcast(BF16)
        wiv = w_in[l].rearrange("(k p) c -> p k c", p=128)
        for k in range(KC):
            for hf in range(2):
                P.dma('pool', Win[:, k, hf * 1288:(hf + 1) * 1288], wiv[:, k, hf * 1288:(hf + 1) * 1288], sem="win", grp=("win", l))
        wov = w_out[l].rearrange("(k p) c -> p k c", p=128)
        for k in range(KC):
            P.dma('pool', Wout[:, k, :], wov[:, k, :], sem="wout", grp=("wout", l))
        P.memset('dve', PWblk, 0.0)
        for g in range(4):
            gs = g % 2; p_ = g // 2
            P.dma('pool', PWblk[64 * gs:64 * gs + 64, p_, 64 * gs:64 * gs + 64], pool_w[l, g], sem="small", grp=("small", l))
        P.dma('pool', wa2b[0:16, :], gla_wa2[l], sem="small", grp=("small", l))
        P.dma('pool', bab[0:1, :], gla_ba[l:l + 1, :], sem="small", grp=("small", l))
        P.dma('sp', gng, gng_d[l], sem="gng")
        if pre is not None:
            pre()
        for mi in range(2):
            P.memset('dve', MX[mi]["S32"][0:48], 0.0)
            P.memset('dve', MX[mi]["Sbf"][0:48, 0], 0.0)
        mod_derive(l, 0)
        A1 = a12[:, l, 0, :]; S1 = modT[:, l, 0:8]

        def proj(hT, c0, c1, bank):
            n = c1 - c0
            for k in range(KC):
                P.mm(ps[bank][:, 0:n], hT[:, k, :], Win[:, k, c0:c1], start=(k == 0), stop=(k == KC - 1))
            return ps[bank][:, 0:n]

        def modtick(n=1):
            if msch is None:
                return
            for _ in range(n):
                if msch.done():
                    return
                n0 = msch.nm
                msch.tick()
                if n0 < 24 <= msch.nm:
                    mod_finish(l, 16, 24, src=ps[0][:, 384:392])
                if msch.done():
                    mod_finish(l, 24, 48, src=ps[0][:, 392:416])

        def stage0(t):
            norm_a(t * 128, 128, (sq, rstd, lnv, tmpf), psn=ps[2][:, 384:512])
            modtick(2 if (msch is not None and msch.nm < 24) else 1)
            norm_b(t * 128, 128, hT2[:, t % 2], (sq, rstd, lnv, tmpf), A=A1, S=S1, alt='dve2')
            yield

        def stage1(t):
            par = t % 2
            dr_ = MX[0]; dg = MX[1]
            hT = hT2[:, t % 2]
            modtick(2 if (msch is not None and msch.nm < 24) else 1)
            for k in range(KC):
                P.mm(ps[0][0:16, 0:128], Win[:, k, 2176:2192], hT[:, k, :], start=(k == 0), stop=(k == KC - 1))
            P.copy('act', gaT[0:16, :], ps[0][0:16, 0:128])
            yield
            pq = proj(hT, 256, 640, 1)
            cosb = costab[:, t, :].unsqueeze(1).broadcast_to([128, 16, 24])
            sinb = sintab[:, t, :].unsqueeze(1).broadcast_to([128, 16, 24])
            pq3 = pq.rearrange("p (c i) -> p c i", i=24)
            P.tt('dve', rotA.rearrange("p (c i) -> p c i", i=24), pq3, cosb, ALU.mult)
            P.tt('dve', rotB.rearrange("p (c i) -> p c i", i=24), pq3, sinb, ALU.mult)
            A4 = rotA.rearrange("p (c f i) -> p c f i", f=2, i=24); B4 = rotB.rearrange("p (c f i) -> p c f i", f=2, i=24)
            R4 = dr_["qk"][:, par].rearrange("p c (f i) -> p c f i", f=2)
            P.tt('dve', R4[:, :, 0, :], A4[:, :, 0, :], B4[:, :, 1, :], ALU.subtract)
            P.tt('dve', R4[:, :, 1, :], A4[:, :, 1, :], B4[:, :, 0, :], ALU.add)
            yield
            P.mm(ps[2][:, 0:192], gaT[0:16, :], wa2b[0:16, :], start=True, stop=False)
            P.mm(ps[2][:, 0:192], onesrow[0:1, :], bab[0:1, :], start=False, stop=True)
            P.act(e1, ps[2][:, 0:192], AF.Exp, scale=-1.0)
            P.act(spb, e1, AF.Ln, bias=kcol[:, 1:2], scale=1.0)
            yield
            pu = proj(hT, 0, 256, 0)
            P.copy('act', ubf[:, t % 3, :], pu)
            yield
            P.mm(ps[2][:, 192:384], tri, spb)
            for h in range(4):
                P.mm(ps[0][0:48, 416 + h:417 + h], spb[:, 48 * h:48 * h + 48], one32)
            P.act(eb, ps[2][:, 192:384], AF.Exp, bias=kcol[:, 2:3], scale=1.0)
            P.act(enb, ps[2][:, 192:384], AF.Exp, scale=-1.0)
            P.act(ebl[0:48, par, :], ps[0][0:48, 416:420], AF.Exp)
            yield
            pv = proj(hT, 640, 1024, 1)
            P.copy('act', dr_["v"][:, par], pv)
            P.tt('dve', dr_["vz"][:, par].rearrange("p (h v) -> p h v", h=4), pv.rearrange("p (h v) -> p h v", h=4),
                 zeta.unsqueeze(2).broadcast_to([128, 4, 96]), ALU.mult)
            yield
            pv = proj(hT, 1792, 2176, 0)
            P.copy('act', dg["v"][:, par], pv)
            yield
            pg = proj(hT, 1408, 1792, 1)
            P.tt('dve', dg["qk"][:, par, 0:4, :], pg[:, 0:192].rearrange("p (c d) -> p c d", c=4), eb.rearrange("p (c d) -> p c d", c=4), ALU.mult)
            P.tt('dve', dg["qk"][:, par, 4:8, :], pg[:, 192:384].rearrange("p (c d) -> p c d", c=4), enb.rearrange("p (c d) -> p c d", c=4), ALU.mult)
            yield
            B3v = B3bf[0:48, :].rearrange("p (c t) -> p c t", c=8)
            for c in range(8):
                P.tr(B3bf[0:48, c * 128:(c + 1) * 128], dr_["qk"][:, par, c, :], ident)
            P.copy('act', dr_["qkT"][0:48, par], B3v)
            P.tt('dve', dr_["qxT"][0:48, par], B3v[:, 0:4, :], xit[0:48], ALU.mult)
            yield
            pgt = proj(hT, 1024, 1408, 0)
            P.act(dr_["sg"][:, par], pgt, AF.Silu)
            yield
            pgt = proj(hT, 2192, 2576, 1)
            P.act(dg["sg"][:, par], pgt, AF.Silu)
            P.tt('dve', dg["sg"][:, par], dg["sg"][:, par], gng, ALU.mult)
            yield
            for c in range(8):
                P.tr(B3bf[0:48, c * 128:(c + 1) * 128], dg["qk"][:, par, c, :], ident)
            P.copy('act', dg["qkT"][0:48, par], B3v)
            yield

        def st_mask(t, mi, kind):
            d = MX[mi]; par = t % 2
            qkT = d["qkT"][:, par]
            STp = ps[5][:, :].rearrange("p (h t) -> p h t", h=4)
            for h in range(4):
                P.mm(STp[:, h, :], qkT[0:48, 4 + h, :], qkT[0:48, h, :])
            if kind == "ret":
                P.tt('dve', d["STm"], STp, DTm, ALU.mult)
            else:
                P.tt('dve', d["STm"], STp, causal.unsqueeze(1).broadcast_to([128, 4, 128]), ALU.mult)

        def core(t, mi, kind):
            d = MX[mi]; par = t % 2
            qkT = d["qkT"][:, par]
            qxT = d["qxT"][:, par] if kind == "ret" else qkT
            vv = d["v"][:, par]
            Op = ps[6][:, 0:384] if kind == "ret" else ps[7][:, 0:384]
            dSbank = ps[4]
            kz = d["qk"][:, par, 4:8, :]
            vz = d["vz"][:, par] if kind == "ret" else vv
            for h in range(4):
                P.mm(dSbank[0:48, 96 * h:96 * h + 96], kz[:, h, :], vz[:, 96 * h:96 * h + 96])
            dSv = dSbank[0:48, 0:384].rearrange("p (h v) -> p h v", h=4)
            S32 = d["S32"]
            for h in range(4):
                P.mm(Op[:, 96 * h:96 * h + 96], d["STm"][:, h, :], vv[:, 96 * h:96 * h + 96], start=True, stop=False)
                P.mm(Op[:, 96 * h:96 * h + 96], qxT[0:48, h, :], d["Sbf"][0:48, par, h, :], start=False, stop=True)
            if kind == "ret":
                for h in range(4):
                    P.stt(S32[0:48, h, :], S32[0:48, h, :], GAMC[h], dSv[:, h, :], ALU.mult, ALU.add)
            else:
                P.tt('dve', d["tmpS"][0:48], dSv, S32[0:48], ALU.add)
                P.tt('dve', S32[0:48], d["tmpS"][0:48], ebl[0:48, par, :].unsqueeze(2).broadcast_to([48, 4, 96]), ALU.mult)
            P.copy('act', d["Sbf"][0:48, 1 - par], S32[0:48])

        def rms_both(t):
            par = t % 2
            Ob = ps67[:, :].rearrange("p (m c) -> p m c", m=2)[:, :, 0:384]
            P.act(sq2, Ob, AF.Square)
            P.op('dve', lambda e, o=ss8, i_=sq2.rearrange("p m (h v) -> p (m h) v", h=4): e.tensor_reduce(o, i_, mybir.AxisListType.X, ALU.add),
                 r=[sq2], w=[ss8])
            P.act(ln8, ss8, AF.Ln, bias=kcol[:, 0:1], scale=1.0 / 96.0)
            P.act(rs8, ln8, AF.Exp, scale=-0.5)
            P.tt('dve', sq2.rearrange("p m (h v) -> p m h v", h=4), Ob.rearrange("p m (h v) -> p m h v", h=4),
                 rs8.rearrange("p (m h) -> p m h", m=2).unsqueeze(3).broadcast_to([128, 2, 4, 96]), ALU.mult)
            P.tt('dve', ybuf, sq2, sgj[:, par], ALU.mult)

        def y_tr(t, mi):
            ti = t % 4
            for c in range(3):
                P.tr(B5bf[:, c * 128:(c + 1) * 128], ybuf[:, mi, c * 128:(c + 1) * 128], ident)
            P.copy('dve', yT[:, 2 + 3 * mi:5 + 3 * mi, ti * 128:(ti + 1) * 128], B5bf[:, 0:384].rearrange("p (c t) -> p c t", c=3))

        def stage2(t):
            ti = t % 4
            st_mask(t, 0, "ret")
            st_mask(t, 1, "gla")
            yield
            core(t, 0, "ret")
            core(t, 1, "gla")
            rms_both(t)
            yield
            cur = t % 3; prv = (t - 1) % 3
            for p_ in range(2):
                for gs in range(2):
                    g = 2 * p_ + gs
                    pp = ps[4][:, 384:512]
                    if t == 0:
                        P.mm(pp, ubf[:, cur, 128 * p_:128 * p_ + 128], Pfirst[:, g, :])
                    else:
                        P.mm(pp, ubf[:, cur, 128 * p_:128 * p_ + 128], Pcur[:, g, :], start=True, stop=False)
                        P.mm(pp, ubf[:, prv, 128 * p_:128 * p_ + 128], Pprev[:, g, :], start=False, stop=True)
                    P.copy('act', pooledT[64 * gs:64 * gs + 64, p_, :], pp[64 * gs:64 * gs + 64, :])
                    yield
                pm = ps[7][:, 384:512]
                P.mm(pm, PWblk[:, p_, :], pooledT[:, p_, :])
                P.act(yT[:, p_, ti * 128:(ti + 1) * 128], pm, AF.Identity, scale=col(("pscale", l), p_, 1))
            y_tr(t, 0)
            yield
            y_tr(t, 1)
            yield
            if ti == 3:
                bk = t // 4
                for m in range(KC):
                    wb = ps[(7, 6)[m % 2]]
                    for k in range(KC):
                        P.mm(wb[:, 0:512], Wout[:, k, m * 128:(m + 1) * 128], yT[:, k, :], start=(k == 0), stop=(k == KC - 1))
                    xs = X[:, m, bk * 512:(bk + 1) * 512]
                    P.stt(xs, wb[:, 0:512], modT[:, l, 16 + m:17 + m], xs, ALU.mult, ALU.add)
                    if m % 2 == 1:
                        yield

        def rr(gens):
            while gens:
                for g_ in list(gens):
                    try:
                        next(g_)
                    except StopIteration:
                        gens.remove(g_)
        rr([stage0(0)])
        rr([stage1(0), stage0(1)])
        for t in range(16):
            gens = [stage2(t)]
            if t + 2 < 16:
                gens.append(stage0(t + 2))
            if t + 1 < 16:
                gens.append(stage1(t + 1))
            rr(gens)

    def ffn(l, nextmod=None):
        RB = Bump(RB0, RLIM)
        h2 = RB.alloc(KC * T * 2, BF16, "p (k t) -> p k t", k=KC)
        actb = RB.alloc(6 * T * 2, BF16, "p (j t) -> p j t", j=6)
        wd = RB.alloc(6 * D * 2, BF16, "p (j c) -> p j c", j=6)
        wu = RB.alloc(2 * KC * 256 * 2, BF16, "p (s k c) -> p s k c", s=2, k=KC)
        NT = 128
        ntmps = []
        for i_ in range(2):
            sq_ = RB.alloc(KC * NT * 2, BF16, "p (k t) -> p k t", k=KC)
            rstd_ = RB.alloc(NT * 4, F32)
            lnv_ = RB.alloc(NT * 4, F32)
            tmpf_ = RB.alloc(KC * NT * 4, F32, "p (k t) -> p k t", k=KC)
            ntmps.append((sq_, rstd_, lnv_, tmpf_))
        mod_derive(l, 1)
        U = RB.alloc(3 * 2 * 514 * 4, F32, "p (s q c) -> p s q c", s=3, q=2)
        Y = RB.alloc(3 * 2 * 512 * 4, F32, "p (s q c) -> p s q c", s=3, q=2)
        wuv = w_up[l].rearrange("(k p) c -> p k c", p=128)
        NSB = T // NT
        def normgen():
            norm_a(0, NT, ntmps[0], psn=ps[6][:, 0:NT])
            for i in range(NSB):
                if i + 1 < NSB:
                    norm_a((i + 1) * NT, NT, ntmps[(i + 1) % 2], psn=ps[6][:, ((i + 1) % 2) * NT:((i + 1) % 2 + 1) * NT])
                norm_b(i * NT, NT, h2[:, :, i * NT:(i + 1) * NT], ntmps[i % 2], A=a12[:, l, 1, :], S=modT[:, l, 24:32])
                yield i
        ngen = normgen()
        ndone = [-1]
        def ensure_norm(tb):
            need = (tb + 1) * (512 // NT) - 1
            while ndone[0] < need:
                ndone[0] = next(ngen)
        def load_wu(j):
            slot = j % 2
            P.dma('pool', wu[:, slot, :, 0:128], wuv[:, :, j * 128:(j + 1) * 128], sem="wu%d" % slot, grp=("wu", l, j))
            P.dma('pool', wu[:, slot, :, 128:256], wuv[:, :, DFF + j * 128:DFF + (j + 1) * 128], sem="wu%d" % slot, grp=("wu", l, j))
        def load_wd(J):
            for jl, j in enumerate(J):
                P.dma('pool', wd[:, jl, :], w_down[l][j * 128:(j + 1) * 128, :], sem="wd%d" % jl)
        gstep = [0]
        ring4 = [arena[:, RLIM - 2048 * (i + 1):RLIM - 2048 * i].bitcast(BF16).rearrange("p (k c) -> p k c", k=8) for i in range(4)]
        msch = ModSched(nextmod, mode='bf16', ring=ring4, semp="modr4_%d") if nextmod is not None else None
        def stageA(st):
            jl, j, tb, g = st
            par = g % 2
            ensure_norm(tb)
            if msch is not None and not msch.done():
                msch.tick()
            slot = j % 2
            for qi in range(2):
                pst = ps[2 * par + qi]
                for k in range(KC):
                    P.mm(pst[:, :], wu[:, slot, k, qi * 128:(qi + 1) * 128], h2[:, k, tb * 512:(tb + 1) * 512],
                         start=(k == 0), stop=(k == KC - 1))
        def cw(j, qi):
            ch = qi * NJ + j
            return (col(("cw", l), 0 * 44 + ch, 1), col(("cw", l), 1 * 44 + ch, 1), col(("cw", l), 2 * 44 + ch, 1), col(("cb", l), ch, 1))
        def stageB(st):
            jl, j, tb, g = st
            par = g % 2; p3 = g % 3
            for qi in range(2):
                pst = ps[2 * par + qi]
                w0, w1, w2, bb = cw(j, qi)
                Ub = U[:, p3, qi, :]; Yb = Y[:, p3, qi, :]
                if tb == 0:
                    P.memset('pool', Ub[:, 0:2], 0.0)
                else:
                    P.copy('pool', Ub[:, 0:2], U[:, (g - 1) % 3, qi, 512:514])
                P.copy('act', Ub[:, 2:514], pst[:, :])
                P.act(Yb, pst[:, :], AF.Identity, bias=bb, scale=w2)
        def stageC(st):
            jl, j, tb, g = st
            p3 = g % 3
            for qi in range(2):
                w0, w1, w2, bb = cw(j, qi)
                Ub = U[:, p3, qi, :]; Yb = Y[:, p3, qi, :]
                P.stt(Yb, Ub[:, 1:513], w1, Yb, ALU.mult, ALU.add)
                P.stt(Yb, Ub[:, 0:512], w0, Yb, ALU.mult, ALU.add)
        def stageD1(st):
            jl, j, tb, g = st
            p3 = g % 3
            P.act(Y[:, p3, 1, :], Y[:, p3, 1, :], AF.Silu)
        def stageD2(st):
            jl, j, tb, g = st
            p3 = g % 3
            P.tt('dve', actb[:, jl, tb * 512:(tb + 1) * 512], Y[:, p3, 0, :], Y[:, p3, 1, :], ALU.mult)
        ycnt = 0
        load_wd(QUARTERS[0])
        load_wu(QUARTERS[0][0])
        def make_steps(q):
            J = QUARTERS[q]
            steps = []
            for jl, j in enumerate(J):
                for tb in range(4):
                    steps.append((jl, j, tb, gstep[0])); gstep[0] += 1
            return steps
        def prefetch_for(q, steps, idx):
            J = QUARTERS[q]
            jl, j, tb, g = steps[idx]
            if tb == 0:
                if jl + 1 < len(J):
                    load_wu(J[jl + 1])
                elif q + 1 < len(QUARTERS):
                    load_wu(QUARTERS[q + 1][0])
        def prologue(q, steps):
            prefetch_for(q, steps, 0)
            stageA(steps[0]); stageB(steps[0])
            prefetch_for(q, steps, 1)
            stageA(steps[1]); stageB(steps[1])
        cur_steps = make_steps(0)
        prologue(0, cur_steps)
        for q, J in enumerate(QUARTERS):
            steps = cur_steps
            n_ = len(steps)
            stageC(steps[0])
            for i in range(n_):
                if i + 2 < n_:
                    prefetch_for(q, steps, i + 2)
                    stageA(steps[i + 2]); stageB(steps[i + 2])
                if i + 1 < n_:
                    stageC(steps[i + 1])
                stageD1(steps[i]); stageD2(steps[i])
            if q + 1 < len(QUARTERS):
                cur_steps = make_steps(q + 1)
                prologue(q + 1, cur_steps)
            for tb in range(4):
                for m in range(KC):
                    pst = ps[4 + ycnt % 2]; ycnt += 1
                    for jl in range(len(J)):
                        P.mm(pst[:, :], wd[:, jl, m * 128:(m + 1) * 128], actb[:, jl, tb * 512:(tb + 1) * 512],
                             start=(jl == 0), stop=(jl == len(J) - 1))
                    xs = X[:, m, tb * 512:(tb + 1) * 512]
                    P.stt(xs, pst[:, :], modT[:, l, 40 + m:41 + m], xs, ALU.mult, ALU.add)
            if q + 1 < len(QUARTERS):
                load_wd(QUARTERS[q + 1])
        if msch is not None:
            msch.run_until(48)
            mod_finish(nextmod, 0, 48)

    def final():
        RB = Bump(RB0, RLIM)
        NT = 256
        tmps = []
        for i in range(2):
            sq = RB.alloc(KC * NT * 2, BF16, "p (k t) -> p k t", k=KC)
            rstd = RB.alloc(NT * 4, F32)
            lnv = RB.alloc(NT * 4, F32)
            tmpf = RB.alloc(KC * NT * 4, F32, "p (k t) -> p k t", k=KC)
            tmps.append((sq, rstd, lnv, tmpf))
        ob = RB.alloc(2 * KC * NT * 4, F32, "p (s k t) -> p s k t", s=2, k=KC)
        nb = T // NT
        norm_a(0, NT, tmps[0], psn=ps[6][:, 0:NT])
        for i in range(nb):
            t0 = i * NT
            if i + 1 < nb:
                norm_a(t0 + NT, NT, tmps[(i + 1) % 2], psn=ps[6 + (i + 1) % 2][:, 0:NT])
            norm_b(t0, NT, ob[:, i % 2], tmps[i % 2], G=col("gf"))
            P.dma('sp' if i % 2 == 0 else 'act', ov[:, :, t0:t0 + NT], ob[:, i % 2], sem="out%d" % (i % 2))

    layers = list(layers)
    for li, l in enumerate(layers):
        nxt = layers[li + 1] if li + 1 < len(layers) else None
        m0 = None
        pre = None
        if li == 0 or not do_ffn:
            if do_mixer:
                def pre():
                    ma = ModSched(l, mode='f32', lo=0, hi=16)
                    ma.run_until(16)
                    mod_finish(l, 0, 16)
                m0 = ModSched(l, mode='bf16', lo=16, hi=48)
                m0.where = lambda jc: ps[0][:, 384 + jc - 16:385 + jc - 16]
            else:
                m0 = ModSched(l)
                m0.run_until(48)
                mod_finish(l, 0, 48)
                m0 = None
        if do_mixer:
            mixer(l, msch=m0, pre=pre)
        if do_ffn:
            ffn(l, nextmod=nxt)
    final()
    st = P.emit()
    return st


def kernel(**inputs):
    inp = {k: np.asarray(v) for k, v in inputs.items()}
    nc = bass.Bass("TRN2", target_bir_lowering=False)
    build(nc)
    in_maps = [host_inputs(inp, b) for b in range(8)]
    res = run_bass_kernel_spmd(nc, in_maps, core_ids=list(range(8)))
    out = np.stack([np.asarray(res.results[b]["outT"]).T for b in range(8)], axis=0)
    return np.ascontiguousarray(out, dtype=np.float32)
```
